# Optimizing a Trainium2 kernel written in Bass

```python
import jax, jax.numpy as jnp
from jax import lax
import numpy as np


D_MODEL = 1024
BATCH = 32
SEQ = 256
DEPTH = 2
DEC_BATCH = 4
DEC_SEQ = 4096
PAST_LEN = 512

GRID_W = 64
N_MOD = 9
D_FF = 2816
EPS = 1e-6
POOL_GROUPS = 4
POOL_GW = 64
POOL_W = POOL_GROUPS * POOL_GW
POOL_WINDOWS = (2, 4, 8, 16)
SGU_GROUPS = 4
SGU_GW = 64
SGU_W = SGU_GROUPS * SGU_GW
CHUNK = 128
MLA_HEADS = 8
QK_NOPE = 64
QK_ROPE = 32
V_HEAD = 64
Q_LORA = 256
KV_LORA = 128
MLA_W = MLA_HEADS * V_HEAD
ROPE_THETA = 10000.0
Q_BLOCK = 128
CONV_W = 256
CONV_K = 3
N_BRANCH = 4
IN_WIDTHS = (POOL_W, SGU_W, SGU_W, Q_LORA, KV_LORA, QK_ROPE, CONV_W, CONV_W, CONV_W)
IN_SPLITS = tuple(int(v) for v in np.cumsum(IN_WIDTHS)[:-1])
D_IN = int(sum(IN_WIDTHS))

kernel_name = 'hybrid_diffusion_parallel_mixer_step'


def rmsnorm(x, g):
    xf = x.astype(jnp.float32)
    y = xf * lax.rsqrt(jnp.mean(xf * xf, axis=-1, keepdims=True) + EPS)
    return (y * g.astype(jnp.float32)).astype(x.dtype)


def swiglu(h, w_gu, w_dn):
    g, u = jnp.split(h @ w_gu, 2, axis=-1)
    return (jax.nn.silu(g) * u) @ w_dn


def centred_window_mean(x, w):
    L = x.shape[1]
    cs = jnp.pad(jnp.cumsum(x, axis=1), ((0, 0), (1, 0), (0, 0)))
    t = jnp.arange(L)
    lo = jnp.clip(t - w // 2, 0, L)
    hi = jnp.clip(t + w - w // 2, 0, L)
    s = jnp.take(cs, hi, axis=1) - jnp.take(cs, lo, axis=1)
    cnt = (hi - lo).astype(jnp.float32)
    return s / cnt[None, :, None]


def pool_mixer(xp, w_pool, scale):
    xf = xp.astype(jnp.float32)
    outs = []
    for g, w in enumerate(POOL_WINDOWS):
        xg = xf[..., g * POOL_GW:(g + 1) * POOL_GW]
        outs.append(centred_window_mean(xg, w) - xg)
    d = jnp.stack(outs, axis=-2).astype(xp.dtype)
    y = jnp.einsum('blgc,gcd->blgd', d, w_pool)
    return y.reshape(xp.shape) * scale


def spatial_gating(u, v, g_norm, w_s, b_s):
    B, L, _ = v.shape
    vc = rmsnorm(v, g_norm).reshape(B, L // CHUNK, CHUNK, SGU_GROUPS, SGU_GW)
    mixed = jnp.einsum('gpq,bnqgc->bnpgc', w_s, vc) + b_s.T[None, None, :, :, None]
    return u * mixed.reshape(B, L, SGU_W)


def short_conv(x, w):
    return lax.conv_general_dilated(x, w[:, None, :], window_strides=(1,),
                                    padding=((CONV_K // 2, CONV_K // 2),),
                                    dimension_numbers=('NWC', 'WIO', 'NWC'),
                                    feature_group_count=CONV_W)


def axial_rope_tables(n):
    rows = n // GRID_W
    r = jnp.broadcast_to(jnp.arange(rows)[:, None], (rows, GRID_W)).reshape(-1).astype(jnp.float32)
    col = jnp.broadcast_to(jnp.arange(GRID_W)[None, :], (rows, GRID_W)).reshape(-1).astype(jnp.float32)
    half = QK_ROPE // 2
    freqs = ROPE_THETA ** (-(2.0 * jnp.arange(half // 2, dtype=jnp.float32)) / half)
    ang = jnp.stack([r[:, None] * freqs, col[:, None] * freqs], axis=1)
    return jnp.cos(ang), jnp.sin(ang)


def apply_axial_rope(x, cos, sin):
    shp = x.shape
    xr = x.reshape(shp[:-1] + (2, 2, QK_ROPE // 4)).astype(jnp.float32)
    x1, x2 = xr[..., 0, :], xr[..., 1, :]
    c = cos[None, :, None]
    s = sin[None, :, None]
    out = jnp.stack([x1 * c - x2 * s, x1 * s + x2 * c], axis=-2)
    return out.reshape(shp).astype(x.dtype)


def mla_kv(ckv_n, kr, w_ukv):
    B, L, _ = ckv_n.shape
    kv = (ckv_n @ w_ukv).reshape(B, L, MLA_HEADS, QK_NOPE + V_HEAD)
    k_nope, v = kv[..., :QK_NOPE], kv[..., QK_NOPE:]
    k = jnp.concatenate([k_nope, jnp.broadcast_to(kr[:, :, None, :], (B, L, MLA_HEADS, QK_ROPE))], axis=-1)
    return k, v


def attend(q, k, v):
    B, Lq, H, Dk = q.shape
    nb = Lq // Q_BLOCK
    qb = q.reshape(B, nb, Q_BLOCK, H, Dk).transpose(1, 0, 2, 3, 4)
    scale = Dk ** -0.5

    def one(qblk):
        s = jnp.einsum('bqhd,bkhd->bhqk', qblk, k).astype(jnp.float32) * scale
        p = jax.nn.softmax(s, axis=-1).astype(v.dtype)
        return jnp.einsum('bhqk,bkhd->bqhd', p, v)

    o = lax.map(one, qb)
    return o.transpose(1, 0, 2, 3, 4).reshape(B, Lq, H, v.shape[-1])


def token_mix(h, l, P, ctx, rope):
    B, L, _ = h.shape
    z = h @ P['w_in'][l]
    zp, zu, zv, cq, ckv, kr, zb, zc, zx = jnp.split(z, IN_SPLITS, axis=-1)
    a = pool_mixer(zp, P['w_pool'][l], P['pool_scale'][l])
    b = spatial_gating(zu, zv, P['g_sgu'][l], P['w_sgu'][l], P['b_sgu'][l])
    q = (rmsnorm(cq, P['g_q'][l]) @ P['w_uq'][l]).reshape(B, L, MLA_HEADS, QK_NOPE + QK_ROPE)
    ckv_n = rmsnorm(ckv, P['g_kv'][l])
    if ctx is None:
        k, v = mla_kv(ckv_n, kr, P['w_ukv'][l])
        att = attend(q, k, v)
        state = (ckv_n, kr)
    else:
        cos, sin = rope
        q = jnp.concatenate([q[..., :QK_NOPE], apply_axial_rope(q[..., QK_NOPE:], cos, sin)], axis=-1)
        kr_rot = apply_axial_rope(kr[:, :, None, :], cos, sin)[:, :, 0, :]
        k_lat, v_lat = mla_kv(ckv_n, kr_rot, P['w_ukv'][l])
        k_ctx, v_ctx = mla_kv(ctx[0], ctx[1], P['w_ukv'][l])
        att = attend(q, jnp.concatenate([k_ctx, k_lat], axis=1), jnp.concatenate([v_ctx, v_lat], axis=1))
        state = None
    att = att.reshape(B, L, MLA_W)
    dconv = zb * short_conv(zc * zx, P['conv_w'][l])
    gates = jax.nn.sigmoid(h @ P['w_gate'][l] + P['b_gate'][l]).reshape(B, L, N_BRANCH, D_MODEL)
    merged = (gates[:, :, 0] * (a @ P['w_br_pool'][l])
              + gates[:, :, 1] * (b @ P['w_br_sgu'][l])
              + gates[:, :, 2] * (att @ P['w_br_mla'][l])
              + gates[:, :, 3] * (dconv @ P['w_br_conv'][l]))
    return merged @ P['w_o'][l], state


def layer(x, cond, l, P, ctx, rope):
    Bc = cond.shape[0]
    mods = (jax.nn.silu(cond) @ P['w_mod'][l] + P['b_mod'][l]).reshape(Bc, N_MOD, 1, D_MODEL)

    def pre(x_, s):
        return rmsnorm(x_, P['g_pre'][l, s]) * (1 + mods[:, 3 * s + 1]) + mods[:, 3 * s]

    f = swiglu(pre(x, 0), P['w_ffn_gu'][l, 0], P['w_ffn_dn'][l, 0])
    x = x + 0.5 * mods[:, 2] * rmsnorm(f, P['g_post'][l, 0])
    m, state = token_mix(pre(x, 1), l, P, ctx, rope)
    x = x + mods[:, 5] * rmsnorm(m, P['g_post'][l, 1])
    f = swiglu(pre(x, 2), P['w_ffn_gu'][l, 1], P['w_ffn_dn'][l, 1])
    x = x + 0.5 * mods[:, 8] * rmsnorm(f, P['g_post'][l, 2])
    return x, state


def setup_inputs(seed: int = 0) -> dict:
    key = jax.random.key(seed)
    ks = iter(jax.random.split(key, 40))

    def nrm(shape, scale):
        return jax.random.normal(next(ks), shape, jnp.float32) * scale

    def gain(shape):
        return 1.0 + nrm(shape, 0.02)

    D, L = D_MODEL, DEPTH
    return {
        'x_prompt': nrm((BATCH, SEQ, D), 1.0),
        'x_sample': nrm((DEC_BATCH, DEC_SEQ, D), 1.0),
        'cache_ckv': nrm((DEC_BATCH, DEPTH, PAST_LEN, KV_LORA), 1.0),
        'cache_krope': nrm((DEC_BATCH, DEPTH, PAST_LEN, QK_ROPE), 1.0),
        'c': nrm((DEC_BATCH, D), 1.0),
        'c_ctx': nrm((D,), 1.0),
        'w_mod': nrm((L, D, N_MOD * D), 0.5 * D ** -0.5),
        'b_mod': nrm((L, N_MOD * D), 0.01),
        'g_pre': gain((L, 3, D)),
        'g_post': gain((L, 3, D)),
        'w_ffn_gu': nrm((L, 2, D, 2 * D_FF), D ** -0.5),
        'w_ffn_dn': nrm((L, 2, D_FF, D), D_FF ** -0.5),
        'w_in': nrm((L, D, D_IN), D ** -0.5),
        'w_pool': nrm((L, POOL_GROUPS, POOL_GW, POOL_GW), POOL_GW ** -0.5),
        'pool_scale': gain((L, POOL_W)),
        'g_sgu': gain((L, SGU_W)),
        'w_sgu': nrm((L, SGU_GROUPS, CHUNK, CHUNK), CHUNK ** -0.5),
        'b_sgu': 1.0 + nrm((L, SGU_GROUPS, CHUNK), 0.01),
        'g_q': gain((L, Q_LORA)),
        'w_uq': nrm((L, Q_LORA, MLA_HEADS * (QK_NOPE + QK_ROPE)), Q_LORA ** -0.5),
        'g_kv': gain((L, KV_LORA)),
        'w_ukv': nrm((L, KV_LORA, MLA_HEADS * (QK_NOPE + V_HEAD)), KV_LORA ** -0.5),
        'conv_w': nrm((L, CONV_K, CONV_W), CONV_K ** -0.5),
        'w_br_pool': nrm((L, POOL_W, D), POOL_W ** -0.5),
        'w_br_sgu': nrm((L, SGU_W, D), SGU_W ** -0.5),
        'w_br_mla': nrm((L, MLA_W, D), MLA_W ** -0.5),
        'w_br_conv': nrm((L, CONV_W, D), CONV_W ** -0.5),
        'w_gate': nrm((L, D, N_BRANCH * D), D ** -0.5),
        'b_gate': nrm((L, N_BRANCH * D), 0.01),
        'w_o': nrm((L, D, D), D ** -0.5),
    }


def reference(x_prompt, x_sample, cache_ckv, cache_krope, c, c_ctx, w_mod, b_mod, g_pre, g_post,
              w_ffn_gu, w_ffn_dn, w_in, w_pool, pool_scale, g_sgu, w_sgu, b_sgu, g_q, w_uq, g_kv,
              w_ukv, conv_w, w_br_pool, w_br_sgu, w_br_mla, w_br_conv, w_gate, b_gate, w_o):
    P = dict(w_mod=w_mod, b_mod=b_mod, g_pre=g_pre, g_post=g_post, w_ffn_gu=w_ffn_gu,
             w_ffn_dn=w_ffn_dn, w_in=w_in, w_pool=w_pool, pool_scale=pool_scale, g_sgu=g_sgu,
             w_sgu=w_sgu, b_sgu=b_sgu, g_q=g_q, w_uq=w_uq, g_kv=g_kv, w_ukv=w_ukv, conv_w=conv_w,
             w_br_pool=w_br_pool, w_br_sgu=w_br_sgu, w_br_mla=w_br_mla, w_br_conv=w_br_conv,
             w_gate=w_gate, b_gate=b_gate, w_o=w_o)
    x = x_prompt
    ckv_states, kr_states = [], []
    for l in range(DEPTH):
        x, st = layer(x, c_ctx[None, :], l, P, None, None)
        ckv_states.append(st[0])
        kr_states.append(st[1])
    y_prompt = x
    state_ckv = jnp.stack(ckv_states, axis=1)
    state_krope = jnp.stack(kr_states, axis=1)
    cos, sin = axial_rope_tables(x_sample.shape[1])
    x = x_sample
    for l in range(DEPTH):
        x, _ = layer(x, c, l, P, (cache_ckv[:, l], cache_krope[:, l]), (cos, sin))
    y_sample = x
    return (y_prompt, y_sample, state_ckv, state_krope)
```

```python
import numpy as np
import concourse.bass as bass
import concourse.mybir as mybir
from concourse.bass_utils import run_bass_kernel_spmd

F32 = mybir.dt.float32
BF16 = mybir.dt.bfloat16
AF = mybir.ActivationFunctionType
ALU = mybir.AluOpType

D = 1024
KC = 8
DFF = 2816
FC = 22
NT = 512
L = 2
EPS = 1e-6
NPT = 1024
NST = 2048
NKS = 512 + 4096
XR = 163
SCALE = 96 ** -0.5
ESZ = {F32: 4, BF16: 2}

V_BMOD = 0
V_GPRE = 72
V_GPOST = 96
V_PSCALE = 120
V_GQ = 122
V_GKV = 124
V_CONVW = 125
V_BGATE = 131
V_N = 163
G_FREQ = L * V_N
G_SIGN = G_FREQ + 1
G_ML = G_FREQ + 2
G_MR = G_FREQ + 3
G_INVW = G_FREQ + 4
G_SEL = G_FREQ + 6
G_N = G_FREQ + 10


class Prog:
    RING = 8

    def __init__(self, nc):
        self.nc = nc
        self.ins = []
        self.live_w = {}
        self.live_r = {}
        self.sb = {}
        self.ps = set()
        self.dram_out = set()

    def acc(self, ap):
        name = ap.name
        apl = ap.ap
        if name in self.sb:
            base, es = self.sb[name]
            ps = apl[0][0]
            fo = ap.offset % ps
            ext = sum((c - 1) * s for s, c in apl[1:])
            return ('sb', base + fo * es, base + (fo + ext + 1) * es)
        if name in self.ps:
            return (name, 0, 512)
        ext = sum((c - 1) * s for s, c in apl)
        return (name, ap.offset, ap.offset + ext + 1)

    def op(self, eng, fn, reads=(), writes=(), kind='c'):
        idx = len(self.ins)
        racc = [self.acc(a) for a in reads]
        wacc = [self.acc(a) for a in writes]
        raw, other = set(), set()
        for (sp, lo, hi) in racc:
            for (l2, h2, j) in self.live_w.get(sp, ()):
                if l2 < hi and lo < h2:
                    raw.add(j)
        for (sp, lo, hi) in wacc:
            for (l2, h2, j) in self.live_w.get(sp, ()):
                if l2 < hi and lo < h2:
                    other.add(j)
            for (l2, h2, j) in self.live_r.get(sp, ()):
                if l2 < hi and lo < h2:
                    other.add(j)
        deps = set()
        for j in raw | other:
            pj = self.ins[j]
            if pj['eng'] == eng and kind == 'c' and pj['kind'] == 'c':
                if eng == 'pe':
                    continue
            deps.add(j)
        deps.discard(idx)
        for (sp, lo, hi) in wacc:
            lw = self.live_w.setdefault(sp, [])
            lw[:] = [r for r in lw if not (lo <= r[0] and r[1] <= hi)]
            lr = self.live_r.setdefault(sp, [])
            lr[:] = [r for r in lr if not (lo <= r[0] and r[1] <= hi)]
            lw.append((lo, hi, idx))
        for (sp, lo, hi) in racc:
            lr = self.live_r.setdefault(sp, [])
            if kind == 'c':
                lr[:] = [r for r in lr if not (r[0] == lo and r[1] == hi and self.ins[r[2]]['eng'] == eng
                                               and self.ins[r[2]]['kind'] == 'c')]
            lr.append((lo, hi, idx))
        is_out = any(a.name in self.dram_out for a in writes)
        self.ins.append(dict(eng=eng, fn=fn, kind=kind, deps=deps, out=is_out))
        return idx

    def finalize(self, es, block):
        nc = self.nc
        ins = self.ins
        has_dep = [False] * len(ins)
        for it in ins:
            for j in it['deps']:
                has_dep[j] = True
        engs = ['pe', 'act', 'dve', 'pool', 'sp']
        esem = {e: es.enter_context(nc.semaphore('s_' + e)) for e in engs}
        ring = {q: [es.enter_context(nc.semaphore('r_%s%d' % (q, i))) for i in range(self.RING)] for q in ('sp', 'pool')}
        ccsem = es.enter_context(nc.semaphore('s_cc'))
        ecnt = {e: 0 for e in engs}
        dcnt = {'sp': 0, 'pool': 0}
        cccnt = 0
        for i, it in enumerate(ins):
            e = it['eng']
            it['prewait'] = None
            if it['kind'] == 'c':
                if has_dep[i]:
                    ecnt[e] += 1
                    it['sig'] = (esem[e], ecnt[e], 1)
                else:
                    it['sig'] = None
            elif it['kind'] == 'd':
                n = dcnt[e]
                dcnt[e] += 1
                slot = n % self.RING
                use = n // self.RING
                it['sig'] = (ring[e][slot], 16 * (use + 1), 16)
                if use > 0:
                    it['prewait'] = (ring[e][slot], 16 * use)
            else:
                cccnt += 1
                it['sig'] = (ccsem, cccnt, None)
        print('tracker: instrs', len(ins), 'sem counts', ecnt, dcnt, cccnt, flush=True)
        per = {e: [] for e in engs}
        for i, it in enumerate(ins):
            per[it['eng']].append(i)
        outs_wait = [ins[i]['sig'] for i in range(len(ins)) if ins[i]['out']]
        eobj = {'pe': nc.tensor, 'act': nc.scalar, 'dve': nc.vector, 'pool': nc.gpsimd, 'sp': nc.sync}

        def run(e):
            def body(engine):
                seen = {}
                for i in per[e]:
                    it = ins[i]
                    waits = {}
                    for j in it['deps']:
                        sg = ins[j]['sig']
                        k = id(sg[0])
                        if k not in waits or waits[k][1] < sg[1]:
                            waits[k] = (sg[0], sg[1])
                    if it['prewait'] is not None:
                        sg = it['prewait']
                        k = id(sg[0])
                        if k not in waits or waits[k][1] < sg[1]:
                            waits[k] = sg
                    for k, (s, v) in waits.items():
                        if seen.get(k, 0) < v:
                            engine.wait_ge(s, v)
                            seen[k] = v
                    r = it['fn'](engine)
                    sg = it['sig']
                    if sg is not None:
                        if sg[2] is None:
                            r.then_inc(sg[0])
                        else:
                            r.then_inc(sg[0], sg[2])
                if e == 'sp':
                    for (s, v, _) in outs_wait:
                        if seen.get(id(s), 0) < v:
                            engine.wait_ge(s, v)
                            seen[id(s)] = v
            return body
        block.tensor(run('pe'))
        block.scalar(run('act'))
        block.vector(run('dve'))
        block.gpsimd(run('pool'))
        block.sync(run('sp'))


class Builder:
    def __init__(self, debug_stage=None):
        self.debug_stage = debug_stage
        self.nc = bass.Bass("TRN2", target_bir_lowering=False)
        self.P = Prog(self.nc)
        self.sp_top = 16512
        self.psum_rr = 0
        self.rr = {}
        self.wr_i = 0
        self.att_pend = None

    def sb(self, name, free, dt, at=None):
        es = ESZ[dt]
        if at is None:
            at = self.sp_top
        at = (at + 31) // 32 * 32
        t = self.nc.alloc_sbuf_tensor_at(name, [128, free], dt, offset=at)
        end = at + free * es
        assert end <= 229000, (name, end)
        self.P.sb[t[:, 0:1].name] = (at, es)
        if end > self.sp_top:
            self.sp_top = end
        self.peak = max(getattr(self, 'peak', 0), end)
        return t

    def din(self, name, shape):
        return self.nc.dram_tensor(name, list(shape), F32, kind="ExternalInput")

    def dout(self, name, shape):
        t = self.nc.dram_tensor(name, list(shape), F32, kind="ExternalOutput")
        self.P.dram_out.add(t.ap().name)
        return t

    def mm(self, out, lhsT, rhs, start, stop):
        self.P.op('pe', lambda e: e.matmul(out, lhsT, rhs, start=start, stop=stop), [lhsT, rhs], [out])

    def act(self, out, in_, func, bias=None, scale=None, accum=None, extra_r=()):
        kw = {}
        rd = [in_] + list(extra_r)
        if bias is not None:
            kw['bias'] = bias
            if not isinstance(bias, float):
                rd.append(bias)
        if scale is not None:
            kw['scale'] = scale
            if not isinstance(scale, float):
                rd.append(scale)
        wr = [out]
        if accum is not None:
            kw['accum_out'] = accum
            wr.append(accum)
        self.P.op('act', lambda e: e.activation(out, in_, func, **kw), rd, wr)

    def tt(self, out, in0, in1, op, eng='dve'):
        self.P.op(eng, lambda e: e.tensor_tensor(out, in0, in1, op), [in0, in1], [out])

    def ts(self, out, in0, s1, s2, op0, op1=None, eng='dve'):
        rd = [in0] + [s for s in (s1, s2) if s is not None and not isinstance(s, float)]
        if op1 is None:
            self.P.op(eng, lambda e: e.tensor_scalar(out, in0, s1, None, op0), rd, [out])
        else:
            self.P.op(eng, lambda e: e.tensor_scalar(out, in0, s1, s2, op0, op1), rd, [out])

    def stt(self, out, in0, scalar, in1, op0, op1, eng='dve'):
        rd = [in0, in1] + ([] if isinstance(scalar, float) else [scalar])
        self.P.op(eng, lambda e: e.scalar_tensor_tensor(out, in0, scalar, in1, op0, op1), rd, [out])

    def recip(self, out, in_):
        self.P.op('dve', lambda e: e.reciprocal(out, in_), [in_], [out])

    def copy(self, out, in_, eng='dve'):
        self.P.op(eng, lambda e: e.tensor_copy(out, in_), [in_], [out])

    def memset(self, ap, val, eng='pool'):
        self.P.op(eng, lambda e: e.memset(ap, val), [], [ap])

    def dma(self, out, in_, q='sp'):
        self.P.op(q, lambda e: e.dma_start(out=out, in_=in_, allow_slow_non_contiguous=True), [in_], [out], kind='d')

    def psum(self):
        b = self.pbanks[self.psum_rr % 6]
        self.psum_rr += 1
        return b

    def tmp(self, key):
        lst = self.tmps[key]
        i = self.rr.get(key, 0)
        self.rr[key] = i + 1
        return lst[i % len(lst)]

    def wload(self, src2d, kc, bw):
        slot = self.wring[self.wr_i % len(self.wring)]
        self.wr_i += 1
        n = kc * bw
        dst = slot[:, 0:n]
        self.dma(dst, src2d, q='pool')
        return dst.rearrange("p (k c) -> p k c", k=kc)

    def rstd_from_ss(self, ss_ps, n, nfeat, rows=slice(0, 128)):
        s = self.tmp('nrm')
        self.act(s[rows, 0:n], ss_ps, AF.Sqrt, bias=self.eps_col[rows, 0:1], scale=1.0 / nfeat)
        r = self.tmp('nrm')
        self.recip(r[rows, 0:n], s[rows, 0:n])
        return r

    def prenorm(self, xt, n, Acols, Bcols, h):
        ss = self.ssbank()
        for c in range(KC):
            sq = self.tmp('sq')
            self.act(sq[:, 0:n], xt[:, c, :], AF.Square)
            self.mm(ss[:, 0:n], self.ones_bf[:, :], sq[:, 0:n], c == 0, c == KC - 1)
        r = self.rstd_from_ss(ss[:, 0:n], n, D)
        for c in range(KC):
            t = self.tmp('f32')
            self.tt(t[:, 0:n], xt[:, c, :], r[:, 0:n], ALU.mult)
            self.act(h[:, c, :], t[:, 0:n], AF.Identity, bias=Bcols[:, c:c + 1], scale=Acols[:, c:c + 1])

    def ssbank(self):
        b = self.pbanks[6 + (self.rr.get('ss', 0) % 2)]
        self.rr['ss'] = self.rr.get('ss', 0) + 1
        return b

    def post_residual(self, fs, ss, xt, n, Ccols):
        r = self.rstd_from_ss(ss[:, 0:n], n, D)
        for m in range(KC):
            t = self.tmp('f32')
            self.tt(t[:, 0:n], fs[:, m, :], r[:, 0:n], ALU.mult)
            self.stt(xt[:, m, :], t[:, 0:n], Ccols[:, m:m + 1], xt[:, m, :], ALU.mult, ALU.add)

    def evac_f(self, ps, n, fs_m, ss, first, last):
        self.act(fs_m, ps, AF.Copy)
        sq = self.tmp('sq')
        self.act(sq[:, 0:n], ps, AF.Square)
        return lambda: self.mm(ss[:, 0:n], self.ones_bf[:, :], sq[:, 0:n], first, last)

    def ffn(self, l, f, tiles, Acols, Bcols, Ccols):
        G = len(tiles)
        for g, (xt, n) in enumerate(tiles):
            self.prenorm(xt, n, Acols, Bcols, self.f_h[g][:, :, 0:n])
        for j in range(FC):
            wb = self.wload(self.w_gu[l, f, j], KC, 256)
            for g, (xt, n) in enumerate(tiles):
                h = self.f_h[g]
                pg = self.psum()
                pu = self.psum()
                for k in range(KC):
                    self.mm(pg[:, 0:n], wb[:, k, 0:128], h[:, k, 0:n], k == 0, k == KC - 1)
                for k in range(KC):
                    self.mm(pu[:, 0:n], wb[:, k, 128:256], h[:, k, 0:n], k == 0, k == KC - 1)
                sg = self.tmp('f32')
                self.act(sg[:, 0:n], pg[:, 0:n], AF.Silu)
                self.tt(self.f_hid[g][:, j, 0:n], sg[:, 0:n], pu[:, 0:n], ALU.mult)
        sss = [self.ssbank() for _ in tiles]
        pend = None
        for m in range(KC):
            wb = self.wload(self.w_dn[l, f, m], FC, 128)
            for g, (xt, n) in enumerate(tiles):
                ps = self.psum()
                for j in range(FC):
                    self.mm(ps[:, 0:n], wb[:, j, :], self.f_hid[g][:, j, 0:n], j == 0, j == FC - 1)
                if pend is not None:
                    pend()
                pend = self.evac_f(ps[:, 0:n], n, self.f_fs[g][:, m, 0:n], sss[g], m == 0, m == KC - 1)
        pend()
        for g, (xt, n) in enumerate(tiles):
            self.post_residual(self.f_fs[g][:, :, 0:n], sss[g], xt, n, Ccols)

    def fm_norm(self, src, nch, n, gcols, out, out_dt_bf=True):
        ss = self.ssbank()
        for c in range(nch):
            sq = self.tmp('sq')
            self.act(sq[:, 0:n], src[:, c, :], AF.Square)
            self.mm(ss[:, 0:n], self.ones_bf[:, :], sq[:, 0:n], c == 0, c == nch - 1)
        r = self.rstd_from_ss(ss[:, 0:n], n, nch * 128)
        for c in range(nch):
            t = self.tmp('f32')
            self.tt(t[:, 0:n], src[:, c, :], r[:, 0:n], ALU.mult)
            self.act(out[:, c, :], t[:, 0:n], AF.Copy, scale=gcols[:, c:c + 1])

    def mixer_p1(self, l, S, group):
        vl = self.vecs
        vb = l * V_N
        for g, ti in enumerate(group):
            xt = S['x'][:, :, ti * NT:(ti + 1) * NT]
            self.prenorm(xt, NT, S['A'][l][1], S['B'][l][1], self.hbuf[g][:, :, 0:NT])
        nblk = 9 if S['rope'] else 8
        import os
        only = os.environ.get('P1B')
        for b in range(nblk):
            if only is not None and str(b) not in only.split(','):
                continue
            wb = self.wload(self.w_in[l, b], KC, 256)
            for g, ti in enumerate(group):
                h = self.hbuf[g]
                t0 = ti * NT
                if b == 2:
                    self.sgu(l, S, g, ti, wb)
                    continue
                pss = []
                if b in (4, 8):
                    ps = self.psum()
                    if b == 4:
                        for k in range(KC):
                            self.mm(ps[:, 0:NT], wb[:, k, 0:128], h[:, k, 0:NT], k == 0, k == KC - 1)
                    else:
                        for k in range(KC):
                            self.mm(ps[0:96, 0:NT], wb[:, k, 0:96], h[:, k, 0:NT], k == 0, k == KC - 1)
                    pss.append(ps)
                else:
                    for c in range(2):
                        ps = self.psum()
                        for k in range(KC):
                            self.mm(ps[:, 0:NT], wb[:, k, c * 128:(c + 1) * 128], h[:, k, 0:NT], k == 0, k == KC - 1)
                        pss.append(ps)
                if b == 0:
                    for c in range(2):
                        self.act(self.zc_sb[g][:, c, :], pss[c][:, 0:NT], AF.Copy)
                    for (po, to, ln) in S['pieces'](ti):
                        self.dma(S['zp_d'][:, :, po + 8:po + 8 + ln], self.zc_sb[g][:, :, to:to + ln])
                elif b == 1:
                    for c in range(2):
                        self.act(self.u_sb[g][:, c, :], pss[c][:, 0:NT], AF.Copy)
                elif b == 3:
                    for c in range(2):
                        self.act(self.cq_sb[:, c, :], pss[c][:, 0:NT], AF.Copy)
                    self.fm_norm(self.cq_sb, 2, NT, vl[:, vb + V_GQ:vb + V_GQ + 2], S['cqn'][:, :, t0:t0 + NT])
                elif b == 4:
                    self.act(self.cq_sb[:, 0, :], pss[0][:, 0:NT], AF.Copy)
                    self.fm_norm(self.cq_sb[:, 0:1, :], 1, NT, vl[:, vb + V_GKV:vb + V_GKV + 1], self.ckvn_f3)
                    ps2 = self.psum()
                    for k in range(KC):
                        self.mm(ps2[0:96, 0:NT], wb[:, k, 128:224], h[:, k, 0:NT], k == 0, k == KC - 1)
                    self.act(self.kr_f[g][64:96, :], ps2[64:96, 0:NT], AF.Copy)
                    if S['rope']:
                        self.dma(self.xin[0:128, t0:t0 + NT], self.ckvn_f[:, :])
                    else:
                        self.dma(self.st_ckv[l, :, t0:t0 + NT], self.ckvn_f[:, :])
                        self.dma(self.st_kr[l, :, t0:t0 + NT], self.kr_f[g][64:96, :])
                        self.copy(S['ckvT'][:, t0:t0 + NT], self.ckvn_f[:, :])
                        self.copy(S['krT'][64:96, t0:t0 + NT], self.kr_f[g][64:96, :])
                elif b == 5:
                    for c in range(2):
                        self.act(self.zb_t[:, c, :], pss[c][:, 0:NT], AF.Copy)
                    self.dma(S['zb_d'][:, :, t0:t0 + NT], self.zb_t)
                elif b == 6:
                    for c in range(2):
                        self.act(self.zc_sb[g][:, c, :], pss[c][:, 0:NT], AF.Copy)
                elif b == 7:
                    for c in range(2):
                        self.tt(self.u_sb[g][:, c, :], self.zc_sb[g][:, c, :], pss[c][:, 0:NT], ALU.mult)
                    for (po, to, ln) in S['pieces'](ti):
                        p1 = S['cpos'](po)
                        self.dma(S['pc_d'][:, :, p1 + 1:p1 + 1 + ln], self.u_sb[g][:, :, to:to + ln])
                elif b == 8:
                    r = slice(64, 96)
                    t1 = self.tmp('f32')
                    t2 = self.tmp('f32')
                    self.tt(t1[r, 0:NT], self.kr_f[g][r, :], self.cosT[r, t0:t0 + NT], ALU.mult)
                    self.tt(t2[r, 0:NT], pss[0][r, 0:NT], self.sinT[r, t0:t0 + NT], ALU.mult)
                    self.tt(t1[r, 0:NT], t1[r, 0:NT], t2[r, 0:NT], ALU.add)
                    self.dma(self.xin[128:160, t0:t0 + NT], t1[r, 0:NT])

    def sgu(self, l, S, g, ti, wb):
        h = self.hbuf[g]
        t0 = ti * NT
        vn = self.vn_sb
        import os
        lvl = int(os.environ.get('SGU', '9'))
        for half in range(2):
            pv = self.psum()
            for bb in range(2):
                blk = half * 2 + bb
                for k in range(KC):
                    self.mm(pv[:, bb * 256:(bb + 1) * 256], h[:, k, blk * 128:(blk + 1) * 128], wb[:, k, 0:256],
                            k == 0, k == KC - 1)
            for bb in range(2):
                if lvl < 2:
                    break
                blk = half * 2 + bb
                junk = self.tmp('f32')
                self.act(junk[:, 0:256], pv[:, bb * 256:(bb + 1) * 256], AF.Square)
                jj, so = junk[:, 0:256], self.ssq[:, blk:blk + 1]
                self.P.op('dve', lambda e, jj=jj, so=so: e.reduce_sum(so, jj, mybir.AxisListType.X), [jj], [so])
                if lvl < 3:
                    continue
                self.act(self.ssq[:, 4 + blk:5 + blk], self.ssq[:, blk:blk + 1], AF.Sqrt, bias=self.eps_col[:, 0:1], scale=1.0 / 256)
                self.recip(self.ssq[:, 8 + blk:9 + blk], self.ssq[:, 4 + blk:5 + blk])
                self.stt(vn[:, blk, :], pv[:, bb * 256:(bb + 1) * 256], self.ssq[:, 8 + blk:9 + blk], self.gsgu[:, :],
                         ALU.mult, ALU.mult)
        for c in range(2):
            if lvl < 4:
                break
            pa = self.psum()
            pb = self.psum()
            for blk in range(4):
                self.mm(pa[:, blk * 128:(blk + 1) * 128], vn[:, blk, c * 128:(c + 1) * 128], self.wsT[:, 2 * c, :], True, True)
                self.mm(pb[:, blk * 128:(blk + 1) * 128], vn[:, blk, c * 128:(c + 1) * 128], self.wsT[:, 2 * c + 1, :], True, True)
            for (rows, pp) in ((slice(0, 64), pa), (slice(64, 128), pb)):
                if lvl < 5:
                    break
                t = self.tmp('f32')
                for blk in range(4):
                    self.tt(t[rows, blk * 128:(blk + 1) * 128], pp[rows, blk * 128:(blk + 1) * 128], self.bsT[rows, c, :], ALU.add)
                self.tt(self.bsg_t[rows, c, :], t[rows, 0:NT], self.u_sb[g][rows, c, :], ALU.mult)
        if lvl >= 6:
            self.dma(S['bsg_d'][:, :, t0:t0 + NT], self.bsg_t)

    def pool_conv(self, l, S, ti):
        vb = l * V_N
        vl = self.vecs
        t0 = ti * NT
        self.dma(self.zb_t, S['zb_d'][:, :, t0:t0 + NT])
        self.dma(self.bsg_t, S['bsg_d'][:, :, t0:t0 + NT])
        for (po, to, ln) in S['pieces'](ti):
            zw = self.zpw
            pw_ = self.pcw
            p1 = S['cpos'](po)
            self.dma(zw[:, :, 0:ln + 16], S['zp_d'][:, :, po:po + ln + 16])
            self.dma(pw_[:, :, 0:ln + 2], S['pc_d'][:, :, p1:p1 + ln + 2])
            first, last = S['edges'](ti, to)
            for c in range(2):
                s2 = self.tmp('pw')
                s4 = self.tmp('pw')
                lo = slice(0, 64)
                hi = slice(64, 128)
                self.tt(s2[:, 1:ln + 16], zw[:, c, 0:ln + 15], zw[:, c, 1:ln + 16], ALU.add)
                if c == 0:
                    self.tt(s4[hi, 2:ln + 14], s2[hi, 1:ln + 13], s2[hi, 3:ln + 15], ALU.add)
                    srcs = ((lo, s2), (hi, s4))
                else:
                    self.tt(s4[:, 2:ln + 14], s2[:, 1:ln + 13], s2[:, 3:ln + 15], ALU.add)
                    s8 = self.tmp('pw')
                    self.tt(s8[:, 4:ln + 12], s4[:, 2:ln + 10], s4[:, 6:ln + 14], ALU.add)
                    s16 = self.tmp('pw')
                    self.tt(s16[hi, 8:ln + 8], s8[hi, 4:ln + 4], s8[hi, 12:ln + 12], ALU.add)
                    srcs = ((lo, s8), (hi, s16))
                dd = self.tmp('bfn')
                iw = vl[:, G_INVW + c:G_INVW + c + 1]
                for (rows, sw) in srcs:
                    self.stt(dd[rows, 0:ln], sw[rows, 8:8 + ln], iw[rows, :], zw[rows, c, 8:8 + ln], ALU.mult, ALU.subtract)
                    for (flag, d0, e0) in ((first, 0, 0), (last, ln - 8, 8)):
                        if flag:
                            t = self.tmp('f32')
                            self.tt(t[rows, 0:8], sw[rows, 8 + d0:16 + d0], S['icntE'][rows, c, e0:e0 + 8], ALU.mult)
                            self.tt(dd[rows, d0:d0 + 8], t[rows, 0:8], zw[rows, c, 8 + d0:16 + d0], ALU.subtract)
                ps = self.psum()
                self.mm(ps[:, 0:ln], self.wpool[:, c, :], dd[:, 0:ln], True, True)
                self.act(self.a_t[:, c, to:to + ln], ps[:, 0:ln], AF.Copy, scale=vl[:, vb + V_PSCALE + c:vb + V_PSCALE + c + 1])
                y = self.tmp('f32')
                cw = vb + V_CONVW
                self.ts(y[:, 0:ln], pw_[:, c, 1:1 + ln], vl[:, cw + 2 + c:cw + 3 + c], None, ALU.mult)
                self.stt(y[:, 0:ln], pw_[:, c, 0:ln], vl[:, cw + c:cw + c + 1], y[:, 0:ln], ALU.mult, ALU.add)
                self.stt(y[:, 0:ln], pw_[:, c, 2:2 + ln], vl[:, cw + 4 + c:cw + 5 + c], y[:, 0:ln], ALU.mult, ALU.add)
                self.tt(self.dc_t[:, c, to:to + ln], y[:, 0:ln], self.zb_t[:, c, to:to + ln], ALU.mult)

    def attention(self, l, S, cT, nk, qsets, kr_fill):
        nkc = nk // 128
        for kb_ in self.kbuf:
            kr_fill(kb_)

        def build_kv(h):
            kb = self.kbuf[h % 2]
            vbuf = self.vbuf[h % 2]
            voff = 0 if h % 2 == 0 else 64
            for k0 in range(0, nk, NT):
                n = min(NT, nk - k0)
                ps = self.psum()
                self.mm(ps[0:64, 0:n], self.wukv[:, h * 128:h * 128 + 64], cT[:, k0:k0 + n], True, True)
                self.copy(kb[0:64, k0:k0 + n], ps[0:64, 0:n])
            for c0 in range(0, nkc, 8):
                ncb = min(8, nkc - c0)
                ps = self.psum()
                for cc in range(ncb):
                    kc = c0 + cc
                    self.mm(ps[:, cc * 64:(cc + 1) * 64], cT[:, kc * 128:(kc + 1) * 128],
                            self.wukv[:, h * 128 + 64:h * 128 + 128], True, True)
                self.copy(vbuf[:, c0:c0 + ncb, voff:voff + 64], ps[:, 0:ncb * 64].rearrange("p (c d) -> p c d", d=64))

        build_kv(0)
        for h in range(8):
            kb = self.kbuf[h % 2]
            vbuf = self.vbuf[h % 2]
            def q_prep(q0, n):
                qh = self.tmp('qh')
                pq = self.psum()
                for k in range(2):
                    self.mm(pq[0:96, 0:n], self.wuq[:, k, h * 96:(h + 1) * 96], S['cqn'][:, k, q0:q0 + n], k == 0, k == 1)
                self.copy(qh[0:64, 0:n], pq[0:64, 0:n])
                r = slice(64, 96)
                if S['rope']:
                    pq2 = self.psum()
                    for k in range(2):
                        self.mm(pq2[0:96, 0:n], self.wuqs[:, k, h * 96:(h + 1) * 96], S['cqn'][:, k, q0:q0 + n], k == 0, k == 1)
                    t1 = self.tmp('f32')
                    t2 = self.tmp('f32')
                    self.tt(t1[r, 0:n], pq[r, 0:n], self.cosT[r, q0:q0 + n], ALU.mult)
                    self.tt(t2[r, 0:n], pq2[r, 0:n], self.sinT[r, q0:q0 + n], ALU.mult)
                    self.tt(qh[r, 0:n], t1[r, 0:n], t2[r, 0:n], ALU.add)
                else:
                    self.copy(qh[r, 0:n], pq[r, 0:n])
                return qh

            qh_next = q_prep(*qsets[0])
            for qi, (q0, n) in enumerate(qsets):
                qh = qh_next
                po = self.ssbank()
                pss = {}

                def emit_st(kc):
                    ps = self.psum()
                    self.mm(ps[:, 0:n], kb[0:96, kc * 128:(kc + 1) * 128], qh[0:96, 0:n], True, True)
                    pss[kc] = ps
                LA = 2
                for kc in range(min(LA, nkc)):
                    emit_st(kc)
                if qi + 1 < len(qsets):
                    qh_next = q_prep(*qsets[qi + 1])
                if qi == 0 and h + 1 < 8:
                    build_kv(h + 1)
                for kc in range(nkc):
                    if kc + LA < nkc:
                        emit_st(kc + LA)
                    if kc == min(10, nkc - 1) and self.att_pend is not None:
                        self.att_pend()
                        self.att_pend = None
                    pt = self.tmp('pt')
                    self.act(pt[:, 0:n], pss.pop(kc)[:, 0:n], AF.Exp, scale=SCALE)
                    self.mm(po[:, 0:n], vbuf[:, kc, :], pt[:, 0:n], kc == 0, kc == nkc - 1)
                dp = 64 if h % 2 == 0 else 0
                rows = slice(0, 64) if h % 2 == 0 else slice(64, 128)
                rs = self.tmp('rs')
                self.recip(rs[dp:dp + 1, 0:n], po[dp:dp + 1, 0:n])

                def fin(po=po, rs=rs, dp=dp, rows=rows, n=n, q0=q0, hh=h):
                    pb = self.psum()
                    self.mm(pb[:, 0:n], self.ones_f[dp:dp + 1, :], rs[dp:dp + 1, 0:n], True, True)
                    bs = self.tmp('f32')
                    self.copy(bs[rows, 0:n], pb[rows, 0:n])
                    self.tt(S['att'][rows, hh // 2, q0:q0 + n], po[rows, 0:n], bs[rows, 0:n], ALU.mult)
                self.att_pend = fin

    def att_flush(self):
        if self.att_pend is not None:
            self.att_pend()
            self.att_pend = None

    def merge(self, l, S, ti):
        vb = l * V_N
        vl = self.vecs
        g = 0
        t0 = ti * NT
        xt = S['x'][:, :, t0:t0 + NT]
        self.prenorm(xt, NT, S['A'][l][1], S['B'][l][1], self.hbuf[g][:, :, 0:NT])
        self.pool_conv(l, S, ti)
        brs = ((self.a_t, 0, 0, 2), (self.bsg_t, 0, 2, 2), (S['att'], t0, 4, 4), (self.dc_t, 0, 8, 2))
        h = self.hbuf[g]
        for m in range(KC):
            accf = self.tmp('acc')
            for half in range(2):
                if half == 0:
                    wbr = self.wload(self.w_br[l, m][:, 0:512], 4, 128)
                    kb0 = 0
                else:
                    wbr = self.wload(self.w_br[l, m][:, 512:1280], 6, 128)
                    kb0 = 4
                wg = self.wload(self.w_gate[l, m, half], KC, 256)
                for bj in range(2):
                    bi = half * 2 + bj
                    (src, off, k0, nk) = brs[bi]
                    pb = self.psum()
                    for k in range(nk):
                        self.mm(pb[:, 0:NT], wbr[:, k0 - kb0 + k, :], src[:, k, off:off + NT], k == 0, k == nk - 1)
                    pg = self.psum()
                    for k in range(KC):
                        self.mm(pg[:, 0:NT], wg[:, k, bj * 128:(bj + 1) * 128], h[:, k, 0:NT], k == 0, k == KC - 1)
                    gt = self.tmp('f32')
                    bc = vb + V_BGATE + bi * 8 + m
                    self.act(gt[:, 0:NT], pg[:, 0:NT], AF.Sigmoid, bias=vl[:, bc:bc + 1])
                    if bi == 0:
                        self.tt(accf[:, 0:NT], gt[:, 0:NT], pb[:, 0:NT], ALU.mult)
                    else:
                        t = self.tmp('f32')
                        self.tt(t[:, 0:NT], gt[:, 0:NT], pb[:, 0:NT], ALU.mult)
                        if bi < 3:
                            self.tt(accf[:, 0:NT], accf[:, 0:NT], t[:, 0:NT], ALU.add)
                        else:
                            self.tt(self.mrg[:, m, :], accf[:, 0:NT], t[:, 0:NT], ALU.add)
        ss = self.ssbank()
        pend = None
        for mo in range(KC):
            wo = self.wload(self.w_o[l, mo], KC, 128)
            ps = self.psum()
            for m in range(KC):
                self.mm(ps[:, 0:NT], wo[:, m, :], self.mrg[:, m, :], m == 0, m == KC - 1)
            if pend is not None:
                pend()
            pend = self.evac_f(ps[:, 0:NT], NT, self.fs[g][:, mo, 0:NT], ss, mo == 0, mo == KC - 1)
        pend()
        self.post_residual(self.fs[g][:, :, 0:NT], ss, xt, NT, S['C'][l][1])

    def v3(self, t, c):
        return t.ap().rearrange("p (c n) -> p c n", c=c)

    def build(self):
        nc = self.nc
        from contextlib import ExitStack
        self.xT_P = self.din("xT_P", [128, KC * NPT])
        self.xT_S = self.din("xT_S", [128, KC * NST])
        self.condT = self.din("condT", [128, KC * 2])
        self.cache_c = self.din("cache_c", [L, 128, 512])
        self.cache_k = self.din("cache_k", [L, 32, 512])
        self.vecs_d = self.din("vecs", [128, G_N])
        self.gsgu_d = self.din("gsgu", [L, 128, 256])
        self.bsT_d = self.din("bsT", [L, 128, 2 * 128])
        self.wsT_d = self.din("wsT", [L, 128, 4 * 128])
        self.wpool_d = self.din("wpool", [L, 128, 2 * 128])
        self.wuq_d = self.din("wuq", [L, 128, 2 * 768])
        self.wuqs_d = self.din("wuqs", [L, 128, 2 * 768])
        self.wukv_d = self.din("wukv", [L, 128, 1024])
        self.rpos_d = self.din("rpos", [128, 2 * NST])
        self.icntP_d = self.din("icntP", [128, 2 * 16])
        self.icntS_d = self.din("icntS", [128, 2 * 16])
        self.w_mod = self.din("w_mod", [L, 36, 128, KC * 128])
        self.w_gu = self.din("w_gu", [L, 2, FC, 128, KC * 256])
        self.w_dn = self.din("w_dn", [L, 2, KC, 128, FC * 128])
        self.w_in = self.din("w_in", [L, 9, 128, KC * 256])
        self.w_gate = self.din("w_gate", [L, KC, 2, 128, KC * 256])
        self.w_br = self.din("w_br", [L, KC, 128, 10 * 128])
        self.w_o = self.din("w_o", [L, KC, 128, KC * 128])
        self.yT_P = self.dout("yT_P", [128, KC * NPT])
        self.yT_S = self.dout("yT_S", [128, KC * NST])
        self.st_ckv = self.dout("st_ckv", [L, 128, NPT])
        self.st_kr = self.dout("st_kr", [L, 32, NPT])
        self.xin = nc.dram_tensor("xin", [XR, NST], F32)
        self.mg_in = nc.dram_tensor("mg_in", [128, 144], F32)
        self.mg_out = nc.dram_tensor("mg_out", [2 * 128, 144], F32)
        self.xout = nc.dram_tensor("xout", [2 * XR, NST], F32)
        dbg = self.debug_stage

        with ExitStack() as es:
            P = self.P
            self.pbanks = []
            for i in range(8):
                t = es.enter_context(nc.psum_tensor("pb%d" % i, [128, 512], F32))
                P.ps.add(t[:, 0:1].name)
                self.pbanks.append(t)
            self.vecs = self.sb("vecs", G_N, F32)
            self.ones_bf = self.sb("ones_bf", 128, BF16)
            self.ones_f = self.sb("ones_f", 128, F32)
            self.zeros_f = self.sb("zeros_f", 16, F32)
            self.eps_col = self.sb("eps", 1, F32)
            self.mods = self.sb("mods", L * 2 * 72, F32)
            self.drv = self.sb("drv", L * 2 * 3 * 16, F32)
            self.cs_bf = self.sb("cs_bf", 16, BF16)
            self.cs_f = self.sb("cs_f", 16, F32)
            self.mg_sb = self.sb("mg_sb", 144, F32)
            self.mg_all = self.sb("mg_all", 288, F32)
            self.ssq = self.sb("ssq", 16, F32)
            self.gsgu = self.sb("gsgu", 256, F32)
            self.bsT = self.v3(self.sb("bsT", 256, F32), 2)
            self.wsT = self.v3(self.sb("wsT", 512, BF16), 4)
            self.wpool = self.v3(self.sb("wpool", 256, BF16), 2)
            self.wuq = self.v3(self.sb("wuq", 1536, BF16), 2)
            self.wuqs = self.v3(self.sb("wuqs", 1536, BF16), 2)
            self.wukv = self.sb("wukv", 1024, BF16)
            self.wring = [self.sb("wring%d" % i, 2816, BF16) for i in range(4)]
            self.tmps = {
                'f32': [self.sb("tf%d" % i, NT, F32) for i in range(5)],
                'nrm': [self.sb("tn%d" % i, NT, F32) for i in range(2)],
                'sq': [self.sb("tq%d" % i, NT, BF16) for i in range(2)],
            }
            base = self.sp_top
            print("persistent sbuf bytes", base - 16512, flush=True)
            self.dma(self.vecs[:, :], self.vecs_d[:, :])
            self.memset(self.ones_bf[:, :], 1.0)
            self.memset(self.ones_f[:, :], 1.0)
            self.memset(self.zeros_f[:, :], 0.0)
            self.memset(self.eps_col[:, :], EPS)
            self.dma(self.cs_f[:, :], self.condT[:, :])
            self.act(self.cs_bf[:, :], self.cs_f[:, :], AF.Silu)
            csv = self.cs_bf.ap().rearrange("p (k c) -> p k c", c=2)
            pm = self.pbanks[0]
            for l in range(L):
                for blk in range(36):
                    wb = self.wload(self.w_mod[l, blk], KC, 128)
                    o0 = (l * 36 + blk) * 2
                    for k in range(KC):
                        self.mm(pm[:, o0:o0 + 2], wb[:, k, :], csv[:, k, :], k == 0, k == KC - 1)
            self.act(self.mg_sb[:, :], pm[:, 0:144], AF.Copy)
            self.dma(self.mg_in[:, :], self.mg_sb[:, :])
            mi = self.mg_in.ap().opt()
            mo_ = self.mg_out.ap().opt()
            self.P.op('pool', lambda e: e.collective_compute("AllGather", ALU.bypass, replica_groups=[[0, 1], [2, 3], [4, 5], [6, 7]],
                                                            ins=[mi], outs=[mo_]), [self.mg_in.ap()], [self.mg_out.ap()], kind='cc')
            self.dma(self.mg_all.ap().rearrange("p (r c) -> p r c", r=2), self.mg_out.ap().rearrange("(r p) c -> p r c", r=2))
            gv = self.mg_all.ap().rearrange("p (r l b j) -> p r l b j", r=2, l=L, b=36)
            for l in range(L):
                bm = self.vecs[:, l * V_N + V_BMOD:l * V_N + V_BMOD + 72].rearrange("p (r b) -> p r b", r=2)
                for c in range(2):
                    mc = self.mods[:, (l * 2 + c) * 72:(l * 2 + c) * 72 + 72].rearrange("p (r b) -> p r b", r=2)
                    self.tt(mc, gv[:, :, l, :, c], bm, ALU.add)
            streams = {}
            for c, nm in ((0, 'P'), (1, 'S')):
                A = [[None] * 3 for _ in range(L)]
                Bc = [[None] * 3 for _ in range(L)]
                Cc = [[None] * 3 for _ in range(L)]
                for l in range(L):
                    mo = (l * 2 + c) * 72
                    for s in range(3):
                        do = ((l * 2 + c) * 3 + s) * 16
                        a = self.drv[:, do:do + 8]
                        cc = self.drv[:, do + 8:do + 16]
                        gp = self.vecs[:, l * V_N + V_GPRE + s * 8:l * V_N + V_GPRE + s * 8 + 8]
                        go = self.vecs[:, l * V_N + V_GPOST + s * 8:l * V_N + V_GPOST + s * 8 + 8]
                        self.stt(a, self.mods[:, mo + (3 * s + 1) * 8:mo + (3 * s + 1) * 8 + 8], 1.0, gp, ALU.add, ALU.mult)
                        self.stt(cc, self.mods[:, mo + (3 * s + 2) * 8:mo + (3 * s + 2) * 8 + 8], 0.5 if s != 1 else 1.0, go, ALU.mult, ALU.mult)
                        A[l][s] = a
                        Bc[l][s] = self.mods[:, mo + 3 * s * 8:mo + 3 * s * 8 + 8]
                        Cc[l][s] = cc
                streams[nm] = dict(A=A, B=Bc, C=Cc)

            for nm in ('P', 'S'):
                S = streams[nm]
                rope = nm == 'S'
                ntok = NST if rope else NPT
                ntile = ntok // NT
                TG = 1 if rope else 2
                self.sp_top = base
                S['rope'] = rope
                xs = self.sb("x_" + nm, KC * ntok, F32)
                S['x'] = xs.ap().rearrange("p (k n) -> p k n", k=KC)
                srcv = (self.xT_S if rope else self.xT_P).ap().rearrange("p (k n) -> p k n", k=KC)
                for k in range(KC):
                    self.dma(S['x'][:, k, :], srcv[:, k, :])
                if rope:
                    lp = NST + 16
                    lc = NST + 2
                    S['pieces'] = lambda ti: [(ti * NT, 0, NT)]
                    S['cpos'] = lambda po: po
                    S['edges'] = lambda ti, to: (ti == 0, ti == 3)
                else:
                    lp = 4 * 272
                    lc = 4 * 258
                    S['pieces'] = lambda ti: [((2 * ti) * 272, 0, 256), ((2 * ti + 1) * 272, 256, 256)]
                    S['cpos'] = lambda po: (po // 272) * 258
                    S['edges'] = lambda ti, to: (True, True)
                S['zp_d'] = nc.dram_tensor("zp_d" + nm, [128, 2, lp], F32).ap()
                S['pc_d'] = nc.dram_tensor("pc_d" + nm, [128, 2, lc], F32).ap()
                S['zb_d'] = nc.dram_tensor("zb_d" + nm, [128, 2, ntok], BF16).ap()
                S['bsg_d'] = nc.dram_tensor("bsg_d" + nm, [128, 2, ntok], BF16).ap()
                S['icntE'] = self.v3(self.sb("icntE_" + nm, 32, F32), 2)
                self.dma(S['icntE'], (self.icntS_d if rope else self.icntP_d).ap().rearrange("p (c n) -> p c n", c=2))
                if rope:
                    self.cosT = self.sb("cosT", NST, BF16)
                    self.sinT = self.sb("sinT", NST, BF16)
                    self.halo = self.sb("halo", 36, F32)
                    self.halo2 = self.sb("halo2", 36, F32)
                ffn_base = self.sp_top
                TGF = 2
                self.f_fs, self.f_h, self.f_hid = [], [], []
                for g in range(TGF):
                    a0 = self.sp_top
                    self.f_fs.append(self.v3(self.sb("ffs%d_%s" % (g, nm), KC * NT, F32), KC))
                    self.f_h.append(self.v3(self.sb("fh%d_%s" % (g, nm), KC * NT, BF16, at=a0), KC))
                    self.f_hid.append(self.v3(self.sb("fhid%d_%s" % (g, nm), FC * NT, BF16), FC))
                top_ffn = self.sp_top
                self.sp_top = ffn_base
                S['cqn'] = self.v3(self.sb("cqn_" + nm, 2 * ntok, BF16), 2)
                S['att'] = self.v3(self.sb("att_" + nm, 4 * ntok, BF16), 4)
                if not rope:
                    S['ckvT'] = self.sb("ckvT_P", NPT, BF16)
                    S['krT'] = self.sb("krT_P", NPT, BF16)
                top_stream = self.sp_top
                self.hbuf, self.fs = [], []
                for g in range(TG):
                    a0 = self.sp_top
                    self.fs.append(self.v3(self.sb("fs%d_%s" % (g, nm), KC * NT, F32), KC))
                    self.hbuf.append(self.v3(self.sb("h%d_%s" % (g, nm), KC * NT, BF16, at=a0), KC))
                top_common = self.sp_top
                self.zb_t = self.v3(self.sb("zbt_" + nm, 2 * NT, BF16), 2)
                self.bsg_t = self.v3(self.sb("bsgt_" + nm, 2 * NT, BF16), 2)
                top_m0 = self.sp_top
                self.u_sb = [self.v3(self.sb("u%d_%s" % (g, nm), 2 * NT, F32), 2) for g in range(TG)]
                self.zc_sb = [self.v3(self.sb("zc%d_%s" % (g, nm), 2 * NT, F32), 2) for g in range(TG)]
                self.kr_f = [self.sb("krf%d_%s" % (g, nm), NT, F32) for g in range(TG)]
                self.cq_sb = self.v3(self.sb("cq_" + nm, 2 * NT, F32), 2)
                self.ckvn_f = self.sb("ckvn_" + nm, NT, F32)
                self.ckvn_f3 = self.ckvn_f.ap().rearrange("p (o n) -> p o n", o=1)
                self.vn_sb = self.v3(self.sb("vn_" + nm, 4 * 256, BF16), 4)
                top_p1 = self.sp_top
                self.sp_top = top_stream
                nk = NKS if rope else 256
                nkb = 2
                self.kbuf = [self.sb("kb%d_%s" % (i, nm), nk, BF16) for i in range(nkb)]
                self.vbuf = [self.sb("vb%d_%s" % (i, nm), (nk // 128) * 128, BF16).ap().rearrange("p (c d) -> p c d", d=128) for i in range(2)]
                if rope:
                    self.c_all = self.sb("c_all", NKS, BF16)
                self.tmps['pt'] = [self.sb("tp%d_%s" % (i, nm), NT, BF16) for i in range(3)]
                self.tmps['qh'] = [self.sb("tqh%d_%s" % (i, nm), NT, BF16) for i in range(2)]
                self.tmps['rs'] = [self.sb("trs%d_%s" % (i, nm), NT, F32) for i in range(2)]
                top_att = self.sp_top
                self.sp_top = top_m0
                self.zpw = self.v3(self.sb("zpw_" + nm, 2 * (NT + 16), F32), 2)
                self.pcw = self.v3(self.sb("pcw_" + nm, 2 * (NT + 2), F32), 2)
                self.a_t = self.v3(self.sb("at_" + nm, 2 * NT, BF16), 2)
                self.dc_t = self.v3(self.sb("dct_" + nm, 2 * NT, BF16), 2)
                self.mrg = self.v3(self.sb("mrg_" + nm, KC * NT, BF16), KC)
                self.tmps['pw'] = [self.sb("pw%d_%s" % (i, nm), NT + 16, F32) for i in range(4)]
                self.tmps['acc'] = [self.sb("ta%d_%s" % (i, nm), NT, F32) for i in range(1)]
                self.tmps['bfn'] = [self.sb("tb%d_%s" % (i, nm), NT, BF16) for i in range(1)]
                top_mg = self.sp_top
                print("stream", nm, "tops: common", top_common, "ffn", top_ffn, "p1", top_p1, "att", top_att, "merge", top_mg, flush=True)

                for c in range(2):
                    if rope:
                        pass
                    else:
                        for sq in range(4):
                            self.dma(S['zp_d'][:, c, sq * 272:sq * 272 + 8], self.zeros_f[:, 0:8])
                            self.dma(S['zp_d'][:, c, sq * 272 + 264:sq * 272 + 272], self.zeros_f[:, 0:8])
                            self.dma(S['pc_d'][:, c, sq * 258:sq * 258 + 1], self.zeros_f[:, 0:1])
                            self.dma(S['pc_d'][:, c, sq * 258 + 257:sq * 258 + 258], self.zeros_f[:, 0:1])
                if rope:
                    self.rope_tables()
                    self.dma(self.xin[162:163, 512:2048].rearrange("r (q e) -> (r q) e", e=16), self.zeros_f[0:96, 0:16])

                groups = [list(range(g0, min(g0 + TG, ntile))) for g0 in range(0, ntile, TG)]
                fgroups = [list(range(g0, min(g0 + TGF, ntile))) for g0 in range(0, ntile, TGF)]
                stop = False
                for l in range(L):
                    self.dma(self.gsgu[:, :], self.gsgu_d[l])
                    self.dma(self.bsT, self.bsT_d[l].rearrange("p (c n) -> p c n", c=2))
                    self.dma(self.wsT, self.wsT_d[l].rearrange("p (g n) -> p g n", g=4), q='pool')
                    self.dma(self.wpool, self.wpool_d[l].rearrange("p (c n) -> p c n", c=2), q='pool')
                    self.dma(self.wuq, self.wuq_d[l].rearrange("p (k n) -> p k n", k=2), q='pool')
                    self.dma(self.wuqs, self.wuqs_d[l].rearrange("p (k n) -> p k n", k=2), q='pool')
                    self.dma(self.wukv[:, :], self.wukv_d[l], q='pool')
                    for grp in fgroups:
                        self.ffn(l, 0, [(S['x'][:, :, ti * NT:(ti + 1) * NT], NT) for ti in grp], S['A'][l][0], S['B'][l][0], S['C'][l][0])
                    if dbg == 'ffn0':
                        break
                    for grp in groups:
                        self.mixer_p1(l, S, grp)
                    if dbg == 'p1':
                        break
                    for i in range(2):
                        self.memset(self.vbuf[i], 0.0)
                    for kb in self.kbuf:
                        self.memset(kb[:, :], 0.0)
                    self.memset(self.vbuf[0][:, :, 64:65], 1.0)
                    self.memset(self.vbuf[1][:, :, 0:1], 1.0)
                    if rope:
                        self.exchange(l, S)
                    if dbg == 'xch':
                        break
                    if rope:

                        def kr_fill(kb, l=l):
                            self.dma(kb[64:96, 0:512], self.cache_k[l], q='pool')
                            self.dma(kb[64:96, 512:512 + NST], self.xout[128:160, :], q='pool')
                            self.dma(kb[64:96, 512 + NST:NKS], self.xout[XR + 128:XR + 160, :], q='pool')
                        self.attention(l, S, self.c_all, NKS, [(ti * NT, NT) for ti in range(ntile)], kr_fill)
                    else:
                        for sq in range(4):
                            def kr_fill(kb, sq=sq):
                                self.copy(kb[64:96, 0:256], S['krT'][64:96, sq * 256:(sq + 1) * 256])
                            self.attention(l, S, S['ckvT'][:, sq * 256:(sq + 1) * 256], 256, [(sq * 256, 256)], kr_fill)
                    self.att_flush()
                    if dbg == 'att0':
                        break
                    for ti in range(ntile):
                        self.merge(l, S, ti)
                    if dbg == 'mix0':
                        break
                    for grp in fgroups:
                        self.ffn(l, 1, [(S['x'][:, :, ti * NT:(ti + 1) * NT], NT) for ti in grp], S['A'][l][2], S['B'][l][2], S['C'][l][2])
                    if dbg == 'l0':
                        break
                dst = (self.yT_S if rope else self.yT_P).ap().rearrange("p (k n) -> p k n", k=KC)
                for k in range(KC):
                    self.dma(dst[:, k, :], S['x'][:, k, :])
                if dbg in ('att0',):
                    self.dbg_att = S['att']
            block = es.enter_context(nc.Block())
            P.finalize(es, block)
        return nc

    def rope_tables(self):
        r = slice(64, 96)
        self.dma(self.cosT[r, :], self.rpos_d[64:96, 0:NST], q='pool')
        self.dma(self.sinT[r, :], self.rpos_d[64:96, NST:2 * NST], q='pool')

    def exchange(self, l, S):
        zpd, pcd = S['zp_d'], S['pc_d']
        hv, hv2 = self.halo, self.halo2
        x160 = self.xin[160:162, :].rearrange("r (q e) -> (r q) e", e=32)
        x162 = self.xin[162:163, 0:512].rearrange("r (q e) -> (r q) e", e=4)
        for c in range(2):
            self.dma(hv[:, c * 16:c * 16 + 8], zpd[:, c, 8:16])
            self.dma(hv[:, c * 16 + 8:c * 16 + 16], zpd[:, c, NST:NST + 8])
            self.dma(hv[:, 32 + c * 2:32 + c * 2 + 1], pcd[:, c, 1:2])
            self.dma(hv[:, 32 + c * 2 + 1:32 + c * 2 + 2], pcd[:, c, NST:NST + 1])
        self.dma(x160, hv[:, 0:32])
        self.dma(x162, hv[:, 32:36])
        xi = self.xin.ap().opt()
        xo = self.xout.ap().opt()
        self.P.op('pool', lambda e: e.collective_compute("AllGather", ALU.bypass, replica_groups=[[0, 1], [2, 3], [4, 5], [6, 7]],
                                                        ins=[xi], outs=[xo]), [self.xin.ap()], [self.xout.ap()], kind='cc')
        self.dma(self.c_all[:, 0:512], self.cache_c[l], q='pool')
        self.dma(self.c_all[:, 512:512 + NST], self.xout[0:128, :], q='pool')
        self.dma(self.c_all[:, 512 + NST:NKS], self.xout[XR:XR + 128, :], q='pool')
        o160a = self.xout[160:162, :].rearrange("r (q e) -> (r q) e", e=32)
        o162a = self.xout[162:163, 0:512].rearrange("r (q e) -> (r q) e", e=4)
        o160b = self.xout[XR + 160:XR + 162, :].rearrange("r (q e) -> (r q) e", e=32)
        o162b = self.xout[XR + 162:XR + 163, 0:512].rearrange("r (q e) -> (r q) e", e=4)
        self.dma(hv[:, 0:32], o160a)
        self.dma(hv[:, 32:36], o162a)
        self.dma(hv2[:, 0:32], o160b)
        self.dma(hv2[:, 32:36], o162b)
        ml = self.vecs[:, G_ML:G_ML + 1]
        mr = self.vecs[:, G_MR:G_MR + 1]
        self.ts(hv[:, :], hv[:, :], ml, None, ALU.mult)
        self.ts(hv2[:, :], hv2[:, :], mr, None, ALU.mult)
        for c in range(2):
            self.dma(zpd[:, c, 0:8], hv[:, c * 16 + 8:c * 16 + 16])
            self.dma(zpd[:, c, 8 + NST:16 + NST], hv2[:, c * 16:c * 16 + 8])
            self.dma(pcd[:, c, 0:1], hv[:, 32 + c * 2 + 1:32 + c * 2 + 2])
            self.dma(pcd[:, c, NST + 1:NST + 2], hv2[:, 32 + c * 2:32 + c * 2 + 1])


def _tile_w(W, bw):
    Din, Dout = W.shape
    kc = Din // 128
    nb = Dout // bw
    t = W.reshape(kc, 128, nb, bw).transpose(2, 1, 0, 3)
    return np.ascontiguousarray(t).reshape(nb, 128, kc * bw)


def _cols(v):
    return np.ascontiguousarray(v.reshape(-1, 128).T)


def _host_prep(inp):
    f = np.float32
    g = {k: np.asarray(v) for k, v in inp.items()}
    sh = {}
    wgu = np.empty((L, 2, FC, 128, KC * 256), f)
    wdn = np.empty((L, 2, KC, 128, FC * 128), f)
    for l in range(L):
        for j in range(2):
            W = g['w_ffn_gu'][l, j]
            Wg = W[:, :DFF].reshape(KC, 128, FC, 128)
            Wu = W[:, DFF:].reshape(KC, 128, FC, 128)
            t = np.stack([Wg, Wu], axis=3)
            wgu[l, j] = t.transpose(2, 1, 0, 3, 4).reshape(FC, 128, KC * 256)
            wdn[l, j] = _tile_w(g['w_ffn_dn'][l, j], 128)
    sh['w_gu'] = wgu
    sh['w_dn'] = wdn
    win = np.zeros((L, 9, 128, KC * 256), f)
    perm = np.concatenate([np.arange(8, 16), np.arange(0, 8), np.arange(24, 32), np.arange(16, 24)])
    for l in range(L):
        W = g['w_in'][l]
        Wp = np.zeros((D, 9 * 256), f)
        Wp[:, 0:256] = W[:, 0:256]
        Wp[:, 256:512] = W[:, 256:512]
        Wp[:, 512:768] = W[:, 512:768]
        Wp[:, 768:1024] = W[:, 768:1024]
        Wp[:, 1024:1152] = W[:, 1024:1152]
        Wp[:, 1152 + 64:1152 + 96] = W[:, 1152:1184]
        Wp[:, 1280:1536] = W[:, 1184:1440]
        Wp[:, 1536:1792] = W[:, 1440:1696]
        Wp[:, 1792:2048] = W[:, 1696:1952]
        Wp[:, 2048 + 64:2048 + 96] = W[:, 1152:1184][:, perm]
        win[l] = _tile_w(Wp, 256)
    sh['w_in'] = win
    wgate = np.empty((L, KC, 2, 128, KC * 256), f)
    wbr = np.empty((L, KC, 128, 10 * 128), f)
    wo = np.empty((L, KC, 128, KC * 128), f)
    for l in range(L):
        W = g['w_gate'][l].reshape(D, 4, KC, 128)
        Wm = W.transpose(0, 2, 1, 3).reshape(D, KC * 2, 256)
        Wm = Wm.reshape(D, KC * 2 * 256)
        wgate[l] = _tile_w(Wm, 256).reshape(KC, 2, 128, KC * 256)
        Wb = np.concatenate([g['w_br_pool'][l], g['w_br_sgu'][l], g['w_br_mla'][l], g['w_br_conv'][l]], axis=0)
        wbr[l] = _tile_w(Wb, 128)
        wo[l] = _tile_w(g['w_o'][l], 128)
    sh['w_gate'] = wgate
    sh['w_br'] = wbr
    sh['w_o'] = wo
    vecs = np.zeros((128, G_N), f)
    for l in range(L):
        b = l * V_N
        vecs[:, b + V_BMOD:b + V_BMOD + 72] = _cols(g['b_mod'][l])
        for s in range(3):
            vecs[:, b + V_GPRE + s * 8:b + V_GPRE + s * 8 + 8] = _cols(g['g_pre'][l, s])
            vecs[:, b + V_GPOST + s * 8:b + V_GPOST + s * 8 + 8] = _cols(g['g_post'][l, s])
        vecs[:, b + V_PSCALE:b + V_PSCALE + 2] = _cols(g['pool_scale'][l])
        vecs[:, b + V_GQ:b + V_GQ + 2] = _cols(g['g_q'][l])
        vecs[:, b + V_GKV:b + V_GKV + 1] = _cols(g['g_kv'][l])
        for k in range(3):
            vecs[:, b + V_CONVW + k * 2:b + V_CONVW + k * 2 + 2] = _cols(g['conv_w'][l, k])
        vecs[:, b + V_BGATE:b + V_BGATE + 32] = _cols(g['b_gate'][l])
    half_d = 16
    freqs = (10000.0 ** (-(2.0 * np.arange(8, dtype=np.float32)) / half_d)).astype(f)
    for ff in range(32):
        vecs[64 + ff, G_FREQ] = freqs[ff % 8]
        vecs[64 + ff, G_SIGN] = -1.0 if (ff % 16) < 8 else 1.0
    for c in range(2):
        vecs[0:64, G_INVW + c] = 1.0 / (2, 8)[c]
        vecs[64:128, G_INVW + c] = 1.0 / (4, 16)[c]
    sh['vecs'] = vecs
    sh['gsgu'] = np.ascontiguousarray(np.broadcast_to(g['g_sgu'][:, None, :], (L, 128, 256))).astype(f)
    bsT = np.zeros((L, 128, 2, 128), f)
    for l in range(L):
        for c in range(2):
            bsT[l, 0:64, c, :] = g['b_sgu'][l, 2 * c][None, :]
            bsT[l, 64:128, c, :] = g['b_sgu'][l, 2 * c + 1][None, :]
    sh['bsT'] = bsT.reshape(L, 128, 256)
    sh['wsT'] = np.ascontiguousarray(g['w_sgu'].transpose(0, 3, 1, 2)).reshape(L, 128, 512)
    wpool = np.zeros((L, 128, 2, 128), f)
    for l in range(L):
        for c in range(2):
            wpool[l, 0:64, c, 0:64] = g['w_pool'][l, 2 * c]
            wpool[l, 64:128, c, 64:128] = g['w_pool'][l, 2 * c + 1]
    sh['wpool'] = wpool.reshape(L, 128, 256)
    wuq = g['w_uq']
    wuqs = wuq.copy()
    for h in range(8):
        wuqs[:, :, h * 96 + 64:h * 96 + 96] = wuq[:, :, h * 96 + 64:h * 96 + 96][:, :, perm]
    sh['wuq'] = np.ascontiguousarray(wuq.reshape(L, 2, 128, 768).transpose(0, 2, 1, 3)).reshape(L, 128, 1536)
    sh['wuqs'] = np.ascontiguousarray(wuqs.reshape(L, 2, 128, 768).transpose(0, 2, 1, 3)).reshape(L, 128, 1536)
    sh['wukv'] = np.ascontiguousarray(g['w_ukv'])
    def edge_tab(first_edge, last_edge):
        t = np.zeros((128, 2, 16), f)
        for c in range(2):
            for hf in range(2):
                w = ((2, 4), (8, 16))[c][hf]
                rows = slice(hf * 64, hf * 64 + 64)
                for i in range(8):
                    cnt_f = (min(i + w // 2, 10 ** 9) - max(i - w // 2, 0)) if first_edge else w
                    d = 8 - i
                    cnt_l = (min(w // 2, d) + w // 2) if last_edge else w
                    t[rows, c, i] = 1.0 / cnt_f
                    t[rows, c, 8 + i] = 1.0 / cnt_l
        return t.reshape(128, 32)
    sh['icntP'] = edge_tab(True, True)
    sh['_edge'] = edge_tab
    return g, sh


_NC_CACHE = {}


def _get_nc(debug_stage=None):
    if debug_stage not in _NC_CACHE:
        b = Builder(debug_stage)
        _NC_CACHE[debug_stage] = (b.build(), b)
    return _NC_CACHE[debug_stage]


def _in_maps(inputs):
    f = np.float32
    g, sh = _host_prep(inputs)
    in_maps = []
    for i in range(8):
        b, half = i // 2, i % 2
        m = dict((k, v) for k, v in sh.items() if not k.startswith('_'))
        xp = g['x_prompt'][4 * i:4 * i + 4].reshape(NPT, KC, 128)
        m['xT_P'] = np.ascontiguousarray(xp.transpose(2, 1, 0)).reshape(128, KC * NPT)
        xs = g['x_sample'][b, half * NST:(half + 1) * NST].reshape(NST, KC, 128)
        m['xT_S'] = np.ascontiguousarray(xs.transpose(2, 1, 0)).reshape(128, KC * NST)
        cond = np.stack([g['c_ctx'], g['c'][b]], axis=-1)
        m['condT'] = np.ascontiguousarray(cond.reshape(KC, 128, 2).transpose(1, 0, 2)).reshape(128, KC * 2)
        m['w_mod'] = np.stack([_tile_w(g['w_mod'][l][:, half * 4608:(half + 1) * 4608], 128) for l in range(L)])
        m['cache_c'] = np.ascontiguousarray(g['cache_ckv'][b].transpose(0, 2, 1))
        m['cache_k'] = np.ascontiguousarray(g['cache_krope'][b].transpose(0, 2, 1))
        v = sh['vecs'].copy()
        v[:, G_ML] = 1.0 if half == 1 else 0.0
        v[:, G_MR] = 1.0 if half == 0 else 0.0
        v[:, G_SEL:G_SEL + 4] = 0.0
        v[:, G_SEL + b] = 1.0
        m['vecs'] = v
        pos = half * NST + np.arange(NST)
        rp = np.zeros((128, 2 * NST), f)
        for ff in range(32):
            pv = ((pos // 64) if ff < 16 else (pos % 64)).astype(f)
            ang = pv * sh['vecs'][64 + ff, G_FREQ]
            rp[64 + ff, 0:NST] = np.cos(ang)
            rp[64 + ff, NST:] = np.sin(ang) * sh['vecs'][64 + ff, G_SIGN]
        m['rpos'] = rp
        m['icntS'] = sh['_edge'](half == 0, half == 1)
        in_maps.append(m)
    return in_maps


def kernel(debug_stage=None, **inputs):
    f = np.float32
    nc, _ = _get_nc(debug_stage)
    in_maps = _in_maps(inputs)
    res = run_bass_kernel_spmd(nc, in_maps, core_ids=list(range(8)))
    R = res.results
    y_prompt = np.empty((32, 256, D), f)
    y_sample = np.empty((4, 4096, D), f)
    st_ckv = np.empty((32, L, 256, 128), f)
    st_kr = np.empty((32, L, 256, 32), f)
    for i in range(8):
        b, half = i // 2, i % 2
        yp = R[i]['yT_P'].reshape(128, KC, NPT).transpose(2, 1, 0).reshape(4, 256, D)
        y_prompt[4 * i:4 * i + 4] = yp
        ys = R[i]['yT_S'].reshape(128, KC, NST).transpose(2, 1, 0).reshape(NST, D)
        y_sample[b, half * NST:(half + 1) * NST] = ys
        sc = R[i]['st_ckv'].reshape(L, 128, 4, 256).transpose(2, 0, 3, 1)
        st_ckv[4 * i:4 * i + 4] = sc
        sk = R[i]['st_kr'].reshape(L, 32, 4, 256).transpose(2, 0, 3, 1)
        st_kr[4 * i:4 * i + 4] = sk
    return (y_prompt, y_sample, st_ckv, st_kr)
```

```python
import numpy as np
import concourse.bass as bass
import concourse.mybir as mybir
from concourse.bass_utils import run_bass_kernel_spmd

F32 = mybir.dt.float32
BF16 = mybir.dt.bfloat16
AF = mybir.ActivationFunctionType
ALU = mybir.AluOpType

D = 1024
KC = 8
DFF = 2816
FC = 22
NT = 512
L = 2
EPS = 1e-6
NPT = 1024
NST = 2048
NKS = 512 + 4096
XR = 163
SCALE = 96 ** -0.5
ESZ = {F32: 4, BF16: 2}

V_BMOD = 0
V_GPRE = 72
V_GPOST = 96
V_PSCALE = 120
V_GQ = 122
V_GKV = 124
V_CONVW = 125
V_BGATE = 131
V_N = 163
G_FREQ = L * V_N
G_SIGN = G_FREQ + 1
G_ML = G_FREQ + 2
G_MR = G_FREQ + 3
G_INVW = G_FREQ + 4
G_SEL = G_FREQ + 6
G_N = G_FREQ + 10


class Prog:
    RING = 8

    def __init__(self, nc):
        self.nc = nc
        self.ins = []
        self.live_w = {}
        self.live_r = {}
        self.sb = {}
        self.ps = set()
        self.dram_out = set()

    def acc(self, ap):
        name = ap.name
        apl = ap.ap
        if name in self.sb:
            base, es = self.sb[name]
            ps = apl[0][0]
            fo = ap.offset % ps
            ext = sum((c - 1) * s for s, c in apl[1:])
            return ('sb', base + fo * es, base + (fo + ext + 1) * es)
        if name in self.ps:
            return (name, 0, 512)
        ext = sum((c - 1) * s for s, c in apl)
        return (name, ap.offset, ap.offset + ext + 1)

    def op(self, eng, fn, reads=(), writes=(), kind='c'):
        idx = len(self.ins)
        racc = [self.acc(a) for a in reads]
        wacc = [self.acc(a) for a in writes]
        raw, other = set(), set()
        for (sp, lo, hi) in racc:
            for (l2, h2, j) in self.live_w.get(sp, ()):
                if l2 < hi and lo < h2:
                    raw.add(j)
        for (sp, lo, hi) in wacc:
            for (l2, h2, j) in self.live_w.get(sp, ()):
                if l2 < hi and lo < h2:
                    other.add(j)
            for (l2, h2, j) in self.live_r.get(sp, ()):
                if l2 < hi and lo < h2:
                    other.add(j)
        deps = set()
        for j in raw | other:
            pj = self.ins[j]
            if pj['eng'] == eng and kind == 'c' and pj['kind'] == 'c':
                if eng == 'pe':
                    continue
            deps.add(j)
        deps.discard(idx)
        for (sp, lo, hi) in wacc:
            lw = self.live_w.setdefault(sp, [])
            lw[:] = [r for r in lw if not (lo <= r[0] and r[1] <= hi)]
            lr = self.live_r.setdefault(sp, [])
            lr[:] = [r for r in lr if not (lo <= r[0] and r[1] <= hi)]
            lw.append((lo, hi, idx))
        for (sp, lo, hi) in racc:
            lr = self.live_r.setdefault(sp, [])
            if kind == 'c':
                lr[:] = [r for r in lr if not (r[0] == lo and r[1] == hi and self.ins[r[2]]['eng'] == eng
                                               and self.ins[r[2]]['kind'] == 'c')]
            lr.append((lo, hi, idx))
        is_out = any(a.name in self.dram_out for a in writes)
        self.ins.append(dict(eng=eng, fn=fn, kind=kind, deps=deps, out=is_out))
        return idx

    def finalize(self, es, block):
        nc = self.nc
        ins = self.ins
        has_dep = [False] * len(ins)
        for it in ins:
            for j in it['deps']:
                has_dep[j] = True
        engs = ['pe', 'act', 'dve', 'pool', 'sp']
        esem = {e: es.enter_context(nc.semaphore('s_' + e)) for e in engs}
        ring = {q: [es.enter_context(nc.semaphore('r_%s%d' % (q, i))) for i in range(self.RING)] for q in ('sp', 'pool')}
        ccsem = es.enter_context(nc.semaphore('s_cc'))
        ecnt = {e: 0 for e in engs}
        dcnt = {'sp': 0, 'pool': 0}
        cccnt = 0
        for i, it in enumerate(ins):
            e = it['eng']
            it['prewait'] = None
            if it['kind'] == 'c':
                if has_dep[i]:
                    ecnt[e] += 1
                    it['sig'] = (esem[e], ecnt[e], 1)
                else:
                    it['sig'] = None
            elif it['kind'] == 'd':
                n = dcnt[e]
                dcnt[e] += 1
                slot = n % self.RING
                use = n // self.RING
                it['sig'] = (ring[e][slot], 16 * (use + 1), 16)
                if use > 0:
                    it['prewait'] = (ring[e][slot], 16 * use)
            else:
                cccnt += 1
                it['sig'] = (ccsem, cccnt, None)
        print('tracker: instrs', len(ins), 'sem counts', ecnt, dcnt, cccnt, flush=True)
        per = {e: [] for e in engs}
        for i, it in enumerate(ins):
            per[it['eng']].append(i)
        outs_wait = [ins[i]['sig'] for i in range(len(ins)) if ins[i]['out']]
        eobj = {'pe': nc.tensor, 'act': nc.scalar, 'dve': nc.vector, 'pool': nc.gpsimd, 'sp': nc.sync}

        def run(e):
            def body(engine):
                seen = {}
                for i in per[e]:
                    it = ins[i]
                    waits = {}
                    for j in it['deps']:
                        sg = ins[j]['sig']
                        k = id(sg[0])
                        if k not in waits or waits[k][1] < sg[1]:
                            waits[k] = (sg[0], sg[1])
                    if it['prewait'] is not None:
                        sg = it['prewait']
                        k = id(sg[0])
                        if k not in waits or waits[k][1] < sg[1]:
                            waits[k] = sg
                    for k, (s, v) in waits.items():
                        if seen.get(k, 0) < v:
                            engine.wait_ge(s, v)
                            seen[k] = v
                    r = it['fn'](engine)
                    sg = it['sig']
                    if sg is not None:
                        if sg[2] is None:
                            r.then_inc(sg[0])
                        else:
                            r.then_inc(sg[0], sg[2])
                if e == 'sp':
                    for (s, v, _) in outs_wait:
                        if seen.get(id(s), 0) < v:
                            engine.wait_ge(s, v)
                            seen[id(s)] = v
            return body
        block.tensor(run('pe'))
        block.scalar(run('act'))
        block.vector(run('dve'))
        block.gpsimd(run('pool'))
        block.sync(run('sp'))


class Builder:
    def __init__(self, debug_stage=None):
        self.debug_stage = debug_stage
        self.nc = bass.Bass("TRN2", target_bir_lowering=False)
        self.P = Prog(self.nc)
        self.sp_top = 16512
        self.psum_rr = 0
        self.rr = {}
        self.wr_i = 0
        self.att_pend = None

    def sb(self, name, free, dt, at=None):
        es = ESZ[dt]
        if at is None:
            at = self.sp_top
        at = (at + 31) // 32 * 32
        t = self.nc.alloc_sbuf_tensor_at(name, [128, free], dt, offset=at)
        end = at + free * es
        assert end <= 229000, (name, end)
        self.P.sb[t[:, 0:1].name] = (at, es)
        if end > self.sp_top:
            self.sp_top = end
        self.peak = max(getattr(self, 'peak', 0), end)
        return t

    def din(self, name, shape):
        return self.nc.dram_tensor(name, list(shape), F32, kind="ExternalInput")

    def dout(self, name, shape):
        t = self.nc.dram_tensor(name, list(shape), F32, kind="ExternalOutput")
        self.P.dram_out.add(t.ap().name)
        return t

    def mm(self, out, lhsT, rhs, start, stop):
        self.P.op('pe', lambda e: e.matmul(out, lhsT, rhs, start=start, stop=stop), [lhsT, rhs], [out])

    def act(self, out, in_, func, bias=None, scale=None, accum=None, extra_r=()):
        kw = {}
        rd = [in_] + list(extra_r)
        if bias is not None:
            kw['bias'] = bias
            if not isinstance(bias, float):
                rd.append(bias)
        if scale is not None:
            kw['scale'] = scale
            if not isinstance(scale, float):
                rd.append(scale)
        wr = [out]
        if accum is not None:
            kw['accum_out'] = accum
            wr.append(accum)
        self.P.op('act', lambda e: e.activation(out, in_, func, **kw), rd, wr)

    def tt(self, out, in0, in1, op, eng='dve'):
        self.P.op(eng, lambda e: e.tensor_tensor(out, in0, in1, op), [in0, in1], [out])

    def ts(self, out, in0, s1, s2, op0, op1=None, eng='dve'):
        rd = [in0] + [s for s in (s1, s2) if s is not None and not isinstance(s, float)]
        if op1 is None:
            self.P.op(eng, lambda e: e.tensor_scalar(out, in0, s1, None, op0), rd, [out])
        else:
            self.P.op(eng, lambda e: e.tensor_scalar(out, in0, s1, s2, op0, op1), rd, [out])

    def stt(self, out, in0, scalar, in1, op0, op1, eng='dve'):
        rd = [in0, in1] + ([] if isinstance(scalar, float) else [scalar])
        self.P.op(eng, lambda e: e.scalar_tensor_tensor(out, in0, scalar, in1, op0, op1), rd, [out])

    def recip(self, out, in_):
        self.P.op('dve', lambda e: e.reciprocal(out, in_), [in_], [out])

    def copy(self, out, in_, eng='dve'):
        self.P.op(eng, lambda e: e.tensor_copy(out, in_), [in_], [out])

    def memset(self, ap, val, eng='pool'):
        self.P.op(eng, lambda e: e.memset(ap, val), [], [ap])

    def dma(self, out, in_, q='sp'):
        self.P.op(q, lambda e: e.dma_start(out=out, in_=in_, allow_slow_non_contiguous=True), [in_], [out], kind='d')

    def psum(self):
        b = self.pbanks[self.psum_rr % 6]
        self.psum_rr += 1
        return b

    def tmp(self, key):
        lst = self.tmps[key]
        i = self.rr.get(key, 0)
        self.rr[key] = i + 1
        return lst[i % len(lst)]

    def wload(self, src2d, kc, bw):
        slot = self.wring[self.wr_i % len(self.wring)]
        self.wr_i += 1
        n = kc * bw
        dst = slot[:, 0:n]
        self.dma(dst, src2d, q='pool')
        return dst.rearrange("p (k c) -> p k c", k=kc)

    def rstd_from_ss(self, ss_ps, n, nfeat, rows=slice(0, 128)):
        s = self.tmp('nrm')
        self.act(s[rows, 0:n], ss_ps, AF.Sqrt, bias=self.eps_col[rows, 0:1], scale=1.0 / nfeat)
        r = self.tmp('nrm')
        self.recip(r[rows, 0:n], s[rows, 0:n])
        return r

    def prenorm(self, xt, n, Acols, Bcols, h):
        ss = self.ssbank()
        for c in range(KC):
            sq = self.tmp('sq')
            self.act(sq[:, 0:n], xt[:, c, :], AF.Square)
            self.mm(ss[:, 0:n], self.ones_bf[:, :], sq[:, 0:n], c == 0, c == KC - 1)
        r = self.rstd_from_ss(ss[:, 0:n], n, D)
        for c in range(KC):
            t = self.tmp('f32')
            self.tt(t[:, 0:n], xt[:, c, :], r[:, 0:n], ALU.mult)
            self.act(h[:, c, :], t[:, 0:n], AF.Identity, bias=Bcols[:, c:c + 1], scale=Acols[:, c:c + 1])

    def ssbank(self):
        b = self.pbanks[6 + (self.rr.get('ss', 0) % 2)]
        self.rr['ss'] = self.rr.get('ss', 0) + 1
        return b

    def post_residual(self, fs, ss, xt, n, Ccols):
        r = self.rstd_from_ss(ss[:, 0:n], n, D)
        for m in range(KC):
            t = self.tmp('f32')
            self.tt(t[:, 0:n], fs[:, m, :], r[:, 0:n], ALU.mult)
            self.stt(xt[:, m, :], t[:, 0:n], Ccols[:, m:m + 1], xt[:, m, :], ALU.mult, ALU.add)

    def evac_f(self, ps, n, fs_m, ss, first, last):
        self.act(fs_m, ps, AF.Copy)
        sq = self.tmp('sq')
        self.act(sq[:, 0:n], ps, AF.Square)
        return lambda: self.mm(ss[:, 0:n], self.ones_bf[:, :], sq[:, 0:n], first, last)

    def ffn(self, l, f, tiles, Acols, Bcols, Ccols):
        G = len(tiles)
        for g, (xt, n) in enumerate(tiles):
            self.prenorm(xt, n, Acols, Bcols, self.f_h[g][:, :, 0:n])
        for j in range(FC):
            wb = self.wload(self.w_gu[l, f, j], KC, 256)
            for g, (xt, n) in enumerate(tiles):
                h = self.f_h[g]
                pg = self.psum()
                pu = self.psum()
                for k in range(KC):
                    self.mm(pg[:, 0:n], wb[:, k, 0:128], h[:, k, 0:n], k == 0, k == KC - 1)
                for k in range(KC):
                    self.mm(pu[:, 0:n], wb[:, k, 128:256], h[:, k, 0:n], k == 0, k == KC - 1)
                sg = self.tmp('f32')
                self.act(sg[:, 0:n], pg[:, 0:n], AF.Silu)
                self.tt(self.f_hid[g][:, j, 0:n], sg[:, 0:n], pu[:, 0:n], ALU.mult)
        sss = [self.ssbank() for _ in tiles]
        pend = None
        for m in range(KC):
            wb = self.wload(self.w_dn[l, f, m], FC, 128)
            for g, (xt, n) in enumerate(tiles):
                ps = self.psum()
                for j in range(FC):
                    self.mm(ps[:, 0:n], wb[:, j, :], self.f_hid[g][:, j, 0:n], j == 0, j == FC - 1)
                if pend is not None:
                    pend()
                pend = self.evac_f(ps[:, 0:n], n, self.f_fs[g][:, m, 0:n], sss[g], m == 0, m == KC - 1)
        pend()
        for g, (xt, n) in enumerate(tiles):
            self.post_residual(self.f_fs[g][:, :, 0:n], sss[g], xt, n, Ccols)

    def fm_norm(self, src, nch, n, gcols, out, out_dt_bf=True):
        ss = self.ssbank()
        for c in range(nch):
            sq = self.tmp('sq')
            self.act(sq[:, 0:n], src[:, c, :], AF.Square)
            self.mm(ss[:, 0:n], self.ones_bf[:, :], sq[:, 0:n], c == 0, c == nch - 1)
        r = self.rstd_from_ss(ss[:, 0:n], n, nch * 128)
        for c in range(nch):
            t = self.tmp('f32')
            self.tt(t[:, 0:n], src[:, c, :], r[:, 0:n], ALU.mult)
            self.act(out[:, c, :], t[:, 0:n], AF.Copy, scale=gcols[:, c:c + 1])

    def mixer_p1(self, l, S, group):
        vl = self.vecs
        vb = l * V_N
        for g, ti in enumerate(group):
            xt = S['x'][:, :, ti * NT:(ti + 1) * NT]
            self.prenorm(xt, NT, S['A'][l][1], S['B'][l][1], self.hbuf[g][:, :, 0:NT])
        nblk = 9 if S['rope'] else 8
        import os
        only = os.environ.get('P1B')
        for b in range(nblk):
            if only is not None and str(b) not in only.split(','):
                continue
            wb = self.wload(self.w_in[l, b], KC, 256)
            for g, ti in enumerate(group):
                h = self.hbuf[g]
                t0 = ti * NT
                if b == 2:
                    self.sgu(l, S, g, ti, wb)
                    continue
                pss = []
                if b in (4, 8):
                    ps = self.psum()
                    if b == 4:
                        for k in range(KC):
                            self.mm(ps[:, 0:NT], wb[:, k, 0:128], h[:, k, 0:NT], k == 0, k == KC - 1)
                    else:
                        for k in range(KC):
                            self.mm(ps[0:96, 0:NT], wb[:, k, 0:96], h[:, k, 0:NT], k == 0, k == KC - 1)
                    pss.append(ps)
                else:
                    for c in range(2):
                        ps = self.psum()
                        for k in range(KC):
                            self.mm(ps[:, 0:NT], wb[:, k, c * 128:(c + 1) * 128], h[:, k, 0:NT], k == 0, k == KC - 1)
                        pss.append(ps)
                if b == 0:
                    for c in range(2):
                        self.act(self.zc_sb[g][:, c, :], pss[c][:, 0:NT], AF.Copy)
                    for (po, to, ln) in S['pieces'](ti):
                        self.dma(S['zp_d'][:, :, po + 8:po + 8 + ln], self.zc_sb[g][:, :, to:to + ln])
                elif b == 1:
                    for c in range(2):
                        self.act(self.u_sb[g][:, c, :], pss[c][:, 0:NT], AF.Copy)
                elif b == 3:
                    for c in range(2):
                        self.act(self.cq_sb[:, c, :], pss[c][:, 0:NT], AF.Copy)
                    self.fm_norm(self.cq_sb, 2, NT, vl[:, vb + V_GQ:vb + V_GQ + 2], S['cqn'][:, :, t0:t0 + NT])
                elif b == 4:
                    self.act(self.cq_sb[:, 0, :], pss[0][:, 0:NT], AF.Copy)
                    self.fm_norm(self.cq_sb[:, 0:1, :], 1, NT, vl[:, vb + V_GKV:vb + V_GKV + 1], self.ckvn_f3)
                    ps2 = self.psum()
                    for k in range(KC):
                        self.mm(ps2[0:96, 0:NT], wb[:, k, 128:224], h[:, k, 0:NT], k == 0, k == KC - 1)
                    self.act(self.kr_f[g][64:96, :], ps2[64:96, 0:NT], AF.Copy)
                    if S['rope']:
                        self.dma(self.xin[0:128, t0:t0 + NT], self.ckvn_f[:, :])
                    else:
                        self.dma(self.st_ckv[l, :, t0:t0 + NT], self.ckvn_f[:, :])
                        self.dma(self.st_kr[l, :, t0:t0 + NT], self.kr_f[g][64:96, :])
                        self.copy(S['ckvT'][:, t0:t0 + NT], self.ckvn_f[:, :])
                        self.copy(S['krT'][64:96, t0:t0 + NT], self.kr_f[g][64:96, :])
                elif b == 5:
                    for c in range(2):
                        self.act(self.zb_t[:, c, :], pss[c][:, 0:NT], AF.Copy)
                    self.dma(S['zb_d'][:, :, t0:t0 + NT], self.zb_t)
                elif b == 6:
                    for c in range(2):
                        self.act(self.zc_sb[g][:, c, :], pss[c][:, 0:NT], AF.Copy)
                elif b == 7:
                    for c in range(2):
                        self.tt(self.u_sb[g][:, c, :], self.zc_sb[g][:, c, :], pss[c][:, 0:NT], ALU.mult)
                    for (po, to, ln) in S['pieces'](ti):
                        p1 = S['cpos'](po)
                        self.dma(S['pc_d'][:, :, p1 + 1:p1 + 1 + ln], self.u_sb[g][:, :, to:to + ln])
                elif b == 8:
                    r = slice(64, 96)
                    t1 = self.tmp('f32')
                    t2 = self.tmp('f32')
                    self.tt(t1[r, 0:NT], self.kr_f[g][r, :], self.cosT[r, t0:t0 + NT], ALU.mult)
                    self.tt(t2[r, 0:NT], pss[0][r, 0:NT], self.sinT[r, t0:t0 + NT], ALU.mult)
                    self.tt(t1[r, 0:NT], t1[r, 0:NT], t2[r, 0:NT], ALU.add)
                    self.dma(self.xin[128:160, t0:t0 + NT], t1[r, 0:NT])

    def sgu(self, l, S, g, ti, wb):
        h = self.hbuf[g]
        t0 = ti * NT
        vn = self.vn_sb
        import os
        lvl = int(os.environ.get('SGU', '9'))
        for half in range(2):
            pv = self.psum()
            for bb in range(2):
                blk = half * 2 + bb
                for k in range(KC):
                    self.mm(pv[:, bb * 256:(bb + 1) * 256], h[:, k, blk * 128:(blk + 1) * 128], wb[:, k, 0:256],
                            k == 0, k == KC - 1)
            for bb in range(2):
                if lvl < 2:
                    break
                blk = half * 2 + bb
                junk = self.tmp('f32')
                self.act(junk[:, 0:256], pv[:, bb * 256:(bb + 1) * 256], AF.Square)
                jj, so = junk[:, 0:256], self.ssq[:, blk:blk + 1]
                self.P.op('dve', lambda e, jj=jj, so=so: e.reduce_sum(so, jj, mybir.AxisListType.X), [jj], [so])
                if lvl < 3:
                    continue
                self.act(self.ssq[:, 4 + blk:5 + blk], self.ssq[:, blk:blk + 1], AF.Sqrt, bias=self.eps_col[:, 0:1], scale=1.0 / 256)
                self.recip(self.ssq[:, 8 + blk:9 + blk], self.ssq[:, 4 + blk:5 + blk])
                self.stt(vn[:, blk, :], pv[:, bb * 256:(bb + 1) * 256], self.ssq[:, 8 + blk:9 + blk], self.gsgu[:, :],
                         ALU.mult, ALU.mult)
        for c in range(2):
            if lvl < 4:
                break
            pa = self.psum()
            pb = self.psum()
            for blk in range(4):
                self.mm(pa[:, blk * 128:(blk + 1) * 128], vn[:, blk, c * 128:(c + 1) * 128], self.wsT[:, 2 * c, :], True, True)
                self.mm(pb[:, blk * 128:(blk + 1) * 128], vn[:, blk, c * 128:(c + 1) * 128], self.wsT[:, 2 * c + 1, :], True, True)
            for (rows, pp) in ((slice(0, 64), pa), (slice(64, 128), pb)):
                if lvl < 5:
                    break
                t = self.tmp('f32')
                for blk in range(4):
                    self.tt(t[rows, blk * 128:(blk + 1) * 128], pp[rows, blk * 128:(blk + 1) * 128], self.bsT[rows, c, :], ALU.add)
                self.tt(self.bsg_t[rows, c, :], t[rows, 0:NT], self.u_sb[g][rows, c, :], ALU.mult)
        if lvl >= 6:
            self.dma(S['bsg_d'][:, :, t0:t0 + NT], self.bsg_t)

    def pool_conv(self, l, S, ti):
        vb = l * V_N
        vl = self.vecs
        t0 = ti * NT
        self.dma(self.zb_t, S['zb_d'][:, :, t0:t0 + NT])
        self.dma(self.bsg_t, S['bsg_d'][:, :, t0:t0 + NT])
        for (po, to, ln) in S['pieces'](ti):
            zw = self.zpw
            pw_ = self.pcw
            p1 = S['cpos'](po)
            self.dma(zw[:, :, 0:ln + 16], S['zp_d'][:, :, po:po + ln + 16])
            self.dma(pw_[:, :, 0:ln + 2], S['pc_d'][:, :, p1:p1 + ln + 2])
            first, last = S['edges'](ti, to)
            for c in range(2):
                s2 = self.tmp('pw')
                s4 = self.tmp('pw')
                lo = slice(0, 64)
                hi = slice(64, 128)
                self.tt(s2[:, 1:ln + 16], zw[:, c, 0:ln + 15], zw[:, c, 1:ln + 16], ALU.add)
                if c == 0:
                    self.tt(s4[hi, 2:ln + 14], s2[hi, 1:ln + 13], s2[hi, 3:ln + 15], ALU.add)
                    srcs = ((lo, s2), (hi, s4))
                else:
                    self.tt(s4[:, 2:ln + 14], s2[:, 1:ln + 13], s2[:, 3:ln + 15], ALU.add)
                    s8 = self.tmp('pw')
                    self.tt(s8[:, 4:ln + 12], s4[:, 2:ln + 10], s4[:, 6:ln + 14], ALU.add)
                    s16 = self.tmp('pw')
                    self.tt(s16[hi, 8:ln + 8], s8[hi, 4:ln + 4], s8[hi, 12:ln + 12], ALU.add)
                    srcs = ((lo, s8), (hi, s16))
                dd = self.tmp('bfn')
                iw = vl[:, G_INVW + c:G_INVW + c + 1]
                for (rows, sw) in srcs:
                    self.stt(dd[rows, 0:ln], sw[rows, 8:8 + ln], iw[rows, :], zw[rows, c, 8:8 + ln], ALU.mult, ALU.subtract)
                    for (flag, d0, e0) in ((first, 0, 0), (last, ln - 8, 8)):
                        if flag:
                            t = self.tmp('f32')
                            self.tt(t[rows, 0:8], sw[rows, 8 + d0:16 + d0], S['icntE'][rows, c, e0:e0 + 8], ALU.mult)
                            self.tt(dd[rows, d0:d0 + 8], t[rows, 0:8], zw[rows, c, 8 + d0:16 + d0], ALU.subtract)
                ps = self.psum()
                self.mm(ps[:, 0:ln], self.wpool[:, c, :], dd[:, 0:ln], True, True)
                self.act(self.a_t[:, c, to:to + ln], ps[:, 0:ln], AF.Copy, scale=vl[:, vb + V_PSCALE + c:vb + V_PSCALE + c + 1])
                y = self.tmp('f32')
                cw = vb + V_CONVW
                self.ts(y[:, 0:ln], pw_[:, c, 1:1 + ln], vl[:, cw + 2 + c:cw + 3 + c], None, ALU.mult)
                self.stt(y[:, 0:ln], pw_[:, c, 0:ln], vl[:, cw + c:cw + c + 1], y[:, 0:ln], ALU.mult, ALU.add)
                self.stt(y[:, 0:ln], pw_[:, c, 2:2 + ln], vl[:, cw + 4 + c:cw + 5 + c], y[:, 0:ln], ALU.mult, ALU.add)
                self.tt(self.dc_t[:, c, to:to + ln], y[:, 0:ln], self.zb_t[:, c, to:to + ln], ALU.mult)

    def attention(self, l, S, cT, nk, qsets, kr_fill):
        nkc = nk // 128
        for kb_ in self.kbuf:
            kr_fill(kb_)

        def build_kv(h):
            kb = self.kbuf[h % 2]
            vbuf = self.vbuf[h % 2]
            voff = 0 if h % 2 == 0 else 64
            for k0 in range(0, nk, NT):
                n = min(NT, nk - k0)
                ps = self.psum()
                self.mm(ps[0:64, 0:n], self.wukv[:, h * 128:h * 128 + 64], cT[:, k0:k0 + n], True, True)
                self.copy(kb[0:64, k0:k0 + n], ps[0:64, 0:n])
            for c0 in range(0, nkc, 8):
                ncb = min(8, nkc - c0)
                ps = self.psum()
                for cc in range(ncb):
                    kc = c0 + cc
                    self.mm(ps[:, cc * 64:(cc + 1) * 64], cT[:, kc * 128:(kc + 1) * 128],
                            self.wukv[:, h * 128 + 64:h * 128 + 128], True, True)
                self.copy(vbuf[:, c0:c0 + ncb, voff:voff + 64], ps[:, 0:ncb * 64].rearrange("p (c d) -> p c d", d=64))

        build_kv(0)
        for h in range(8):
            kb = self.kbuf[h % 2]
            vbuf = self.vbuf[h % 2]
            def q_prep(q0, n):
                qh = self.tmp('qh')
                pq = self.psum()
                for k in range(2):
                    self.mm(pq[0:96, 0:n], self.wuq[:, k, h * 96:(h + 1) * 96], S['cqn'][:, k, q0:q0 + n], k == 0, k == 1)
                self.copy(qh[0:64, 0:n], pq[0:64, 0:n])
                r = slice(64, 96)
                if S['rope']:
                    pq2 = self.psum()
                    for k in range(2):
                        self.mm(pq2[0:96, 0:n], self.wuqs[:, k, h * 96:(h + 1) * 96], S['cqn'][:, k, q0:q0 + n], k == 0, k == 1)
                    t1 = self.tmp('f32')
                    t2 = self.tmp('f32')
                    self.tt(t1[r, 0:n], pq[r, 0:n], self.cosT[r, q0:q0 + n], ALU.mult)
                    self.tt(t2[r, 0:n], pq2[r, 0:n], self.sinT[r, q0:q0 + n], ALU.mult)
                    self.tt(qh[r, 0:n], t1[r, 0:n], t2[r, 0:n], ALU.add)
                else:
                    self.copy(qh[r, 0:n], pq[r, 0:n])
                return qh

            qh_next = q_prep(*qsets[0])
            for qi, (q0, n) in enumerate(qsets):
                qh = qh_next
                po = self.ssbank()
                pss = {}

                def emit_st(kc):
                    ps = self.psum()
                    self.mm(ps[:, 0:n], kb[0:96, kc * 128:(kc + 1) * 128], qh[0:96, 0:n], True, True)
                    pss[kc] = ps
                LA = 2
                for kc in range(min(LA, nkc)):
                    emit_st(kc)
                if qi + 1 < len(qsets):
                    qh_next = q_prep(*qsets[qi + 1])
                if qi == 0 and h + 1 < 8:
                    build_kv(h + 1)
                for kc in range(nkc):
                    if kc + LA < nkc:
                        emit_st(kc + LA)
                    if kc == min(10, nkc - 1) and self.att_pend is not None:
                        self.att_pend()
                        self.att_pend = None
                    pt = self.tmp('pt')
                    self.act(pt[:, 0:n], pss.pop(kc)[:, 0:n], AF.Exp, scale=SCALE)
                    self.mm(po[:, 0:n], vbuf[:, kc, :], pt[:, 0:n], kc == 0, kc == nkc - 1)
                dp = 64 if h % 2 == 0 else 0
                rows = slice(0, 64) if h % 2 == 0 else slice(64, 128)
                rs = self.tmp('rs')
                self.recip(rs[dp:dp + 1, 0:n], po[dp:dp + 1, 0:n])

                def fin(po=po, rs=rs, dp=dp, rows=rows, n=n, q0=q0, hh=h):
                    pb = self.psum()
                    self.mm(pb[:, 0:n], self.ones_f[dp:dp + 1, :], rs[dp:dp + 1, 0:n], True, True)
                    bs = self.tmp('f32')
                    self.copy(bs[rows, 0:n], pb[rows, 0:n])
                    self.tt(S['att'][rows, hh // 2, q0:q0 + n], po[rows, 0:n], bs[rows, 0:n], ALU.mult)
                self.att_pend = fin

    def att_flush(self):
        if self.att_pend is not None:
            self.att_pend()
            self.att_pend = None

    def merge(self, l, S, ti):
        vb = l * V_N
        vl = self.vecs
        g = 0
        t0 = ti * NT
        xt = S['x'][:, :, t0:t0 + NT]
        self.prenorm(xt, NT, S['A'][l][1], S['B'][l][1], self.hbuf[g][:, :, 0:NT])
        self.pool_conv(l, S, ti)
        brs = ((self.a_t, 0, 0, 2), (self.bsg_t, 0, 2, 2), (S['att'], t0, 4, 4), (self.dc_t, 0, 8, 2))
        h = self.hbuf[g]
        for m in range(KC):
            accf = self.tmp('acc')
            for half in range(2):
                if half == 0:
                    wbr = self.wload(self.w_br[l, m][:, 0:512], 4, 128)
                    kb0 = 0
                else:
                    wbr = self.wload(self.w_br[l, m][:, 512:1280], 6, 128)
                    kb0 = 4
                wg = self.wload(self.w_gate[l, m, half], KC, 256)
                for bj in range(2):
                    bi = half * 2 + bj
                    (src, off, k0, nk) = brs[bi]
                    pb = self.psum()
                    for k in range(nk):
                        self.mm(pb[:, 0:NT], wbr[:, k0 - kb0 + k, :], src[:, k, off:off + NT], k == 0, k == nk - 1)
                    pg = self.psum()
                    for k in range(KC):
                        self.mm(pg[:, 0:NT], wg[:, k, bj * 128:(bj + 1) * 128], h[:, k, 0:NT], k == 0, k == KC - 1)
                    gt = self.tmp('f32')
                    bc = vb + V_BGATE + bi * 8 + m
                    self.act(gt[:, 0:NT], pg[:, 0:NT], AF.Sigmoid, bias=vl[:, bc:bc + 1])
                    if bi == 0:
                        self.tt(accf[:, 0:NT], gt[:, 0:NT], pb[:, 0:NT], ALU.mult)
                    else:
                        t = self.tmp('f32')
                        self.tt(t[:, 0:NT], gt[:, 0:NT], pb[:, 0:NT], ALU.mult)
                        if bi < 3:
                            self.tt(accf[:, 0:NT], accf[:, 0:NT], t[:, 0:NT], ALU.add)
                        else:
                            self.tt(self.mrg[:, m, :], accf[:, 0:NT], t[:, 0:NT], ALU.add)
        ss = self.ssbank()
        pend = None
        for mo in range(KC):
            wo = self.wload(self.w_o[l, mo], KC, 128)
            ps = self.psum()
            for m in range(KC):
                self.mm(ps[:, 0:NT], wo[:, m, :], self.mrg[:, m, :], m == 0, m == KC - 1)
            if pend is not None:
                pend()
            pend = self.evac_f(ps[:, 0:NT], NT, self.fs[g][:, mo, 0:NT], ss, mo == 0, mo == KC - 1)
        pend()
        self.post_residual(self.fs[g][:, :, 0:NT], ss, xt, NT, S['C'][l][1])

    def v3(self, t, c):
        return t.ap().rearrange("p (c n) -> p c n", c=c)

    def build(self):
        nc = self.nc
        from contextlib import ExitStack
        self.xT_P = self.din("xT_P", [128, KC * NPT])
        self.xT_S = self.din("xT_S", [128, KC * NST])
        self.condT = self.din("condT", [128, KC * 2])
        self.cache_c = self.din("cache_c", [L, 128, 512])
        self.cache_k = self.din("cache_k", [L, 32, 512])
        self.vecs_d = self.din("vecs", [128, G_N])
        self.gsgu_d = self.din("gsgu", [L, 128, 256])
        self.bsT_d = self.din("bsT", [L, 128, 2 * 128])
        self.wsT_d = self.din("wsT", [L, 128, 4 * 128])
        self.wpool_d = self.din("wpool", [L, 128, 2 * 128])
        self.wuq_d = self.din("wuq", [L, 128, 2 * 768])
        self.wuqs_d = self.din("wuqs", [L, 128, 2 * 768])
        self.wukv_d = self.din("wukv", [L, 128, 1024])
        self.rpos_d = self.din("rpos", [128, 2 * NST])
        self.icntP_d = self.din("icntP", [128, 2 * 16])
        self.icntS_d = self.din("icntS", [128, 2 * 16])
        self.w_mod = self.din("w_mod", [L, 36, 128, KC * 128])
        self.w_gu = self.din("w_gu", [L, 2, FC, 128, KC * 256])
        self.w_dn = self.din("w_dn", [L, 2, KC, 128, FC * 128])
        self.w_in = self.din("w_in", [L, 9, 128, KC * 256])
        self.w_gate = self.din("w_gate", [L, KC, 2, 128, KC * 256])
        self.w_br = self.din("w_br", [L, KC, 128, 10 * 128])
        self.w_o = self.din("w_o", [L, KC, 128, KC * 128])
        self.yT_P = self.dout("yT_P", [128, KC * NPT])
        self.yT_S = self.dout("yT_S", [128, KC * NST])
        self.st_ckv = self.dout("st_ckv", [L, 128, NPT])
        self.st_kr = self.dout("st_kr", [L, 32, NPT])
        self.xin = nc.dram_tensor("xin", [XR, NST], F32)
        self.mg_in = nc.dram_tensor("mg_in", [128, 144], F32)
        self.mg_out = nc.dram_tensor("mg_out", [2 * 128, 144], F32)
        self.xout = nc.dram_tensor("xout", [2 * XR, NST], F32)
        dbg = self.debug_stage

        with ExitStack() as es:
            P = self.P
            self.pbanks = []
            for i in range(8):
                t = es.enter_context(nc.psum_tensor("pb%d" % i, [128, 512], F32))
                P.ps.add(t[:, 0:1].name)
                self.pbanks.append(t)
            self.vecs = self.sb("vecs", G_N, F32)
            self.ones_bf = self.sb("ones_bf", 128, BF16)
            self.ones_f = self.sb("ones_f", 128, F32)
            self.zeros_f = self.sb("zeros_f", 16, F32)
            self.eps_col = self.sb("eps", 1, F32)
            self.mods = self.sb("mods", L * 2 * 72, F32)
            self.drv = self.sb("drv", L * 2 * 3 * 16, F32)
            self.cs_bf = self.sb("cs_bf", 16, BF16)
            self.cs_f = self.sb("cs_f", 16, F32)
            self.mg_sb = self.sb("mg_sb", 144, F32)
            self.mg_all = self.sb("mg_all", 288, F32)
            self.ssq = self.sb("ssq", 16, F32)
            self.gsgu = self.sb("gsgu", 256, F32)
            self.bsT = self.v3(self.sb("bsT", 256, F32), 2)
            self.wsT = self.v3(self.sb("wsT", 512, BF16), 4)
            self.wpool = self.v3(self.sb("wpool", 256, BF16), 2)
            self.wuq = self.v3(self.sb("wuq", 1536, BF16), 2)
            self.wuqs = self.v3(self.sb("wuqs", 1536, BF16), 2)
            self.wukv = self.sb("wukv", 1024, BF16)
            self.wring = [self.sb("wring%d" % i, 2816, BF16) for i in range(4)]
            self.tmps = {
                'f32': [self.sb("tf%d" % i, NT, F32) for i in range(5)],
                'nrm': [self.sb("tn%d" % i, NT, F32) for i in range(2)],
                'sq': [self.sb("tq%d" % i, NT, BF16) for i in range(2)],
            }
            base = self.sp_top
            print("persistent sbuf bytes", base - 16512, flush=True)
            self.dma(self.vecs[:, :], self.vecs_d[:, :])
            self.memset(self.ones_bf[:, :], 1.0)
            self.memset(self.ones_f[:, :], 1.0)
            self.memset(self.zeros_f[:, :], 0.0)
            self.memset(self.eps_col[:, :], EPS)
            self.dma(self.cs_f[:, :], self.condT[:, :])
            self.act(self.cs_bf[:, :], self.cs_f[:, :], AF.Silu)
            csv = self.cs_bf.ap().rearrange("p (k c) -> p k c", c=2)
            pm = self.pbanks[0]
            for l in range(L):
                for blk in range(36):
                    wb = self.wload(self.w_mod[l, blk], KC, 128)
                    o0 = (l * 36 + blk) * 2
                    for k in range(KC):
                        self.mm(pm[:, o0:o0 + 2], wb[:, k, :], csv[:, k, :], k == 0, k == KC - 1)
            self.act(self.mg_sb[:, :], pm[:, 0:144], AF.Copy)
            self.dma(self.mg_in[:, :], self.mg_sb[:, :])
            mi = self.mg_in.ap().opt()
            mo_ = self.mg_out.ap().opt()
            self.P.op('pool', lambda e: e.collective_compute("AllGather", ALU.bypass, replica_groups=[[0, 1], [2, 3], [4, 5], [6, 7]],
                                                            ins=[mi], outs=[mo_]), [self.mg_in.ap()], [self.mg_out.ap()], kind='cc')
            self.dma(self.mg_all.ap().rearrange("p (r c) -> p r c", r=2), self.mg_out.ap().rearrange("(r p) c -> p r c", r=2))
            gv = self.mg_all.ap().rearrange("p (r l b j) -> p r l b j", r=2, l=L, b=36)
            for l in range(L):
                bm = self.vecs[:, l * V_N + V_BMOD:l * V_N + V_BMOD + 72].rearrange("p (r b) -> p r b", r=2)
                for c in range(2):
                    mc = self.mods[:, (l * 2 + c) * 72:(l * 2 + c) * 72 + 72].rearrange("p (r b) -> p r b", r=2)
                    self.tt(mc, gv[:, :, l, :, c], bm, ALU.add)
            streams = {}
            for c, nm in ((0, 'P'), (1, 'S')):
                A = [[None] * 3 for _ in range(L)]
                Bc = [[None] * 3 for _ in range(L)]
                Cc = [[None] * 3 for _ in range(L)]
                for l in range(L):
                    mo = (l * 2 + c) * 72
                    for s in range(3):
                        do = ((l * 2 + c) * 3 + s) * 16
                        a = self.drv[:, do:do + 8]
                        cc = self.drv[:, do + 8:do + 16]
                        gp = self.vecs[:, l * V_N + V_GPRE + s * 8:l * V_N + V_GPRE + s * 8 + 8]
                        go = self.vecs[:, l * V_N + V_GPOST + s * 8:l * V_N + V_GPOST + s * 8 + 8]
                        self.stt(a, self.mods[:, mo + (3 * s + 1) * 8:mo + (3 * s + 1) * 8 + 8], 1.0, gp, ALU.add, ALU.mult)
                        self.stt(cc, self.mods[:, mo + (3 * s + 2) * 8:mo + (3 * s + 2) * 8 + 8], 0.5 if s != 1 else 1.0, go, ALU.mult, ALU.mult)
                        A[l][s] = a
                        Bc[l][s] = self.mods[:, mo + 3 * s * 8:mo + 3 * s * 8 + 8]
                        Cc[l][s] = cc
                streams[nm] = dict(A=A, B=Bc, C=Cc)

            for nm in ('S', 'P'):
                S = streams[nm]
                rope = nm == 'S'
                ntok = NST if rope else NPT
                ntile = ntok // NT
                TG = 1 if rope else 2
                self.sp_top = base
                S['rope'] = rope
                xs = self.sb("x_" + nm, KC * ntok, F32)
                S['x'] = xs.ap().rearrange("p (k n) -> p k n", k=KC)
                srcv = (self.xT_S if rope else self.xT_P).ap().rearrange("p (k n) -> p k n", k=KC)
                for k in range(KC):
                    self.dma(S['x'][:, k, :], srcv[:, k, :])
                if rope:
                    lp = NST + 16
                    lc = NST + 2
                    S['pieces'] = lambda ti: [(ti * NT, 0, NT)]
                    S['cpos'] = lambda po: po
                    S['edges'] = lambda ti, to: (ti == 0, ti == 3)
                else:
                    lp = 4 * 272
                    lc = 4 * 258
                    S['pieces'] = lambda ti: [((2 * ti) * 272, 0, 256), ((2 * ti + 1) * 272, 256, 256)]
                    S['cpos'] = lambda po: (po // 272) * 258
                    S['edges'] = lambda ti, to: (True, True)
                S['zp_d'] = nc.dram_tensor("zp_d" + nm, [128, 2, lp], F32).ap()
                S['pc_d'] = nc.dram_tensor("pc_d" + nm, [128, 2, lc], F32).ap()
                S['zb_d'] = nc.dram_tensor("zb_d" + nm, [128, 2, ntok], BF16).ap()
                S['bsg_d'] = nc.dram_tensor("bsg_d" + nm, [128, 2, ntok], BF16).ap()
                S['icntE'] = self.v3(self.sb("icntE_" + nm, 32, F32), 2)
                self.dma(S['icntE'], (self.icntS_d if rope else self.icntP_d).ap().rearrange("p (c n) -> p c n", c=2))
                if rope:
                    self.cosT = self.sb("cosT", NST, BF16)
                    self.sinT = self.sb("sinT", NST, BF16)
                    self.halo = self.sb("halo", 36, F32)
                    self.halo2 = self.sb("halo2", 36, F32)
                ffn_base = self.sp_top
                TGF = 2
                self.f_fs, self.f_h, self.f_hid = [], [], []
                for g in range(TGF):
                    a0 = self.sp_top
                    self.f_fs.append(self.v3(self.sb("ffs%d_%s" % (g, nm), KC * NT, F32), KC))
                    self.f_h.append(self.v3(self.sb("fh%d_%s" % (g, nm), KC * NT, BF16, at=a0), KC))
                    self.f_hid.append(self.v3(self.sb("fhid%d_%s" % (g, nm), FC * NT, BF16), FC))
                top_ffn = self.sp_top
                self.sp_top = ffn_base
                S['cqn'] = self.v3(self.sb("cqn_" + nm, 2 * ntok, BF16), 2)
                S['att'] = self.v3(self.sb("att_" + nm, 4 * ntok, BF16), 4)
                if not rope:
                    S['ckvT'] = self.sb("ckvT_P", NPT, BF16)
                    S['krT'] = self.sb("krT_P", NPT, BF16)
                top_stream = self.sp_top
                self.hbuf, self.fs = [], []
                for g in range(TG):
                    a0 = self.sp_top
                    self.fs.append(self.v3(self.sb("fs%d_%s" % (g, nm), KC * NT, F32), KC))
                    self.hbuf.append(self.v3(self.sb("h%d_%s" % (g, nm), KC * NT, BF16, at=a0), KC))
                top_common = self.sp_top
                self.zb_t = self.v3(self.sb("zbt_" + nm, 2 * NT, BF16), 2)
                self.bsg_t = self.v3(self.sb("bsgt_" + nm, 2 * NT, BF16), 2)
                top_m0 = self.sp_top
                self.u_sb = [self.v3(self.sb("u%d_%s" % (g, nm), 2 * NT, F32), 2) for g in range(TG)]
                self.zc_sb = [self.v3(self.sb("zc%d_%s" % (g, nm), 2 * NT, F32), 2) for g in range(TG)]
                self.kr_f = [self.sb("krf%d_%s" % (g, nm), NT, F32) for g in range(TG)]
                self.cq_sb = self.v3(self.sb("cq_" + nm, 2 * NT, F32), 2)
                self.ckvn_f = self.sb("ckvn_" + nm, NT, F32)
                self.ckvn_f3 = self.ckvn_f.ap().rearrange("p (o n) -> p o n", o=1)
                self.vn_sb = self.v3(self.sb("vn_" + nm, 4 * 256, BF16), 4)
                top_p1 = self.sp_top
                self.sp_top = top_stream
                nk = NKS if rope else 256
                nkb = 2
                self.kbuf = [self.sb("kb%d_%s" % (i, nm), nk, BF16) for i in range(nkb)]
                self.vbuf = [self.sb("vb%d_%s" % (i, nm), (nk // 128) * 128, BF16).ap().rearrange("p (c d) -> p c d", d=128) for i in range(2)]
                if rope:
                    self.c_all = self.sb("c_all", NKS, BF16)
                self.tmps['pt'] = [self.sb("tp%d_%s" % (i, nm), NT, BF16) for i in range(3)]
                self.tmps['qh'] = [self.sb("tqh%d_%s" % (i, nm), NT, BF16) for i in range(2)]
                self.tmps['rs'] = [self.sb("trs%d_%s" % (i, nm), NT, F32) for i in range(2)]
                top_att = self.sp_top
                self.sp_top = top_m0
                self.zpw = self.v3(self.sb("zpw_" + nm, 2 * (NT + 16), F32), 2)
                self.pcw = self.v3(self.sb("pcw_" + nm, 2 * (NT + 2), F32), 2)
                self.a_t = self.v3(self.sb("at_" + nm, 2 * NT, BF16), 2)
                self.dc_t = self.v3(self.sb("dct_" + nm, 2 * NT, BF16), 2)
                self.mrg = self.v3(self.sb("mrg_" + nm, KC * NT, BF16), KC)
                self.tmps['pw'] = [self.sb("pw%d_%s" % (i, nm), NT + 16, F32) for i in range(4)]
                self.tmps['acc'] = [self.sb("ta%d_%s" % (i, nm), NT, F32) for i in range(1)]
                self.tmps['bfn'] = [self.sb("tb%d_%s" % (i, nm), NT, BF16) for i in range(1)]
                top_mg = self.sp_top
                print("stream", nm, "tops: common", top_common, "ffn", top_ffn, "p1", top_p1, "att", top_att, "merge", top_mg, flush=True)

                for c in range(2):
                    if rope:
                        pass
                    else:
                        for sq in range(4):
                            self.dma(S['zp_d'][:, c, sq * 272:sq * 272 + 8], self.zeros_f[:, 0:8])
                            self.dma(S['zp_d'][:, c, sq * 272 + 264:sq * 272 + 272], self.zeros_f[:, 0:8])
                            self.dma(S['pc_d'][:, c, sq * 258:sq * 258 + 1], self.zeros_f[:, 0:1])
                            self.dma(S['pc_d'][:, c, sq * 258 + 257:sq * 258 + 258], self.zeros_f[:, 0:1])
                if rope:
                    self.rope_tables()
                    self.dma(self.xin[162:163, 512:2048].rearrange("r (q e) -> (r q) e", e=16), self.zeros_f[0:96, 0:16])

                groups = [list(range(g0, min(g0 + TG, ntile))) for g0 in range(0, ntile, TG)]
                fgroups = [list(range(g0, min(g0 + TGF, ntile))) for g0 in range(0, ntile, TGF)]
                stop = False
                for l in range(L):
                    self.dma(self.gsgu[:, :], self.gsgu_d[l])
                    self.dma(self.bsT, self.bsT_d[l].rearrange("p (c n) -> p c n", c=2))
                    self.dma(self.wsT, self.wsT_d[l].rearrange("p (g n) -> p g n", g=4), q='pool')
                    self.dma(self.wpool, self.wpool_d[l].rearrange("p (c n) -> p c n", c=2), q='pool')
                    self.dma(self.wuq, self.wuq_d[l].rearrange("p (k n) -> p k n", k=2), q='pool')
                    self.dma(self.wuqs, self.wuqs_d[l].rearrange("p (k n) -> p k n", k=2), q='pool')
                    self.dma(self.wukv[:, :], self.wukv_d[l], q='pool')
                    for grp in fgroups:
                        self.ffn(l, 0, [(S['x'][:, :, ti * NT:(ti + 1) * NT], NT) for ti in grp], S['A'][l][0], S['B'][l][0], S['C'][l][0])
                    if dbg == 'ffn0':
                        break
                    for grp in groups:
                        self.mixer_p1(l, S, grp)
                    if dbg == 'p1':
                        break
                    for i in range(2):
                        self.memset(self.vbuf[i], 0.0)
                    for kb in self.kbuf:
                        self.memset(kb[:, :], 0.0)
                    self.memset(self.vbuf[0][:, :, 64:65], 1.0)
                    self.memset(self.vbuf[1][:, :, 0:1], 1.0)
                    if rope:
                        self.exchange(l, S)
                    if dbg == 'xch':
                        break
                    if rope:

                        def kr_fill(kb, l=l):
                            self.dma(kb[64:96, 0:512], self.cache_k[l], q='pool')
                            self.dma(kb[64:96, 512:512 + NST], self.xout[128:160, :], q='pool')
                            self.dma(kb[64:96, 512 + NST:NKS], self.xout[XR + 128:XR + 160, :], q='pool')
                        self.attention(l, S, self.c_all, NKS, [(ti * NT, NT) for ti in range(ntile)], kr_fill)
                    else:
                        for sq in range(4):
                            def kr_fill(kb, sq=sq):
                                self.copy(kb[64:96, 0:256], S['krT'][64:96, sq * 256:(sq + 1) * 256])
                            self.attention(l, S, S['ckvT'][:, sq * 256:(sq + 1) * 256], 256, [(sq * 256, 256)], kr_fill)
                    self.att_flush()
                    if dbg == 'att0':
                        break
                    for ti in range(ntile):
                        self.merge(l, S, ti)
                    if dbg == 'mix0':
                        break
                    for grp in fgroups:
                        self.ffn(l, 1, [(S['x'][:, :, ti * NT:(ti + 1) * NT], NT) for ti in grp], S['A'][l][2], S['B'][l][2], S['C'][l][2])
                    if dbg == 'l0':
                        break
                dst = (self.yT_S if rope else self.yT_P).ap().rearrange("p (k n) -> p k n", k=KC)
                for k in range(KC):
                    self.dma(dst[:, k, :], S['x'][:, k, :])
                if dbg in ('att0',):
                    self.dbg_att = S['att']
            block = es.enter_context(nc.Block())
            P.finalize(es, block)
        return nc

    def rope_tables(self):
        r = slice(64, 96)
        self.dma(self.cosT[r, :], self.rpos_d[64:96, 0:NST], q='pool')
        self.dma(self.sinT[r, :], self.rpos_d[64:96, NST:2 * NST], q='pool')

    def exchange(self, l, S):
        zpd, pcd = S['zp_d'], S['pc_d']
        hv, hv2 = self.halo, self.halo2
        x160 = self.xin[160:162, :].rearrange("r (q e) -> (r q) e", e=32)
        x162 = self.xin[162:163, 0:512].rearrange("r (q e) -> (r q) e", e=4)
        for c in range(2):
            self.dma(hv[:, c * 16:c * 16 + 8], zpd[:, c, 8:16])
            self.dma(hv[:, c * 16 + 8:c * 16 + 16], zpd[:, c, NST:NST + 8])
            self.dma(hv[:, 32 + c * 2:32 + c * 2 + 1], pcd[:, c, 1:2])
            self.dma(hv[:, 32 + c * 2 + 1:32 + c * 2 + 2], pcd[:, c, NST:NST + 1])
        self.dma(x160, hv[:, 0:32])
        self.dma(x162, hv[:, 32:36])
        xi = self.xin.ap().opt()
        xo = self.xout.ap().opt()
        self.P.op('pool', lambda e: e.collective_compute("AllGather", ALU.bypass, replica_groups=[[0, 1], [2, 3], [4, 5], [6, 7]],
                                                        ins=[xi], outs=[xo]), [self.xin.ap()], [self.xout.ap()], kind='cc')
        self.dma(self.c_all[:, 0:512], self.cache_c[l], q='pool')
        self.dma(self.c_all[:, 512:512 + NST], self.xout[0:128, :], q='pool')
        self.dma(self.c_all[:, 512 + NST:NKS], self.xout[XR:XR + 128, :], q='pool')
        o160a = self.xout[160:162, :].rearrange("r (q e) -> (r q) e", e=32)
        o162a = self.xout[162:163, 0:512].rearrange("r (q e) -> (r q) e", e=4)
        o160b = self.xout[XR + 160:XR + 162, :].rearrange("r (q e) -> (r q) e", e=32)
        o162b = self.xout[XR + 162:XR + 163, 0:512].rearrange("r (q e) -> (r q) e", e=4)
        self.dma(hv[:, 0:32], o160a)
        self.dma(hv[:, 32:36], o162a)
        self.dma(hv2[:, 0:32], o160b)
        self.dma(hv2[:, 32:36], o162b)
        ml = self.vecs[:, G_ML:G_ML + 1]
        mr = self.vecs[:, G_MR:G_MR + 1]
        self.ts(hv[:, :], hv[:, :], ml, None, ALU.mult)
        self.ts(hv2[:, :], hv2[:, :], mr, None, ALU.mult)
        for c in range(2):
            self.dma(zpd[:, c, 0:8], hv[:, c * 16 + 8:c * 16 + 16])
            self.dma(zpd[:, c, 8 + NST:16 + NST], hv2[:, c * 16:c * 16 + 8])
            self.dma(pcd[:, c, 0:1], hv[:, 32 + c * 2 + 1:32 + c * 2 + 2])
            self.dma(pcd[:, c, NST + 1:NST + 2], hv2[:, 32 + c * 2:32 + c * 2 + 1])


def _tile_w(W, bw):
    Din, Dout = W.shape
    kc = Din // 128
    nb = Dout // bw
    t = W.reshape(kc, 128, nb, bw).transpose(2, 1, 0, 3)
    return np.ascontiguousarray(t).reshape(nb, 128, kc * bw)


def _cols(v):
    return np.ascontiguousarray(v.reshape(-1, 128).T)


def _host_prep(inp):
    f = np.float32
    g = {k: np.asarray(v) for k, v in inp.items()}
    sh = {}
    wgu = np.empty((L, 2, FC, 128, KC * 256), f)
    wdn = np.empty((L, 2, KC, 128, FC * 128), f)
    for l in range(L):
        for j in range(2):
            W = g['w_ffn_gu'][l, j]
            Wg = W[:, :DFF].reshape(KC, 128, FC, 128)
            Wu = W[:, DFF:].reshape(KC, 128, FC, 128)
            t = np.stack([Wg, Wu], axis=3)
            wgu[l, j] = t.transpose(2, 1, 0, 3, 4).reshape(FC, 128, KC * 256)
            wdn[l, j] = _tile_w(g['w_ffn_dn'][l, j], 128)
    sh['w_gu'] = wgu
    sh['w_dn'] = wdn
    win = np.zeros((L, 9, 128, KC * 256), f)
    perm = np.concatenate([np.arange(8, 16), np.arange(0, 8), np.arange(24, 32), np.arange(16, 24)])
    for l in range(L):
        W = g['w_in'][l]
        Wp = np.zeros((D, 9 * 256), f)
        Wp[:, 0:256] = W[:, 0:256]
        Wp[:, 256:512] = W[:, 256:512]
        Wp[:, 512:768] = W[:, 512:768]
        Wp[:, 768:1024] = W[:, 768:1024]
        Wp[:, 1024:1152] = W[:, 1024:1152]
        Wp[:, 1152 + 64:1152 + 96] = W[:, 1152:1184]
        Wp[:, 1280:1536] = W[:, 1184:1440]
        Wp[:, 1536:1792] = W[:, 1440:1696]
        Wp[:, 1792:2048] = W[:, 1696:1952]
        Wp[:, 2048 + 64:2048 + 96] = W[:, 1152:1184][:, perm]
        win[l] = _tile_w(Wp, 256)
    sh['w_in'] = win
    wgate = np.empty((L, KC, 2, 128, KC * 256), f)
    wbr = np.empty((L, KC, 128, 10 * 128), f)
    wo = np.empty((L, KC, 128, KC * 128), f)
    for l in range(L):
        W = g['w_gate'][l].reshape(D, 4, KC, 128)
        Wm = W.transpose(0, 2, 1, 3).reshape(D, KC * 2, 256)
        Wm = Wm.reshape(D, KC * 2 * 256)
        wgate[l] = _tile_w(Wm, 256).reshape(KC, 2, 128, KC * 256)
        Wb = np.concatenate([g['w_br_pool'][l], g['w_br_sgu'][l], g['w_br_mla'][l], g['w_br_conv'][l]], axis=0)
        wbr[l] = _tile_w(Wb, 128)
        wo[l] = _tile_w(g['w_o'][l], 128)
    sh['w_gate'] = wgate
    sh['w_br'] = wbr
    sh['w_o'] = wo
    vecs = np.zeros((128, G_N), f)
    for l in range(L):
        b = l * V_N
        vecs[:, b + V_BMOD:b + V_BMOD + 72] = _cols(g['b_mod'][l])
        for s in range(3):
            vecs[:, b + V_GPRE + s * 8:b + V_GPRE + s * 8 + 8] = _cols(g['g_pre'][l, s])
            vecs[:, b + V_GPOST + s * 8:b + V_GPOST + s * 8 + 8] = _cols(g['g_post'][l, s])
        vecs[:, b + V_PSCALE:b + V_PSCALE + 2] = _cols(g['pool_scale'][l])
        vecs[:, b + V_GQ:b + V_GQ + 2] = _cols(g['g_q'][l])
        vecs[:, b + V_GKV:b + V_GKV + 1] = _cols(g['g_kv'][l])
        for k in range(3):
            vecs[:, b + V_CONVW + k * 2:b + V_CONVW + k * 2 + 2] = _cols(g['conv_w'][l, k])
        vecs[:, b + V_BGATE:b + V_BGATE + 32] = _cols(g['b_gate'][l])
    half_d = 16
    freqs = (10000.0 ** (-(2.0 * np.arange(8, dtype=np.float32)) / half_d)).astype(f)
    for ff in range(32):
        vecs[64 + ff, G_FREQ] = freqs[ff % 8]
        vecs[64 + ff, G_SIGN] = -1.0 if (ff % 16) < 8 else 1.0
    for c in range(2):
        vecs[0:64, G_INVW + c] = 1.0 / (2, 8)[c]
        vecs[64:128, G_INVW + c] = 1.0 / (4, 16)[c]
    sh['vecs'] = vecs
    sh['gsgu'] = np.ascontiguousarray(np.broadcast_to(g['g_sgu'][:, None, :], (L, 128, 256))).astype(f)
    bsT = np.zeros((L, 128, 2, 128), f)
    for l in range(L):
        for c in range(2):
            bsT[l, 0:64, c, :] = g['b_sgu'][l, 2 * c][None, :]
            bsT[l, 64:128, c, :] = g['b_sgu'][l, 2 * c + 1][None, :]
    sh['bsT'] = bsT.reshape(L, 128, 256)
    sh['wsT'] = np.ascontiguousarray(g['w_sgu'].transpose(0, 3, 1, 2)).reshape(L, 128, 512)
    wpool = np.zeros((L, 128, 2, 128), f)
    for l in range(L):
        for c in range(2):
            wpool[l, 0:64, c, 0:64] = g['w_pool'][l, 2 * c]
            wpool[l, 64:128, c, 64:128] = g['w_pool'][l, 2 * c + 1]
    sh['wpool'] = wpool.reshape(L, 128, 256)
    wuq = g['w_uq']
    wuqs = wuq.copy()
    for h in range(8):
        wuqs[:, :, h * 96 + 64:h * 96 + 96] = wuq[:, :, h * 96 + 64:h * 96 + 96][:, :, perm]
    sh['wuq'] = np.ascontiguousarray(wuq.reshape(L, 2, 128, 768).transpose(0, 2, 1, 3)).reshape(L, 128, 1536)
    sh['wuqs'] = np.ascontiguousarray(wuqs.reshape(L, 2, 128, 768).transpose(0, 2, 1, 3)).reshape(L, 128, 1536)
    sh['wukv'] = np.ascontiguousarray(g['w_ukv'])
    def edge_tab(first_edge, last_edge):
        t = np.zeros((128, 2, 16), f)
        for c in range(2):
            for hf in range(2):
                w = ((2, 4), (8, 16))[c][hf]
                rows = slice(hf * 64, hf * 64 + 64)
                for i in range(8):
                    cnt_f = (min(i + w // 2, 10 ** 9) - max(i - w // 2, 0)) if first_edge else w
                    d = 8 - i
                    cnt_l = (min(w // 2, d) + w // 2) if last_edge else w
                    t[rows, c, i] = 1.0 / cnt_f
                    t[rows, c, 8 + i] = 1.0 / cnt_l
        return t.reshape(128, 32)
    sh['icntP'] = edge_tab(True, True)
    sh['_edge'] = edge_tab
    return g, sh


_NC_CACHE = {}


def _get_nc(debug_stage=None):
    if debug_stage not in _NC_CACHE:
        b = Builder(debug_stage)
        _NC_CACHE[debug_stage] = (b.build(), b)
    return _NC_CACHE[debug_stage]


def _in_maps(inputs):
    f = np.float32
    g, sh = _host_prep(inputs)
    in_maps = []
    for i in range(8):
        b, half = i // 2, i % 2
        m = dict((k, v) for k, v in sh.items() if not k.startswith('_'))
        xp = g['x_prompt'][4 * i:4 * i + 4].reshape(NPT, KC, 128)
        m['xT_P'] = np.ascontiguousarray(xp.transpose(2, 1, 0)).reshape(128, KC * NPT)
        xs = g['x_sample'][b, half * NST:(half + 1) * NST].reshape(NST, KC, 128)
        m['xT_S'] = np.ascontiguousarray(xs.transpose(2, 1, 0)).reshape(128, KC * NST)
        cond = np.stack([g['c_ctx'], g['c'][b]], axis=-1)
        m['condT'] = np.ascontiguousarray(cond.reshape(KC, 128, 2).transpose(1, 0, 2)).reshape(128, KC * 2)
        m['w_mod'] = np.stack([_tile_w(g['w_mod'][l][:, half * 4608:(half + 1) * 4608], 128) for l in range(L)])
        m['cache_c'] = np.ascontiguousarray(g['cache_ckv'][b].transpose(0, 2, 1))
        m['cache_k'] = np.ascontiguousarray(g['cache_krope'][b].transpose(0, 2, 1))
        v = sh['vecs'].copy()
        v[:, G_ML] = 1.0 if half == 1 else 0.0
        v[:, G_MR] = 1.0 if half == 0 else 0.0
        v[:, G_SEL:G_SEL + 4] = 0.0
        v[:, G_SEL + b] = 1.0
        m['vecs'] = v
        pos = half * NST + np.arange(NST)
        rp = np.zeros((128, 2 * NST), f)
        for ff in range(32):
            pv = ((pos // 64) if ff < 16 else (pos % 64)).astype(f)
            ang = pv * sh['vecs'][64 + ff, G_FREQ]
            rp[64 + ff, 0:NST] = np.cos(ang)
            rp[64 + ff, NST:] = np.sin(ang) * sh['vecs'][64 + ff, G_SIGN]
        m['rpos'] = rp
        m['icntS'] = sh['_edge'](half == 0, half == 1)
        in_maps.append(m)
    return in_maps


def kernel(debug_stage=None, **inputs):
    f = np.float32
    nc, _ = _get_nc(debug_stage)
    in_maps = _in_maps(inputs)
    res = run_bass_kernel_spmd(nc, in_maps, core_ids=list(range(8)))
    R = res.results
    y_prompt = np.empty((32, 256, D), f)
    y_sample = np.empty((4, 4096, D), f)
    st_ckv = np.empty((32, L, 256, 128), f)
    st_kr = np.empty((32, L, 256, 32), f)
    for i in range(8):
        b, half = i // 2, i % 2
        yp = R[i]['yT_P'].reshape(128, KC, NPT).transpose(2, 1, 0).reshape(4, 256, D)
        y_prompt[4 * i:4 * i + 4] = yp
        ys = R[i]['yT_S'].reshape(128, KC, NST).transpose(2, 1, 0).reshape(NST, D)
        y_sample[b, half * NST:(half + 1) * NST] = ys
        sc = R[i]['st_ckv'].reshape(L, 128, 4, 256).transpose(2, 0, 3, 1)
        st_ckv[4 * i:4 * i + 4] = sc
        sk = R[i]['st_kr'].reshape(L, 32, 4, 256).transpose(2, 0, 3, 1)
        st_kr[4 * i:4 * i + 4] = sk
    return (y_prompt, y_sample, st_ckv, st_kr)
```

```python
import numpy as np
import concourse.bass as bass
import concourse.mybir as mybir
from concourse.bass_utils import run_bass_kernel_spmd

F32 = mybir.dt.float32
BF16 = mybir.dt.bfloat16
AF = mybir.ActivationFunctionType
ALU = mybir.AluOpType

D = 1024
KC = 8
DFF = 2816
FC = 22
NT = 512
L = 2
EPS = 1e-6
NPT = 1024
NST = 2048
NKS = 512 + 4096
XR = 163
SCALE = 96 ** -0.5
ESZ = {F32: 4, BF16: 2}

V_BMOD = 0
V_GPRE = 72
V_GPOST = 96
V_PSCALE = 120
V_GQ = 122
V_GKV = 124
V_CONVW = 125
V_BGATE = 131
V_N = 163
G_FREQ = L * V_N
G_SIGN = G_FREQ + 1
G_ML = G_FREQ + 2
G_MR = G_FREQ + 3
G_INVW = G_FREQ + 4
G_SEL = G_FREQ + 6
G_N = G_FREQ + 10


class Prog:
    RING = 8

    def __init__(self, nc):
        self.nc = nc
        self.ins = []
        self.live_w = {}
        self.live_r = {}
        self.sb = {}
        self.ps = set()
        self.dram_out = set()

    def acc(self, ap):
        name = ap.name
        apl = ap.ap
        if name in self.sb:
            base, es = self.sb[name]
            ps = apl[0][0]
            fo = ap.offset % ps
            ext = sum((c - 1) * s for s, c in apl[1:])
            return ('sb', base + fo * es, base + (fo + ext + 1) * es)
        if name in self.ps:
            return (name, 0, 512)
        ext = sum((c - 1) * s for s, c in apl)
        return (name, ap.offset, ap.offset + ext + 1)

    def op(self, eng, fn, reads=(), writes=(), kind='c'):
        idx = len(self.ins)
        racc = [self.acc(a) for a in reads]
        wacc = [self.acc(a) for a in writes]
        raw, other = set(), set()
        for (sp, lo, hi) in racc:
            for (l2, h2, j) in self.live_w.get(sp, ()):
                if l2 < hi and lo < h2:
                    raw.add(j)
        for (sp, lo, hi) in wacc:
            for (l2, h2, j) in self.live_w.get(sp, ()):
                if l2 < hi and lo < h2:
                    other.add(j)
            for (l2, h2, j) in self.live_r.get(sp, ()):
                if l2 < hi and lo < h2:
                    other.add(j)
        deps = set()
        for j in raw | other:
            pj = self.ins[j]
            if pj['eng'] == eng and kind == 'c' and pj['kind'] == 'c':
                if eng == 'pe':
                    continue
            deps.add(j)
        deps.discard(idx)
        for (sp, lo, hi) in wacc:
            lw = self.live_w.setdefault(sp, [])
            lw[:] = [r for r in lw if not (lo <= r[0] and r[1] <= hi)]
            lr = self.live_r.setdefault(sp, [])
            lr[:] = [r for r in lr if not (lo <= r[0] and r[1] <= hi)]
            lw.append((lo, hi, idx))
        for (sp, lo, hi) in racc:
            lr = self.live_r.setdefault(sp, [])
            if kind == 'c':
                lr[:] = [r for r in lr if not (r[0] == lo and r[1] == hi and self.ins[r[2]]['eng'] == eng
                                               and self.ins[r[2]]['kind'] == 'c')]
            lr.append((lo, hi, idx))
        is_out = any(a.name in self.dram_out for a in writes)
        self.ins.append(dict(eng=eng, fn=fn, kind=kind, deps=deps, out=is_out))
        return idx

    def finalize(self, es, block):
        nc = self.nc
        ins = self.ins
        has_dep = [False] * len(ins)
        for it in ins:
            for j in it['deps']:
                has_dep[j] = True
        engs = ['pe', 'act', 'dve', 'pool', 'sp']
        esem = {e: es.enter_context(nc.semaphore('s_' + e)) for e in engs}
        ring = {q: [es.enter_context(nc.semaphore('r_%s%d' % (q, i))) for i in range(self.RING)] for q in ('sp', 'pool')}
        ccsem = es.enter_context(nc.semaphore('s_cc'))
        ecnt = {e: 0 for e in engs}
        dcnt = {'sp': 0, 'pool': 0}
        cccnt = 0
        for i, it in enumerate(ins):
            e = it['eng']
            it['prewait'] = None
            if it['kind'] == 'c':
                if has_dep[i]:
                    ecnt[e] += 1
                    it['sig'] = (esem[e], ecnt[e], 1)
                else:
                    it['sig'] = None
            elif it['kind'] == 'd':
                n = dcnt[e]
                dcnt[e] += 1
                slot = n % self.RING
                use = n // self.RING
                it['sig'] = (ring[e][slot], 16 * (use + 1), 16)
                if use > 0:
                    it['prewait'] = (ring[e][slot], 16 * use)
            else:
                cccnt += 1
                it['sig'] = (ccsem, cccnt, None)
        print('tracker: instrs', len(ins), 'sem counts', ecnt, dcnt, cccnt, flush=True)
        per = {e: [] for e in engs}
        for i, it in enumerate(ins):
            per[it['eng']].append(i)
        outs_wait = [ins[i]['sig'] for i in range(len(ins)) if ins[i]['out']]
        eobj = {'pe': nc.tensor, 'act': nc.scalar, 'dve': nc.vector, 'pool': nc.gpsimd, 'sp': nc.sync}

        def run(e):
            def body(engine):
                seen = {}
                for i in per[e]:
                    it = ins[i]
                    waits = {}
                    for j in it['deps']:
                        sg = ins[j]['sig']
                        k = id(sg[0])
                        if k not in waits or waits[k][1] < sg[1]:
                            waits[k] = (sg[0], sg[1])
                    if it['prewait'] is not None:
                        sg = it['prewait']
                        k = id(sg[0])
                        if k not in waits or waits[k][1] < sg[1]:
                            waits[k] = sg
                    for k, (s, v) in waits.items():
                        if seen.get(k, 0) < v:
                            engine.wait_ge(s, v)
                            seen[k] = v
                    r = it['fn'](engine)
                    sg = it['sig']
                    if sg is not None:
                        if sg[2] is None:
                            r.then_inc(sg[0])
                        else:
                            r.then_inc(sg[0], sg[2])
                if e == 'sp':
                    for (s, v, _) in outs_wait:
                        if seen.get(id(s), 0) < v:
                            engine.wait_ge(s, v)
                            seen[id(s)] = v
            return body
        block.tensor(run('pe'))
        block.scalar(run('act'))
        block.vector(run('dve'))
        block.gpsimd(run('pool'))
        block.sync(run('sp'))


class Builder:
    def __init__(self, debug_stage=None):
        self.debug_stage = debug_stage
        self.nc = bass.Bass("TRN2", target_bir_lowering=False)
        self.P = Prog(self.nc)
        self.sp_top = 16512
        self.psum_rr = 0
        self.rr = {}
        self.wr_i = 0
        self.att_pend = None

    def sb(self, name, free, dt, at=None):
        es = ESZ[dt]
        if at is None:
            at = self.sp_top
        at = (at + 31) // 32 * 32
        t = self.nc.alloc_sbuf_tensor_at(name, [128, free], dt, offset=at)
        end = at + free * es
        assert end <= 229000, (name, end)
        self.P.sb[t[:, 0:1].name] = (at, es)
        if end > self.sp_top:
            self.sp_top = end
        self.peak = max(getattr(self, 'peak', 0), end)
        return t

    def din(self, name, shape):
        return self.nc.dram_tensor(name, list(shape), F32, kind="ExternalInput")

    def dout(self, name, shape):
        t = self.nc.dram_tensor(name, list(shape), F32, kind="ExternalOutput")
        self.P.dram_out.add(t.ap().name)
        return t

    def mm(self, out, lhsT, rhs, start, stop):
        self.P.op('pe', lambda e: e.matmul(out, lhsT, rhs, start=start, stop=stop), [lhsT, rhs], [out])

    def act(self, out, in_, func, bias=None, scale=None, accum=None, extra_r=()):
        kw = {}
        rd = [in_] + list(extra_r)
        if bias is not None:
            kw['bias'] = bias
            if not isinstance(bias, float):
                rd.append(bias)
        if scale is not None:
            kw['scale'] = scale
            if not isinstance(scale, float):
                rd.append(scale)
        wr = [out]
        if accum is not None:
            kw['accum_out'] = accum
            wr.append(accum)
        self.P.op('act', lambda e: e.activation(out, in_, func, **kw), rd, wr)

    def tt(self, out, in0, in1, op, eng='dve'):
        self.P.op(eng, lambda e: e.tensor_tensor(out, in0, in1, op), [in0, in1], [out])

    def ts(self, out, in0, s1, s2, op0, op1=None, eng='dve'):
        rd = [in0] + [s for s in (s1, s2) if s is not None and not isinstance(s, float)]
        if op1 is None:
            self.P.op(eng, lambda e: e.tensor_scalar(out, in0, s1, None, op0), rd, [out])
        else:
            self.P.op(eng, lambda e: e.tensor_scalar(out, in0, s1, s2, op0, op1), rd, [out])

    def stt(self, out, in0, scalar, in1, op0, op1, eng='dve'):
        rd = [in0, in1] + ([] if isinstance(scalar, float) else [scalar])
        self.P.op(eng, lambda e: e.scalar_tensor_tensor(out, in0, scalar, in1, op0, op1), rd, [out])

    def recip(self, out, in_):
        self.P.op('dve', lambda e: e.reciprocal(out, in_), [in_], [out])

    def copy(self, out, in_, eng='dve'):
        self.P.op(eng, lambda e: e.tensor_copy(out, in_), [in_], [out])

    def memset(self, ap, val, eng='pool'):
        self.P.op(eng, lambda e: e.memset(ap, val), [], [ap])

    def dma(self, out, in_, q='sp'):
        self.P.op(q, lambda e: e.dma_start(out=out, in_=in_, allow_slow_non_contiguous=True), [in_], [out], kind='d')

    def psum(self):
        b = self.pbanks[self.psum_rr % 6]
        self.psum_rr += 1
        return b

    def tmp(self, key):
        lst = self.tmps[key]
        i = self.rr.get(key, 0)
        self.rr[key] = i + 1
        return lst[i % len(lst)]

    def wload(self, src2d, kc, bw):
        slot = self.wring[self.wr_i % len(self.wring)]
        self.wr_i += 1
        n = kc * bw
        dst = slot[:, 0:n]
        self.dma(dst, src2d, q='pool')
        return dst.rearrange("p (k c) -> p k c", k=kc)

    def rstd_from_ss(self, ss_ps, n, nfeat, rows=slice(0, 128)):
        s = self.tmp('nrm')
        self.act(s[rows, 0:n], ss_ps, AF.Sqrt, bias=self.eps_col[rows, 0:1], scale=1.0 / nfeat)
        r = self.tmp('nrm')
        self.recip(r[rows, 0:n], s[rows, 0:n])
        return r

    def prenorm(self, xt, n, Acols, Bcols, h):
        ss = self.ssbank()
        for c in range(KC):
            sq = self.tmp('sq')
            self.act(sq[:, 0:n], xt[:, c, :], AF.Square)
            self.mm(ss[:, 0:n], self.ones_bf[:, :], sq[:, 0:n], c == 0, c == KC - 1)
        r = self.rstd_from_ss(ss[:, 0:n], n, D)
        for c in range(KC):
            t = self.tmp('f32')
            self.tt(t[:, 0:n], xt[:, c, :], r[:, 0:n], ALU.mult)
            self.act(h[:, c, :], t[:, 0:n], AF.Identity, bias=Bcols[:, c:c + 1], scale=Acols[:, c:c + 1])

    def ssbank(self):
        b = self.pbanks[6 + (self.rr.get('ss', 0) % 2)]
        self.rr['ss'] = self.rr.get('ss', 0) + 1
        return b

    def post_residual(self, fs, ss, xt, n, Ccols):
        r = self.rstd_from_ss(ss[:, 0:n], n, D)
        for m in range(KC):
            t = self.tmp('f32')
            self.tt(t[:, 0:n], fs[:, m, :], r[:, 0:n], ALU.mult)
            self.stt(xt[:, m, :], t[:, 0:n], Ccols[:, m:m + 1], xt[:, m, :], ALU.mult, ALU.add)

    def evac_f(self, ps, n, fs_m, ss, first, last):
        self.act(fs_m, ps, AF.Copy)
        sq = self.tmp('sq')
        self.act(sq[:, 0:n], ps, AF.Square)
        return lambda: self.mm(ss[:, 0:n], self.ones_bf[:, :], sq[:, 0:n], first, last)

    def ffn(self, l, f, tiles, Acols, Bcols, Ccols):
        G = len(tiles)
        for g, (xt, n) in enumerate(tiles):
            self.prenorm(xt, n, Acols, Bcols, self.f_h[g][:, :, 0:n])
        for j in range(FC):
            wb = self.wload(self.w_gu[l, f, j], KC, 256)
            for g, (xt, n) in enumerate(tiles):
                h = self.f_h[g]
                pg = self.psum()
                pu = self.psum()
                for k in range(KC):
                    self.mm(pg[:, 0:n], wb[:, k, 0:128], h[:, k, 0:n], k == 0, k == KC - 1)
                for k in range(KC):
                    self.mm(pu[:, 0:n], wb[:, k, 128:256], h[:, k, 0:n], k == 0, k == KC - 1)
                sg = self.tmp('f32')
                self.act(sg[:, 0:n], pg[:, 0:n], AF.Silu)
                self.tt(self.f_hid[g][:, j, 0:n], sg[:, 0:n], pu[:, 0:n], ALU.mult)
        sss = [self.ssbank() for _ in tiles]
        pend = None
        for m in range(KC):
            wb = self.wload(self.w_dn[l, f, m], FC, 128)
            for g, (xt, n) in enumerate(tiles):
                ps = self.psum()
                for j in range(FC):
                    self.mm(ps[:, 0:n], wb[:, j, :], self.f_hid[g][:, j, 0:n], j == 0, j == FC - 1)
                if pend is not None:
                    pend()
                pend = self.evac_f(ps[:, 0:n], n, self.f_fs[g][:, m, 0:n], sss[g], m == 0, m == KC - 1)
        pend()
        for g, (xt, n) in enumerate(tiles):
            self.post_residual(self.f_fs[g][:, :, 0:n], sss[g], xt, n, Ccols)

    def fm_norm(self, src, nch, n, gcols, out, out_dt_bf=True):
        ss = self.ssbank()
        for c in range(nch):
            sq = self.tmp('sq')
            self.act(sq[:, 0:n], src[:, c, :], AF.Square)
            self.mm(ss[:, 0:n], self.ones_bf[:, :], sq[:, 0:n], c == 0, c == nch - 1)
        r = self.rstd_from_ss(ss[:, 0:n], n, nch * 128)
        for c in range(nch):
            t = self.tmp('f32')
            self.tt(t[:, 0:n], src[:, c, :], r[:, 0:n], ALU.mult)
            self.act(out[:, c, :], t[:, 0:n], AF.Copy, scale=gcols[:, c:c + 1])

    def mixer_p1(self, l, S, group):
        vl = self.vecs
        vb = l * V_N
        for g, ti in enumerate(group):
            xt = S['x'][:, :, ti * NT:(ti + 1) * NT]
            self.prenorm(xt, NT, S['A'][l][1], S['B'][l][1], self.hbuf[g][:, :, 0:NT])
        nblk = 9 if S['rope'] else 8
        import os
        only = os.environ.get('P1B')
        for b in range(nblk):
            if only is not None and str(b) not in only.split(','):
                continue
            wb = self.wload(self.w_in[l, b], KC, 256)
            for g, ti in enumerate(group):
                h = self.hbuf[g]
                t0 = ti * NT
                if b == 2:
                    self.sgu(l, S, g, ti, wb)
                    continue
                pss = []
                if b in (4, 8):
                    ps = self.psum()
                    if b == 4:
                        for k in range(KC):
                            self.mm(ps[:, 0:NT], wb[:, k, 0:128], h[:, k, 0:NT], k == 0, k == KC - 1)
                    else:
                        for k in range(KC):
                            self.mm(ps[0:96, 0:NT], wb[:, k, 0:96], h[:, k, 0:NT], k == 0, k == KC - 1)
                    pss.append(ps)
                else:
                    for c in range(2):
                        ps = self.psum()
                        for k in range(KC):
                            self.mm(ps[:, 0:NT], wb[:, k, c * 128:(c + 1) * 128], h[:, k, 0:NT], k == 0, k == KC - 1)
                        pss.append(ps)
                if b == 0:
                    for c in range(2):
                        self.act(self.zc_sb[g][:, c, :], pss[c][:, 0:NT], AF.Copy)
                    for (po, to, ln) in S['pieces'](ti):
                        self.dma(S['zp_d'][:, :, po + 8:po + 8 + ln], self.zc_sb[g][:, :, to:to + ln])
                elif b == 1:
                    for c in range(2):
                        self.act(self.u_sb[g][:, c, :], pss[c][:, 0:NT], AF.Copy)
                elif b == 3:
                    for c in range(2):
                        self.act(self.cq_sb[:, c, :], pss[c][:, 0:NT], AF.Copy)
                    self.fm_norm(self.cq_sb, 2, NT, vl[:, vb + V_GQ:vb + V_GQ + 2], S['cqn'][:, :, t0:t0 + NT])
                elif b == 4:
                    self.act(self.cq_sb[:, 0, :], pss[0][:, 0:NT], AF.Copy)
                    self.fm_norm(self.cq_sb[:, 0:1, :], 1, NT, vl[:, vb + V_GKV:vb + V_GKV + 1], self.ckvn_f3)
                    ps2 = self.psum()
                    for k in range(KC):
                        self.mm(ps2[0:96, 0:NT], wb[:, k, 128:224], h[:, k, 0:NT], k == 0, k == KC - 1)
                    self.act(self.kr_f[g][64:96, :], ps2[64:96, 0:NT], AF.Copy)
                    if S['rope']:
                        self.dma(self.xin[0:128, t0:t0 + NT], self.ckvn_f[:, :])
                    else:
                        self.dma(self.st_ckv[l, :, t0:t0 + NT], self.ckvn_f[:, :])
                        self.dma(self.st_kr[l, :, t0:t0 + NT], self.kr_f[g][64:96, :])
                        self.copy(S['ckvT'][:, t0:t0 + NT], self.ckvn_f[:, :])
                        self.copy(S['krT'][64:96, t0:t0 + NT], self.kr_f[g][64:96, :])
                elif b == 5:
                    for c in range(2):
                        self.act(self.zb_t[:, c, :], pss[c][:, 0:NT], AF.Copy)
                    self.dma(S['zb_d'][:, :, t0:t0 + NT], self.zb_t)
                elif b == 6:
                    for c in range(2):
                        self.act(self.zc_sb[g][:, c, :], pss[c][:, 0:NT], AF.Copy)
                elif b == 7:
                    for c in range(2):
                        self.tt(self.u_sb[g][:, c, :], self.zc_sb[g][:, c, :], pss[c][:, 0:NT], ALU.mult)
                    for (po, to, ln) in S['pieces'](ti):
                        p1 = S['cpos'](po)
                        self.dma(S['pc_d'][:, :, p1 + 1:p1 + 1 + ln], self.u_sb[g][:, :, to:to + ln])
                elif b == 8:
                    r = slice(64, 96)
                    t1 = self.tmp('f32')
                    t2 = self.tmp('f32')
                    self.tt(t1[r, 0:NT], self.kr_f[g][r, :], self.cosT[r, t0:t0 + NT], ALU.mult)
                    self.tt(t2[r, 0:NT], pss[0][r, 0:NT], self.sinT[r, t0:t0 + NT], ALU.mult)
                    self.tt(t1[r, 0:NT], t1[r, 0:NT], t2[r, 0:NT], ALU.add)
                    self.dma(self.xin[128:160, t0:t0 + NT], t1[r, 0:NT])

    def sgu(self, l, S, g, ti, wb):
        h = self.hbuf[g]
        t0 = ti * NT
        vn = self.vn_sb
        import os
        lvl = int(os.environ.get('SGU', '9'))
        for half in range(2):
            pv = self.psum()
            for bb in range(2):
                blk = half * 2 + bb
                for k in range(KC):
                    self.mm(pv[:, bb * 256:(bb + 1) * 256], h[:, k, blk * 128:(blk + 1) * 128], wb[:, k, 0:256],
                            k == 0, k == KC - 1)
            for bb in range(2):
                if lvl < 2:
                    break
                blk = half * 2 + bb
                junk = self.tmp('f32')
                self.act(junk[:, 0:256], pv[:, bb * 256:(bb + 1) * 256], AF.Square)
                jj, so = junk[:, 0:256], self.ssq[:, blk:blk + 1]
                self.P.op('dve', lambda e, jj=jj, so=so: e.reduce_sum(so, jj, mybir.AxisListType.X), [jj], [so])
                if lvl < 3:
                    continue
                self.act(self.ssq[:, 4 + blk:5 + blk], self.ssq[:, blk:blk + 1], AF.Sqrt, bias=self.eps_col[:, 0:1], scale=1.0 / 256)
                self.recip(self.ssq[:, 8 + blk:9 + blk], self.ssq[:, 4 + blk:5 + blk])
                self.stt(vn[:, blk, :], pv[:, bb * 256:(bb + 1) * 256], self.ssq[:, 8 + blk:9 + blk], self.gsgu[:, :],
                         ALU.mult, ALU.mult)
        for c in range(2):
            if lvl < 4:
                break
            pa = self.psum()
            pb = self.psum()
            for blk in range(4):
                self.mm(pa[:, blk * 128:(blk + 1) * 128], vn[:, blk, c * 128:(c + 1) * 128], self.wsT[:, 2 * c, :], True, True)
                self.mm(pb[:, blk * 128:(blk + 1) * 128], vn[:, blk, c * 128:(c + 1) * 128], self.wsT[:, 2 * c + 1, :], True, True)
            for (rows, pp) in ((slice(0, 64), pa), (slice(64, 128), pb)):
                if lvl < 5:
                    break
                t = self.tmp('f32')
                for blk in range(4):
                    self.tt(t[rows, blk * 128:(blk + 1) * 128], pp[rows, blk * 128:(blk + 1) * 128], self.bsT[rows, c, :], ALU.add)
                self.tt(self.bsg_t[rows, c, :], t[rows, 0:NT], self.u_sb[g][rows, c, :], ALU.mult)
        if lvl >= 6:
            self.dma(S['bsg_d'][:, :, t0:t0 + NT], self.bsg_t)

    def pool_conv(self, l, S, ti):
        vb = l * V_N
        vl = self.vecs
        t0 = ti * NT
        self.dma(self.zb_t, S['zb_d'][:, :, t0:t0 + NT])
        self.dma(self.bsg_t, S['bsg_d'][:, :, t0:t0 + NT])
        for (po, to, ln) in S['pieces'](ti):
            zw = self.zpw
            pw_ = self.pcw
            p1 = S['cpos'](po)
            self.dma(zw[:, :, 0:ln + 16], S['zp_d'][:, :, po:po + ln + 16])
            self.dma(pw_[:, :, 0:ln + 2], S['pc_d'][:, :, p1:p1 + ln + 2])
            first, last = S['edges'](ti, to)
            for c in range(2):
                s2 = self.tmp('pw')
                s4 = self.tmp('pw')
                lo = slice(0, 64)
                hi = slice(64, 128)
                self.tt(s2[:, 1:ln + 16], zw[:, c, 0:ln + 15], zw[:, c, 1:ln + 16], ALU.add)
                if c == 0:
                    self.tt(s4[hi, 2:ln + 14], s2[hi, 1:ln + 13], s2[hi, 3:ln + 15], ALU.add)
                    srcs = ((lo, s2), (hi, s4))
                else:
                    self.tt(s4[:, 2:ln + 14], s2[:, 1:ln + 13], s2[:, 3:ln + 15], ALU.add)
                    s8 = self.tmp('pw')
                    self.tt(s8[:, 4:ln + 12], s4[:, 2:ln + 10], s4[:, 6:ln + 14], ALU.add)
                    s16 = self.tmp('pw')
                    self.tt(s16[hi, 8:ln + 8], s8[hi, 4:ln + 4], s8[hi, 12:ln + 12], ALU.add)
                    srcs = ((lo, s8), (hi, s16))
                dd = self.tmp('bfn')
                iw = vl[:, G_INVW + c:G_INVW + c + 1]
                for (rows, sw) in srcs:
                    self.stt(dd[rows, 0:ln], sw[rows, 8:8 + ln], iw[rows, :], zw[rows, c, 8:8 + ln], ALU.mult, ALU.subtract)
                    for (flag, d0, e0) in ((first, 0, 0), (last, ln - 8, 8)):
                        if flag:
                            t = self.tmp('f32')
                            self.tt(t[rows, 0:8], sw[rows, 8 + d0:16 + d0], S['icntE'][rows, c, e0:e0 + 8], ALU.mult)
                            self.tt(dd[rows, d0:d0 + 8], t[rows, 0:8], zw[rows, c, 8 + d0:16 + d0], ALU.subtract)
                ps = self.psum()
                self.mm(ps[:, 0:ln], self.wpool[:, c, :], dd[:, 0:ln], True, True)
                self.act(self.a_t[:, c, to:to + ln], ps[:, 0:ln], AF.Copy, scale=vl[:, vb + V_PSCALE + c:vb + V_PSCALE + c + 1])
                y = self.tmp('f32')
                cw = vb + V_CONVW
                self.ts(y[:, 0:ln], pw_[:, c, 1:1 + ln], vl[:, cw + 2 + c:cw + 3 + c], None, ALU.mult)
                self.stt(y[:, 0:ln], pw_[:, c, 0:ln], vl[:, cw + c:cw + c + 1], y[:, 0:ln], ALU.mult, ALU.add)
                self.stt(y[:, 0:ln], pw_[:, c, 2:2 + ln], vl[:, cw + 4 + c:cw + 5 + c], y[:, 0:ln], ALU.mult, ALU.add)
                self.tt(self.dc_t[:, c, to:to + ln], y[:, 0:ln], self.zb_t[:, c, to:to + ln], ALU.mult)

    def attention(self, l, S, cT, nk, qsets, kr_fill):
        nkc = nk // 128
        for kb_ in self.kbuf:
            kr_fill(kb_)

        def build_kv(h):
            kb = self.kbuf[h % 2]
            vbuf = self.vbuf[h % 2]
            voff = 0 if h % 2 == 0 else 64
            for k0 in range(0, nk, NT):
                n = min(NT, nk - k0)
                ps = self.psum()
                self.mm(ps[0:64, 0:n], self.wukv[:, h * 128:h * 128 + 64], cT[:, k0:k0 + n], True, True)
                self.copy(kb[0:64, k0:k0 + n], ps[0:64, 0:n])
            for c0 in range(0, nkc, 8):
                ncb = min(8, nkc - c0)
                ps = self.psum()
                for cc in range(ncb):
                    kc = c0 + cc
                    self.mm(ps[:, cc * 64:(cc + 1) * 64], cT[:, kc * 128:(kc + 1) * 128],
                            self.wukv[:, h * 128 + 64:h * 128 + 128], True, True)
                self.copy(vbuf[:, c0:c0 + ncb, voff:voff + 64], ps[:, 0:ncb * 64].rearrange("p (c d) -> p c d", d=64))

        build_kv(0)
        for h in range(8):
            kb = self.kbuf[h % 2]
            vbuf = self.vbuf[h % 2]
            def q_prep(q0, n):
                qh = self.tmp('qh')
                pq = self.psum()
                for k in range(2):
                    self.mm(pq[0:96, 0:n], self.wuq[:, k, h * 96:(h + 1) * 96], S['cqn'][:, k, q0:q0 + n], k == 0, k == 1)
                self.copy(qh[0:64, 0:n], pq[0:64, 0:n])
                r = slice(64, 96)
                if S['rope']:
                    pq2 = self.psum()
                    for k in range(2):
                        self.mm(pq2[0:96, 0:n], self.wuqs[:, k, h * 96:(h + 1) * 96], S['cqn'][:, k, q0:q0 + n], k == 0, k == 1)
                    t1 = self.tmp('f32')
                    t2 = self.tmp('f32')
                    self.tt(t1[r, 0:n], pq[r, 0:n], self.cosT[r, q0:q0 + n], ALU.mult)
                    self.tt(t2[r, 0:n], pq2[r, 0:n], self.sinT[r, q0:q0 + n], ALU.mult)
                    self.tt(qh[r, 0:n], t1[r, 0:n], t2[r, 0:n], ALU.add)
                else:
                    self.copy(qh[r, 0:n], pq[r, 0:n])
                return qh

            qh_next = q_prep(*qsets[0])
            for qi, (q0, n) in enumerate(qsets):
                qh = qh_next
                po = self.ssbank()
                pss = {}

                def emit_st(kc):
                    ps = self.psum()
                    self.mm(ps[:, 0:n], kb[0:96, kc * 128:(kc + 1) * 128], qh[0:96, 0:n], True, True)
                    pss[kc] = ps
                LA = 3
                for kc in range(min(LA, nkc)):
                    emit_st(kc)
                if qi + 1 < len(qsets):
                    qh_next = q_prep(*qsets[qi + 1])
                if qi == 0 and h + 1 < 8:
                    build_kv(h + 1)
                for kc in range(nkc):
                    if kc + LA < nkc:
                        emit_st(kc + LA)
                    if kc == min(10, nkc - 1) and self.att_pend is not None:
                        self.att_pend()
                        self.att_pend = None
                    pt = self.tmp('pt')
                    self.act(pt[:, 0:n], pss.pop(kc)[:, 0:n], AF.Exp, scale=SCALE)
                    self.mm(po[:, 0:n], vbuf[:, kc, :], pt[:, 0:n], kc == 0, kc == nkc - 1)
                dp = 64 if h % 2 == 0 else 0
                rows = slice(0, 64) if h % 2 == 0 else slice(64, 128)
                rs = self.tmp('rs')
                self.recip(rs[dp:dp + 1, 0:n], po[dp:dp + 1, 0:n])

                def fin(po=po, rs=rs, dp=dp, rows=rows, n=n, q0=q0, hh=h):
                    pb = self.psum()
                    self.mm(pb[:, 0:n], self.ones_f[dp:dp + 1, :], rs[dp:dp + 1, 0:n], True, True)
                    bs = self.tmp('f32')
                    self.copy(bs[rows, 0:n], pb[rows, 0:n])
                    self.tt(S['att'][rows, hh // 2, q0:q0 + n], po[rows, 0:n], bs[rows, 0:n], ALU.mult)
                self.att_pend = fin

    def att_flush(self):
        if self.att_pend is not None:
            self.att_pend()
            self.att_pend = None

    def merge(self, l, S, ti):
        vb = l * V_N
        vl = self.vecs
        g = 0
        t0 = ti * NT
        xt = S['x'][:, :, t0:t0 + NT]
        self.prenorm(xt, NT, S['A'][l][1], S['B'][l][1], self.hbuf[g][:, :, 0:NT])
        self.pool_conv(l, S, ti)
        brs = ((self.a_t, 0, 0, 2), (self.bsg_t, 0, 2, 2), (S['att'], t0, 4, 4), (self.dc_t, 0, 8, 2))
        h = self.hbuf[g]
        for m in range(KC):
            accf = self.tmp('acc')
            for half in range(2):
                if half == 0:
                    wbr = self.wload(self.w_br[l, m][:, 0:512], 4, 128)
                    kb0 = 0
                else:
                    wbr = self.wload(self.w_br[l, m][:, 512:1280], 6, 128)
                    kb0 = 4
                wg = self.wload(self.w_gate[l, m, half], KC, 256)
                for bj in range(2):
                    bi = half * 2 + bj
                    (src, off, k0, nk) = brs[bi]
                    pb = self.psum()
                    for k in range(nk):
                        self.mm(pb[:, 0:NT], wbr[:, k0 - kb0 + k, :], src[:, k, off:off + NT], k == 0, k == nk - 1)
                    pg = self.psum()
                    for k in range(KC):
                        self.mm(pg[:, 0:NT], wg[:, k, bj * 128:(bj + 1) * 128], h[:, k, 0:NT], k == 0, k == KC - 1)
                    gt = self.tmp('f32')
                    bc = vb + V_BGATE + bi * 8 + m
                    self.act(gt[:, 0:NT], pg[:, 0:NT], AF.Sigmoid, bias=vl[:, bc:bc + 1])
                    if bi == 0:
                        self.tt(accf[:, 0:NT], gt[:, 0:NT], pb[:, 0:NT], ALU.mult)
                    else:
                        t = self.tmp('f32')
                        self.tt(t[:, 0:NT], gt[:, 0:NT], pb[:, 0:NT], ALU.mult)
                        if bi < 3:
                            self.tt(accf[:, 0:NT], accf[:, 0:NT], t[:, 0:NT], ALU.add)
                        else:
                            self.tt(self.mrg[:, m, :], accf[:, 0:NT], t[:, 0:NT], ALU.add)
        ss = self.ssbank()
        pend = None
        for mo in range(KC):
            wo = self.wload(self.w_o[l, mo], KC, 128)
            ps = self.psum()
            for m in range(KC):
                self.mm(ps[:, 0:NT], wo[:, m, :], self.mrg[:, m, :], m == 0, m == KC - 1)
            if pend is not None:
                pend()
            pend = self.evac_f(ps[:, 0:NT], NT, self.fs[g][:, mo, 0:NT], ss, mo == 0, mo == KC - 1)
        pend()
        self.post_residual(self.fs[g][:, :, 0:NT], ss, xt, NT, S['C'][l][1])

    def v3(self, t, c):
        return t.ap().rearrange("p (c n) -> p c n", c=c)

    def build(self):
        nc = self.nc
        from contextlib import ExitStack
        self.xT_P = self.din("xT_P", [128, KC * NPT])
        self.xT_S = self.din("xT_S", [128, KC * NST])
        self.condT = self.din("condT", [128, KC * 2])
        self.cache_c = self.din("cache_c", [L, 128, 512])
        self.cache_k = self.din("cache_k", [L, 32, 512])
        self.vecs_d = self.din("vecs", [128, G_N])
        self.gsgu_d = self.din("gsgu", [L, 128, 256])
        self.bsT_d = self.din("bsT", [L, 128, 2 * 128])
        self.wsT_d = self.din("wsT", [L, 128, 4 * 128])
        self.wpool_d = self.din("wpool", [L, 128, 2 * 128])
        self.wuq_d = self.din("wuq", [L, 128, 2 * 768])
        self.wuqs_d = self.din("wuqs", [L, 128, 2 * 768])
        self.wukv_d = self.din("wukv", [L, 128, 1024])
        self.rpos_d = self.din("rpos", [128, 2 * NST])
        self.icntP_d = self.din("icntP", [128, 2 * 16])
        self.icntS_d = self.din("icntS", [128, 2 * 16])
        self.w_mod = self.din("w_mod", [L, 36, 128, KC * 128])
        self.w_gu = self.din("w_gu", [L, 2, FC, 128, KC * 256])
        self.w_dn = self.din("w_dn", [L, 2, KC, 128, FC * 128])
        self.w_in = self.din("w_in", [L, 9, 128, KC * 256])
        self.w_gate = self.din("w_gate", [L, KC, 2, 128, KC * 256])
        self.w_br = self.din("w_br", [L, KC, 128, 10 * 128])
        self.w_o = self.din("w_o", [L, KC, 128, KC * 128])
        self.yT_P = self.dout("yT_P", [128, KC * NPT])
        self.yT_S = self.dout("yT_S", [128, KC * NST])
        self.st_ckv = self.dout("st_ckv", [L, 128, NPT])
        self.st_kr = self.dout("st_kr", [L, 32, NPT])
        self.xin = nc.dram_tensor("xin", [XR, NST], F32)
        self.mg_in = nc.dram_tensor("mg_in", [128, 144], F32)
        self.mg_out = nc.dram_tensor("mg_out", [2 * 128, 144], F32)
        self.xout = nc.dram_tensor("xout", [2 * XR, NST], F32)
        dbg = self.debug_stage

        with ExitStack() as es:
            P = self.P
            self.pbanks = []
            for i in range(8):
                t = es.enter_context(nc.psum_tensor("pb%d" % i, [128, 512], F32))
                P.ps.add(t[:, 0:1].name)
                self.pbanks.append(t)
            self.vecs = self.sb("vecs", G_N, F32)
            self.ones_bf = self.sb("ones_bf", 128, BF16)
            self.ones_f = self.sb("ones_f", 128, F32)
            self.zeros_f = self.sb("zeros_f", 16, F32)
            self.eps_col = self.sb("eps", 1, F32)
            self.mods = self.sb("mods", L * 2 * 72, F32)
            self.drv = self.sb("drv", L * 2 * 3 * 16, F32)
            self.cs_bf = self.sb("cs_bf", 16, BF16)
            self.cs_f = self.sb("cs_f", 16, F32)
            self.mg_sb = self.sb("mg_sb", 144, F32)
            self.mg_all = self.sb("mg_all", 288, F32)
            self.ssq = self.sb("ssq", 16, F32)
            self.gsgu = self.sb("gsgu", 256, F32)
            self.bsT = self.v3(self.sb("bsT", 256, F32), 2)
            self.wsT = self.v3(self.sb("wsT", 512, BF16), 4)
            self.wpool = self.v3(self.sb("wpool", 256, BF16), 2)
            self.wuq = self.v3(self.sb("wuq", 1536, BF16), 2)
            self.wuqs = self.v3(self.sb("wuqs", 1536, BF16), 2)
            self.wukv = self.sb("wukv", 1024, BF16)
            self.wring = [self.sb("wring%d" % i, 2816, BF16) for i in range(4)]
            self.tmps = {
                'f32': [self.sb("tf%d" % i, NT, F32) for i in range(5)],
                'nrm': [self.sb("tn%d" % i, NT, F32) for i in range(2)],
                'sq': [self.sb("tq%d" % i, NT, BF16) for i in range(2)],
            }
            base = self.sp_top
            print("persistent sbuf bytes", base - 16512, flush=True)
            self.dma(self.vecs[:, :], self.vecs_d[:, :])
            self.memset(self.ones_bf[:, :], 1.0)
            self.memset(self.ones_f[:, :], 1.0)
            self.memset(self.zeros_f[:, :], 0.0)
            self.memset(self.eps_col[:, :], EPS)
            self.dma(self.cs_f[:, :], self.condT[:, :])
            self.act(self.cs_bf[:, :], self.cs_f[:, :], AF.Silu)
            csv = self.cs_bf.ap().rearrange("p (k c) -> p k c", c=2)
            pm = self.pbanks[0]
            for l in range(L):
                for blk in range(36):
                    wb = self.wload(self.w_mod[l, blk], KC, 128)
                    o0 = (l * 36 + blk) * 2
                    for k in range(KC):
                        self.mm(pm[:, o0:o0 + 2], wb[:, k, :], csv[:, k, :], k == 0, k == KC - 1)
            self.act(self.mg_sb[:, :], pm[:, 0:144], AF.Copy)
            self.dma(self.mg_in[:, :], self.mg_sb[:, :])
            mi = self.mg_in.ap().opt()
            mo_ = self.mg_out.ap().opt()
            self.P.op('pool', lambda e: e.collective_compute("AllGather", ALU.bypass, replica_groups=[[0, 1], [2, 3], [4, 5], [6, 7]],
                                                            ins=[mi], outs=[mo_]), [self.mg_in.ap()], [self.mg_out.ap()], kind='cc')
            self.dma(self.mg_all.ap().rearrange("p (r c) -> p r c", r=2), self.mg_out.ap().rearrange("(r p) c -> p r c", r=2))
            gv = self.mg_all.ap().rearrange("p (r l b j) -> p r l b j", r=2, l=L, b=36)
            for l in range(L):
                bm = self.vecs[:, l * V_N + V_BMOD:l * V_N + V_BMOD + 72].rearrange("p (r b) -> p r b", r=2)
                for c in range(2):
                    mc = self.mods[:, (l * 2 + c) * 72:(l * 2 + c) * 72 + 72].rearrange("p (r b) -> p r b", r=2)
                    self.tt(mc, gv[:, :, l, :, c], bm, ALU.add)
            streams = {}
            for c, nm in ((0, 'P'), (1, 'S')):
                A = [[None] * 3 for _ in range(L)]
                Bc = [[None] * 3 for _ in range(L)]
                Cc = [[None] * 3 for _ in range(L)]
                for l in range(L):
                    mo = (l * 2 + c) * 72
                    for s in range(3):
                        do = ((l * 2 + c) * 3 + s) * 16
                        a = self.drv[:, do:do + 8]
                        cc = self.drv[:, do + 8:do + 16]
                        gp = self.vecs[:, l * V_N + V_GPRE + s * 8:l * V_N + V_GPRE + s * 8 + 8]
                        go = self.vecs[:, l * V_N + V_GPOST + s * 8:l * V_N + V_GPOST + s * 8 + 8]
                        self.stt(a, self.mods[:, mo + (3 * s + 1) * 8:mo + (3 * s + 1) * 8 + 8], 1.0, gp, ALU.add, ALU.mult)
                        self.stt(cc, self.mods[:, mo + (3 * s + 2) * 8:mo + (3 * s + 2) * 8 + 8], 0.5 if s != 1 else 1.0, go, ALU.mult, ALU.mult)
                        A[l][s] = a
                        Bc[l][s] = self.mods[:, mo + 3 * s * 8:mo + 3 * s * 8 + 8]
                        Cc[l][s] = cc
                streams[nm] = dict(A=A, B=Bc, C=Cc)

            for nm in ('P', 'S'):
                S = streams[nm]
                rope = nm == 'S'
                ntok = NST if rope else NPT
                ntile = ntok // NT
                TG = 1 if rope else 2
                self.sp_top = base
                S['rope'] = rope
                xs = self.sb("x_" + nm, KC * ntok, F32)
                S['x'] = xs.ap().rearrange("p (k n) -> p k n", k=KC)
                srcv = (self.xT_S if rope else self.xT_P).ap().rearrange("p (k n) -> p k n", k=KC)
                for k in range(KC):
                    self.dma(S['x'][:, k, :], srcv[:, k, :])
                if rope:
                    lp = NST + 16
                    lc = NST + 2
                    S['pieces'] = lambda ti: [(ti * NT, 0, NT)]
                    S['cpos'] = lambda po: po
                    S['edges'] = lambda ti, to: (ti == 0, ti == 3)
                else:
                    lp = 4 * 272
                    lc = 4 * 258
                    S['pieces'] = lambda ti: [((2 * ti) * 272, 0, 256), ((2 * ti + 1) * 272, 256, 256)]
                    S['cpos'] = lambda po: (po // 272) * 258
                    S['edges'] = lambda ti, to: (True, True)
                S['zp_d'] = nc.dram_tensor("zp_d" + nm, [128, 2, lp], F32).ap()
                S['pc_d'] = nc.dram_tensor("pc_d" + nm, [128, 2, lc], F32).ap()
                S['zb_d'] = nc.dram_tensor("zb_d" + nm, [128, 2, ntok], BF16).ap()
                S['bsg_d'] = nc.dram_tensor("bsg_d" + nm, [128, 2, ntok], BF16).ap()
                S['icntE'] = self.v3(self.sb("icntE_" + nm, 32, F32), 2)
                self.dma(S['icntE'], (self.icntS_d if rope else self.icntP_d).ap().rearrange("p (c n) -> p c n", c=2))
                if rope:
                    self.cosT = self.sb("cosT", NST, BF16)
                    self.sinT = self.sb("sinT", NST, BF16)
                    self.halo = self.sb("halo", 36, F32)
                    self.halo2 = self.sb("halo2", 36, F32)
                ffn_base = self.sp_top
                TGF = 2
                self.f_fs, self.f_h, self.f_hid = [], [], []
                for g in range(TGF):
                    a0 = self.sp_top
                    self.f_fs.append(self.v3(self.sb("ffs%d_%s" % (g, nm), KC * NT, F32), KC))
                    self.f_h.append(self.v3(self.sb("fh%d_%s" % (g, nm), KC * NT, BF16, at=a0), KC))
                    self.f_hid.append(self.v3(self.sb("fhid%d_%s" % (g, nm), FC * NT, BF16), FC))
                top_ffn = self.sp_top
                self.sp_top = ffn_base
                S['cqn'] = self.v3(self.sb("cqn_" + nm, 2 * ntok, BF16), 2)
                S['att'] = self.v3(self.sb("att_" + nm, 4 * ntok, BF16), 4)
                if not rope:
                    S['ckvT'] = self.sb("ckvT_P", NPT, BF16)
                    S['krT'] = self.sb("krT_P", NPT, BF16)
                top_stream = self.sp_top
                self.hbuf, self.fs = [], []
                for g in range(TG):
                    a0 = self.sp_top
                    self.fs.append(self.v3(self.sb("fs%d_%s" % (g, nm), KC * NT, F32), KC))
                    self.hbuf.append(self.v3(self.sb("h%d_%s" % (g, nm), KC * NT, BF16, at=a0), KC))
                top_common = self.sp_top
                self.zb_t = self.v3(self.sb("zbt_" + nm, 2 * NT, BF16), 2)
                self.bsg_t = self.v3(self.sb("bsgt_" + nm, 2 * NT, BF16), 2)
                top_m0 = self.sp_top
                self.u_sb = [self.v3(self.sb("u%d_%s" % (g, nm), 2 * NT, F32), 2) for g in range(TG)]
                self.zc_sb = [self.v3(self.sb("zc%d_%s" % (g, nm), 2 * NT, F32), 2) for g in range(TG)]
                self.kr_f = [self.sb("krf%d_%s" % (g, nm), NT, F32) for g in range(TG)]
                self.cq_sb = self.v3(self.sb("cq_" + nm, 2 * NT, F32), 2)
                self.ckvn_f = self.sb("ckvn_" + nm, NT, F32)
                self.ckvn_f3 = self.ckvn_f.ap().rearrange("p (o n) -> p o n", o=1)
                self.vn_sb = self.v3(self.sb("vn_" + nm, 4 * 256, BF16), 4)
                top_p1 = self.sp_top
                self.sp_top = top_stream
                nk = NKS if rope else 256
                nkb = 2
                self.kbuf = [self.sb("kb%d_%s" % (i, nm), nk, BF16) for i in range(nkb)]
                self.vbuf = [self.sb("vb%d_%s" % (i, nm), (nk // 128) * 128, BF16).ap().rearrange("p (c d) -> p c d", d=128) for i in range(2)]
                if rope:
                    self.c_all = self.sb("c_all", NKS, BF16)
                self.tmps['pt'] = [self.sb("tp%d_%s" % (i, nm), NT, BF16) for i in range(4)]
                self.tmps['qh'] = [self.sb("tqh%d_%s" % (i, nm), NT, BF16) for i in range(2)]
                self.tmps['rs'] = [self.sb("trs%d_%s" % (i, nm), NT, F32) for i in range(2)]
                top_att = self.sp_top
                self.sp_top = top_m0
                self.zpw = self.v3(self.sb("zpw_" + nm, 2 * (NT + 16), F32), 2)
                self.pcw = self.v3(self.sb("pcw_" + nm, 2 * (NT + 2), F32), 2)
                self.a_t = self.v3(self.sb("at_" + nm, 2 * NT, BF16), 2)
                self.dc_t = self.v3(self.sb("dct_" + nm, 2 * NT, BF16), 2)
                self.mrg = self.v3(self.sb("mrg_" + nm, KC * NT, BF16), KC)
                self.tmps['pw'] = [self.sb("pw%d_%s" % (i, nm), NT + 16, F32) for i in range(4)]
                self.tmps['acc'] = [self.sb("ta%d_%s" % (i, nm), NT, F32) for i in range(1)]
                self.tmps['bfn'] = [self.sb("tb%d_%s" % (i, nm), NT, BF16) for i in range(1)]
                top_mg = self.sp_top
                print("stream", nm, "tops: common", top_common, "ffn", top_ffn, "p1", top_p1, "att", top_att, "merge", top_mg, flush=True)

                for c in range(2):
                    if rope:
                        pass
                    else:
                        for sq in range(4):
                            self.dma(S['zp_d'][:, c, sq * 272:sq * 272 + 8], self.zeros_f[:, 0:8])
                            self.dma(S['zp_d'][:, c, sq * 272 + 264:sq * 272 + 272], self.zeros_f[:, 0:8])
                            self.dma(S['pc_d'][:, c, sq * 258:sq * 258 + 1], self.zeros_f[:, 0:1])
                            self.dma(S['pc_d'][:, c, sq * 258 + 257:sq * 258 + 258], self.zeros_f[:, 0:1])
                if rope:
                    self.rope_tables()
                    self.dma(self.xin[162:163, 512:2048].rearrange("r (q e) -> (r q) e", e=16), self.zeros_f[0:96, 0:16])

                groups = [list(range(g0, min(g0 + TG, ntile))) for g0 in range(0, ntile, TG)]
                fgroups = [list(range(g0, min(g0 + TGF, ntile))) for g0 in range(0, ntile, TGF)]
                stop = False
                for l in range(L):
                    self.dma(self.gsgu[:, :], self.gsgu_d[l])
                    self.dma(self.bsT, self.bsT_d[l].rearrange("p (c n) -> p c n", c=2))
                    self.dma(self.wsT, self.wsT_d[l].rearrange("p (g n) -> p g n", g=4), q='pool')
                    self.dma(self.wpool, self.wpool_d[l].rearrange("p (c n) -> p c n", c=2), q='pool')
                    self.dma(self.wuq, self.wuq_d[l].rearrange("p (k n) -> p k n", k=2), q='pool')
                    self.dma(self.wuqs, self.wuqs_d[l].rearrange("p (k n) -> p k n", k=2), q='pool')
                    self.dma(self.wukv[:, :], self.wukv_d[l], q='pool')
                    for grp in fgroups:
                        self.ffn(l, 0, [(S['x'][:, :, ti * NT:(ti + 1) * NT], NT) for ti in grp], S['A'][l][0], S['B'][l][0], S['C'][l][0])
                    if dbg == 'ffn0':
                        break
                    for grp in groups:
                        self.mixer_p1(l, S, grp)
                    if dbg == 'p1':
                        break
                    for i in range(2):
                        self.memset(self.vbuf[i], 0.0)
                    for kb in self.kbuf:
                        self.memset(kb[:, :], 0.0)
                    self.memset(self.vbuf[0][:, :, 64:65], 1.0)
                    self.memset(self.vbuf[1][:, :, 0:1], 1.0)
                    if rope:
                        self.exchange(l, S)
                    if dbg == 'xch':
                        break
                    if rope:

                        def kr_fill(kb, l=l):
                            self.dma(kb[64:96, 0:512], self.cache_k[l], q='pool')
                            self.dma(kb[64:96, 512:512 + NST], self.xout[128:160, :], q='pool')
                            self.dma(kb[64:96, 512 + NST:NKS], self.xout[XR + 128:XR + 160, :], q='pool')
                        self.attention(l, S, self.c_all, NKS, [(ti * NT, NT) for ti in range(ntile)], kr_fill)
                    else:
                        for sq in range(4):
                            def kr_fill(kb, sq=sq):
                                self.copy(kb[64:96, 0:256], S['krT'][64:96, sq * 256:(sq + 1) * 256])
                            self.attention(l, S, S['ckvT'][:, sq * 256:(sq + 1) * 256], 256, [(sq * 256, 256)], kr_fill)
                    self.att_flush()
                    if dbg == 'att0':
                        break
                    for ti in range(ntile):
                        self.merge(l, S, ti)
                    if dbg == 'mix0':
                        break
                    for grp in fgroups:
                        self.ffn(l, 1, [(S['x'][:, :, ti * NT:(ti + 1) * NT], NT) for ti in grp], S['A'][l][2], S['B'][l][2], S['C'][l][2])
                    if dbg == 'l0':
                        break
                dst = (self.yT_S if rope else self.yT_P).ap().rearrange("p (k n) -> p k n", k=KC)
                for k in range(KC):
                    self.dma(dst[:, k, :], S['x'][:, k, :])
                if dbg in ('att0',):
                    self.dbg_att = S['att']
            block = es.enter_context(nc.Block())
            P.finalize(es, block)
        return nc

    def rope_tables(self):
        r = slice(64, 96)
        self.dma(self.cosT[r, :], self.rpos_d[64:96, 0:NST], q='pool')
        self.dma(self.sinT[r, :], self.rpos_d[64:96, NST:2 * NST], q='pool')

    def exchange(self, l, S):
        zpd, pcd = S['zp_d'], S['pc_d']
        hv, hv2 = self.halo, self.halo2
        x160 = self.xin[160:162, :].rearrange("r (q e) -> (r q) e", e=32)
        x162 = self.xin[162:163, 0:512].rearrange("r (q e) -> (r q) e", e=4)
        for c in range(2):
            self.dma(hv[:, c * 16:c * 16 + 8], zpd[:, c, 8:16])
            self.dma(hv[:, c * 16 + 8:c * 16 + 16], zpd[:, c, NST:NST + 8])
            self.dma(hv[:, 32 + c * 2:32 + c * 2 + 1], pcd[:, c, 1:2])
            self.dma(hv[:, 32 + c * 2 + 1:32 + c * 2 + 2], pcd[:, c, NST:NST + 1])
        self.dma(x160, hv[:, 0:32])
        self.dma(x162, hv[:, 32:36])
        xi = self.xin.ap().opt()
        xo = self.xout.ap().opt()
        self.P.op('pool', lambda e: e.collective_compute("AllGather", ALU.bypass, replica_groups=[[0, 1], [2, 3], [4, 5], [6, 7]],
                                                        ins=[xi], outs=[xo]), [self.xin.ap()], [self.xout.ap()], kind='cc')
        self.dma(self.c_all[:, 0:512], self.cache_c[l], q='pool')
        self.dma(self.c_all[:, 512:512 + NST], self.xout[0:128, :], q='pool')
        self.dma(self.c_all[:, 512 + NST:NKS], self.xout[XR:XR + 128, :], q='pool')
        o160a = self.xout[160:162, :].rearrange("r (q e) -> (r q) e", e=32)
        o162a = self.xout[162:163, 0:512].rearrange("r (q e) -> (r q) e", e=4)
        o160b = self.xout[XR + 160:XR + 162, :].rearrange("r (q e) -> (r q) e", e=32)
        o162b = self.xout[XR + 162:XR + 163, 0:512].rearrange("r (q e) -> (r q) e", e=4)
        self.dma(hv[:, 0:32], o160a)
        self.dma(hv[:, 32:36], o162a)
        self.dma(hv2[:, 0:32], o160b)
        self.dma(hv2[:, 32:36], o162b)
        ml = self.vecs[:, G_ML:G_ML + 1]
        mr = self.vecs[:, G_MR:G_MR + 1]
        self.ts(hv[:, :], hv[:, :], ml, None, ALU.mult)
        self.ts(hv2[:, :], hv2[:, :], mr, None, ALU.mult)
        for c in range(2):
            self.dma(zpd[:, c, 0:8], hv[:, c * 16 + 8:c * 16 + 16])
            self.dma(zpd[:, c, 8 + NST:16 + NST], hv2[:, c * 16:c * 16 + 8])
            self.dma(pcd[:, c, 0:1], hv[:, 32 + c * 2 + 1:32 + c * 2 + 2])
            self.dma(pcd[:, c, NST + 1:NST + 2], hv2[:, 32 + c * 2:32 + c * 2 + 1])


def _tile_w(W, bw):
    Din, Dout = W.shape
    kc = Din // 128
    nb = Dout // bw
    t = W.reshape(kc, 128, nb, bw).transpose(2, 1, 0, 3)
    return np.ascontiguousarray(t).reshape(nb, 128, kc * bw)


def _cols(v):
    return np.ascontiguousarray(v.reshape(-1, 128).T)


def _host_prep(inp):
    f = np.float32
    g = {k: np.asarray(v) for k, v in inp.items()}
    sh = {}
    wgu = np.empty((L, 2, FC, 128, KC * 256), f)
    wdn = np.empty((L, 2, KC, 128, FC * 128), f)
    for l in range(L):
        for j in range(2):
            W = g['w_ffn_gu'][l, j]
            Wg = W[:, :DFF].reshape(KC, 128, FC, 128)
            Wu = W[:, DFF:].reshape(KC, 128, FC, 128)
            t = np.stack([Wg, Wu], axis=3)
            wgu[l, j] = t.transpose(2, 1, 0, 3, 4).reshape(FC, 128, KC * 256)
            wdn[l, j] = _tile_w(g['w_ffn_dn'][l, j], 128)
    sh['w_gu'] = wgu
    sh['w_dn'] = wdn
    win = np.zeros((L, 9, 128, KC * 256), f)
    perm = np.concatenate([np.arange(8, 16), np.arange(0, 8), np.arange(24, 32), np.arange(16, 24)])
    for l in range(L):
        W = g['w_in'][l]
        Wp = np.zeros((D, 9 * 256), f)
        Wp[:, 0:256] = W[:, 0:256]
        Wp[:, 256:512] = W[:, 256:512]
        Wp[:, 512:768] = W[:, 512:768]
        Wp[:, 768:1024] = W[:, 768:1024]
        Wp[:, 1024:1152] = W[:, 1024:1152]
        Wp[:, 1152 + 64:1152 + 96] = W[:, 1152:1184]
        Wp[:, 1280:1536] = W[:, 1184:1440]
        Wp[:, 1536:1792] = W[:, 1440:1696]
        Wp[:, 1792:2048] = W[:, 1696:1952]
        Wp[:, 2048 + 64:2048 + 96] = W[:, 1152:1184][:, perm]
        win[l] = _tile_w(Wp, 256)
    sh['w_in'] = win
    wgate = np.empty((L, KC, 2, 128, KC * 256), f)
    wbr = np.empty((L, KC, 128, 10 * 128), f)
    wo = np.empty((L, KC, 128, KC * 128), f)
    for l in range(L):
        W = g['w_gate'][l].reshape(D, 4, KC, 128)
        Wm = W.transpose(0, 2, 1, 3).reshape(D, KC * 2, 256)
        Wm = Wm.reshape(D, KC * 2 * 256)
        wgate[l] = _tile_w(Wm, 256).reshape(KC, 2, 128, KC * 256)
        Wb = np.concatenate([g['w_br_pool'][l], g['w_br_sgu'][l], g['w_br_mla'][l], g['w_br_conv'][l]], axis=0)
        wbr[l] = _tile_w(Wb, 128)
        wo[l] = _tile_w(g['w_o'][l], 128)
    sh['w_gate'] = wgate
    sh['w_br'] = wbr
    sh['w_o'] = wo
    vecs = np.zeros((128, G_N), f)
    for l in range(L):
        b = l * V_N
        vecs[:, b + V_BMOD:b + V_BMOD + 72] = _cols(g['b_mod'][l])
        for s in range(3):
            vecs[:, b + V_GPRE + s * 8:b + V_GPRE + s * 8 + 8] = _cols(g['g_pre'][l, s])
            vecs[:, b + V_GPOST + s * 8:b + V_GPOST + s * 8 + 8] = _cols(g['g_post'][l, s])
        vecs[:, b + V_PSCALE:b + V_PSCALE + 2] = _cols(g['pool_scale'][l])
        vecs[:, b + V_GQ:b + V_GQ + 2] = _cols(g['g_q'][l])
        vecs[:, b + V_GKV:b + V_GKV + 1] = _cols(g['g_kv'][l])
        for k in range(3):
            vecs[:, b + V_CONVW + k * 2:b + V_CONVW + k * 2 + 2] = _cols(g['conv_w'][l, k])
        vecs[:, b + V_BGATE:b + V_BGATE + 32] = _cols(g['b_gate'][l])
    half_d = 16
    freqs = (10000.0 ** (-(2.0 * np.arange(8, dtype=np.float32)) / half_d)).astype(f)
    for ff in range(32):
        vecs[64 + ff, G_FREQ] = freqs[ff % 8]
        vecs[64 + ff, G_SIGN] = -1.0 if (ff % 16) < 8 else 1.0
    for c in range(2):
        vecs[0:64, G_INVW + c] = 1.0 / (2, 8)[c]
        vecs[64:128, G_INVW + c] = 1.0 / (4, 16)[c]
    sh['vecs'] = vecs
    sh['gsgu'] = np.ascontiguousarray(np.broadcast_to(g['g_sgu'][:, None, :], (L, 128, 256))).astype(f)
    bsT = np.zeros((L, 128, 2, 128), f)
    for l in range(L):
        for c in range(2):
            bsT[l, 0:64, c, :] = g['b_sgu'][l, 2 * c][None, :]
            bsT[l, 64:128, c, :] = g['b_sgu'][l, 2 * c + 1][None, :]
    sh['bsT'] = bsT.reshape(L, 128, 256)
    sh['wsT'] = np.ascontiguousarray(g['w_sgu'].transpose(0, 3, 1, 2)).reshape(L, 128, 512)
    wpool = np.zeros((L, 128, 2, 128), f)
    for l in range(L):
        for c in range(2):
            wpool[l, 0:64, c, 0:64] = g['w_pool'][l, 2 * c]
            wpool[l, 64:128, c, 64:128] = g['w_pool'][l, 2 * c + 1]
    sh['wpool'] = wpool.reshape(L, 128, 256)
    wuq = g['w_uq']
    wuqs = wuq.copy()
    for h in range(8):
        wuqs[:, :, h * 96 + 64:h * 96 + 96] = wuq[:, :, h * 96 + 64:h * 96 + 96][:, :, perm]
    sh['wuq'] = np.ascontiguousarray(wuq.reshape(L, 2, 128, 768).transpose(0, 2, 1, 3)).reshape(L, 128, 1536)
    sh['wuqs'] = np.ascontiguousarray(wuqs.reshape(L, 2, 128, 768).transpose(0, 2, 1, 3)).reshape(L, 128, 1536)
    sh['wukv'] = np.ascontiguousarray(g['w_ukv'])
    def edge_tab(first_edge, last_edge):
        t = np.zeros((128, 2, 16), f)
        for c in range(2):
            for hf in range(2):
                w = ((2, 4), (8, 16))[c][hf]
                rows = slice(hf * 64, hf * 64 + 64)
                for i in range(8):
                    cnt_f = (min(i + w // 2, 10 ** 9) - max(i - w // 2, 0)) if first_edge else w
                    d = 8 - i
                    cnt_l = (min(w // 2, d) + w // 2) if last_edge else w
                    t[rows, c, i] = 1.0 / cnt_f
                    t[rows, c, 8 + i] = 1.0 / cnt_l
        return t.reshape(128, 32)
    sh['icntP'] = edge_tab(True, True)
    sh['_edge'] = edge_tab
    return g, sh


_NC_CACHE = {}


def _get_nc(debug_stage=None):
    if debug_stage not in _NC_CACHE:
        b = Builder(debug_stage)
        _NC_CACHE[debug_stage] = (b.build(), b)
    return _NC_CACHE[debug_stage]


def _in_maps(inputs):
    f = np.float32
    g, sh = _host_prep(inputs)
    in_maps = []
    for i in range(8):
        b, half = i // 2, i % 2
        m = dict((k, v) for k, v in sh.items() if not k.startswith('_'))
        xp = g['x_prompt'][4 * i:4 * i + 4].reshape(NPT, KC, 128)
        m['xT_P'] = np.ascontiguousarray(xp.transpose(2, 1, 0)).reshape(128, KC * NPT)
        xs = g['x_sample'][b, half * NST:(half + 1) * NST].reshape(NST, KC, 128)
        m['xT_S'] = np.ascontiguousarray(xs.transpose(2, 1, 0)).reshape(128, KC * NST)
        cond = np.stack([g['c_ctx'], g['c'][b]], axis=-1)
        m['condT'] = np.ascontiguousarray(cond.reshape(KC, 128, 2).transpose(1, 0, 2)).reshape(128, KC * 2)
        m['w_mod'] = np.stack([_tile_w(g['w_mod'][l][:, half * 4608:(half + 1) * 4608], 128) for l in range(L)])
        m['cache_c'] = np.ascontiguousarray(g['cache_ckv'][b].transpose(0, 2, 1))
        m['cache_k'] = np.ascontiguousarray(g['cache_krope'][b].transpose(0, 2, 1))
        v = sh['vecs'].copy()
        v[:, G_ML] = 1.0 if half == 1 else 0.0
        v[:, G_MR] = 1.0 if half == 0 else 0.0
        v[:, G_SEL:G_SEL + 4] = 0.0
        v[:, G_SEL + b] = 1.0
        m['vecs'] = v
        pos = half * NST + np.arange(NST)
        rp = np.zeros((128, 2 * NST), f)
        for ff in range(32):
            pv = ((pos // 64) if ff < 16 else (pos % 64)).astype(f)
            ang = pv * sh['vecs'][64 + ff, G_FREQ]
            rp[64 + ff, 0:NST] = np.cos(ang)
            rp[64 + ff, NST:] = np.sin(ang) * sh['vecs'][64 + ff, G_SIGN]
        m['rpos'] = rp
        m['icntS'] = sh['_edge'](half == 0, half == 1)
        in_maps.append(m)
    return in_maps


def kernel(debug_stage=None, **inputs):
    f = np.float32
    nc, _ = _get_nc(debug_stage)
    in_maps = _in_maps(inputs)
    res = run_bass_kernel_spmd(nc, in_maps, core_ids=list(range(8)))
    R = res.results
    y_prompt = np.empty((32, 256, D), f)
    y_sample = np.empty((4, 4096, D), f)
    st_ckv = np.empty((32, L, 256, 128), f)
    st_kr = np.empty((32, L, 256, 32), f)
    for i in range(8):
        b, half = i // 2, i % 2
        yp = R[i]['yT_P'].reshape(128, KC, NPT).transpose(2, 1, 0).reshape(4, 256, D)
        y_prompt[4 * i:4 * i + 4] = yp
        ys = R[i]['yT_S'].reshape(128, KC, NST).transpose(2, 1, 0).reshape(NST, D)
        y_sample[b, half * NST:(half + 1) * NST] = ys
        sc = R[i]['st_ckv'].reshape(L, 128, 4, 256).transpose(2, 0, 3, 1)
        st_ckv[4 * i:4 * i + 4] = sc
        sk = R[i]['st_kr'].reshape(L, 32, 4, 256).transpose(2, 0, 3, 1)
        st_kr[4 * i:4 * i + 4] = sk
    return (y_prompt, y_sample, st_ckv, st_kr)
```
